# Optimizing a Trainium2 kernel written in Bass

```python
import jax, jax.numpy as jnp
from jax import lax
import numpy as np

D_MODEL = 1024
BATCH = 2
SEQ = 8192
DEPTH = 1

GRID_W = 64
CTX_LEN = 256
HEAD_DIM = 64
N_Q_HEADS = 8
N_KV_HEADS = 2
GQA_GROUP = N_Q_HEADS // N_KV_HEADS
WINDOW = 128
BLOCK = 128
ROPE_BASE = 10000.0
ROPE_PAIRS = HEAD_DIM // 4
N_GMLP_GROUPS = 8
GMLP_GROUP_DIM = 64
GMLP_WIDTH = N_GMLP_GROUPS * GMLP_GROUP_DIM
CHUNK = 128
FFN_HIDDEN = ((8 * D_MODEL // 3 + 255) // 256) * 256
Q_W = N_Q_HEADS * HEAD_DIM
KV_W = N_KV_HEADS * HEAD_DIM
IN_SPLITS = (Q_W, Q_W + KV_W, Q_W + 2 * KV_W, Q_W + 2 * KV_W + GMLP_WIDTH,
             Q_W + 2 * KV_W + 2 * GMLP_WIDTH, Q_W + 2 * KV_W + 2 * GMLP_WIDTH + D_MODEL)
IN_W = Q_W + 2 * KV_W + 2 * GMLP_WIDTH + 2 * D_MODEL
LN_EPS = 1e-5
NEG_INF = -1e30
DEEPNORM_ALPHA = (2 * DEPTH) ** 0.25
DEEPNORM_BETA = (8 * DEPTH) ** -0.25

kernel_name = 'hybrid_window_gqa_gmlp_dit_block'


def layer_norm(x, g=None, b=None):
    xf = x.astype(jnp.float32)
    mu = jnp.mean(xf, axis=-1, keepdims=True)
    var = jnp.mean(jnp.square(xf - mu), axis=-1, keepdims=True)
    y = (xf - mu) * lax.rsqrt(var + LN_EPS)
    if g is not None:
        y = y * g.astype(jnp.float32) + b.astype(jnp.float32)
    return y.astype(x.dtype)


def modulate(y, shift, scale):
    return y * (1 + scale[..., None, :]) + shift[..., None, :]


def axial_rope(t, rows, cols):
    inv = ROPE_BASE ** (-jnp.arange(ROPE_PAIRS, dtype=jnp.float32) / ROPE_PAIRS)

    def rot(xa, pos):
        ang = pos.astype(jnp.float32)[:, None] * inv
        cos = jnp.cos(ang)[:, None, :].astype(t.dtype)
        sin = jnp.sin(ang)[:, None, :].astype(t.dtype)
        x1, x2 = xa[..., :ROPE_PAIRS], xa[..., ROPE_PAIRS:]
        return jnp.concatenate([x1 * cos - x2 * sin, x1 * sin + x2 * cos], axis=-1)

    half = HEAD_DIM // 2
    return jnp.concatenate([rot(t[..., :half], rows), rot(t[..., half:], cols)], axis=-1)


def window_attention(q, k, v, kc, vc, sink):
    B, L = q.shape[:2]
    nb = L // BLOCK
    C = kc.shape[1]
    qb = q.reshape(B, nb, BLOCK, N_KV_HEADS, GQA_GROUP, HEAD_DIM)

    def band(t):
        tp = jnp.pad(t, ((0, 0), (BLOCK, BLOCK), (0, 0), (0, 0)))
        tp = tp.reshape(B, nb + 2, BLOCK, N_KV_HEADS, HEAD_DIM)
        return jnp.concatenate([tp[:, :-2], tp[:, 1:-1], tp[:, 2:]], axis=2)

    kw, vw = band(k), band(v)
    scale = HEAD_DIM ** -0.5
    s_loc = jnp.einsum('bnqhgd,bnjhd->bnhgqj', qb, kw).astype(jnp.float32) * scale
    s_ctx = jnp.einsum('bnqhgd,bchd->bnhgqc', qb, kc).astype(jnp.float32) * scale
    qi = jnp.arange(BLOCK)[:, None]
    kj = jnp.arange(3 * BLOCK)[None, :]
    rel = kj - BLOCK - qi
    kpos = jnp.arange(nb)[:, None, None] * BLOCK - BLOCK + kj[None]
    valid = (jnp.abs(rel) <= WINDOW)[None] & (kpos >= 0) & (kpos < L)
    s_loc = jnp.where(valid[None, :, None, None], s_loc, NEG_INF)
    sink_l = jnp.broadcast_to(
        sink.astype(jnp.float32).reshape(N_KV_HEADS, GQA_GROUP)[None, None, :, :, None, None],
        s_loc.shape[:-1] + (1,))
    logits = jnp.concatenate([sink_l, s_ctx, s_loc], axis=-1)
    probs = jax.nn.softmax(logits, axis=-1).astype(v.dtype)
    p_ctx, p_loc = probs[..., 1:1 + C], probs[..., 1 + C:]
    out = (jnp.einsum('bnhgqc,bchd->bnqhgd', p_ctx, vc)
           + jnp.einsum('bnhgqj,bnjhd->bnqhgd', p_loc, vw))
    return out.reshape(B, L, Q_W)


def context_attention(q, k, v, sink):
    B, C = q.shape[:2]
    qg = q.reshape(B, C, N_KV_HEADS, GQA_GROUP, HEAD_DIM)
    s = jnp.einsum('bqhgd,bkhd->bhgqk', qg, k).astype(jnp.float32) * HEAD_DIM ** -0.5
    sink_l = jnp.broadcast_to(
        sink.astype(jnp.float32).reshape(N_KV_HEADS, GQA_GROUP)[None, :, :, None, None],
        s.shape[:-1] + (1,))
    probs = jax.nn.softmax(jnp.concatenate([sink_l, s], axis=-1), axis=-1).astype(v.dtype)
    out = jnp.einsum('bhgqk,bkhd->bqhgd', probs[..., 1:], v)
    return out.reshape(B, C, Q_W)


def chunk_gmlp(u, vb, ln_g, ln_b, w_s, b_s):
    B, L = u.shape[:2]
    nc = L // CHUNK
    vn = layer_norm(vb, ln_g, ln_b).reshape(B, nc, CHUNK, N_GMLP_GROUPS, GMLP_GROUP_DIM)
    s = jnp.einsum('gij,bnjgd->bnigd', w_s, vn) + b_s.T[None, None, :, :, None]
    return u * s.reshape(B, L, GMLP_WIDTH)


def token_mixer(h, w_in, sink, gmlp_g, gmlp_b, w_s, b_s, w_a, w_b, w_o, kc, vc, pos):
    B, L = h.shape[:2]
    q, k, v, u, vb, ga, gb = jnp.split(h @ w_in, IN_SPLITS, axis=-1)
    q = q.reshape(B, L, N_Q_HEADS, HEAD_DIM)
    k = k.reshape(B, L, N_KV_HEADS, HEAD_DIM)
    v = v.reshape(B, L, N_KV_HEADS, HEAD_DIM)
    if pos is None:
        ya = context_attention(q, k, v, sink)
    else:
        rows, cols = pos
        ya = window_attention(axial_rope(q, rows, cols), axial_rope(k, rows, cols), v, kc, vc, sink)
    yb = chunk_gmlp(jax.nn.gelu(u), jax.nn.gelu(vb), gmlp_g, gmlp_b, w_s, b_s)
    merged = jax.nn.sigmoid(ga) * (ya @ w_a) + jax.nn.sigmoid(gb) * (yb @ w_b)
    return merged @ w_o


def context_kv(hc, w_in):
    B, C = hc.shape[:2]
    k, v = jnp.split(hc @ w_in[:, Q_W:Q_W + 2 * KV_W], 2, axis=-1)
    return (k.reshape(B, C, N_KV_HEADS, HEAD_DIM), v.reshape(B, C, N_KV_HEADS, HEAD_DIM))


def swiglu(h, w_ffn_in, w_ffn_out):
    gate, up = jnp.split(h @ w_ffn_in, 2, axis=-1)
    return (jax.nn.silu(gate) * up) @ w_ffn_out


def post_norm(res, out, g, b):
    return layer_norm(DEEPNORM_ALPHA * res + out, g, b)


def setup_inputs(seed: int = 0) -> dict:
    key = jax.random.key(seed)
    ks = jax.random.split(key, 21)
    f32 = jnp.float32

    def nrm(k, shape, s):
        return jax.random.normal(k, shape, f32) * s

    return {
        'x': nrm(ks[0], (BATCH, SEQ, D_MODEL), 1.0),
        'c': nrm(ks[1], (BATCH, D_MODEL), 1.0),
        'ctx': nrm(ks[2], (BATCH, CTX_LEN, D_MODEL), 1.0),
        'c_ctx': nrm(ks[3], (D_MODEL,), 1.0),
        'w_ada': nrm(ks[4], (DEPTH, D_MODEL, 6 * D_MODEL), 0.5 * D_MODEL ** -0.5),
        'b_ada': nrm(ks[5], (DEPTH, 6 * D_MODEL), 0.02),
        'w_in': nrm(ks[6], (DEPTH, D_MODEL, IN_W), D_MODEL ** -0.5),
        'attn_sink': nrm(ks[7], (DEPTH, N_Q_HEADS), 0.5),
        'gmlp_ln_g': 1.0 + nrm(ks[8], (DEPTH, GMLP_WIDTH), 0.02),
        'gmlp_ln_b': nrm(ks[9], (DEPTH, GMLP_WIDTH), 0.02),
        'w_spatial': nrm(ks[10], (DEPTH, N_GMLP_GROUPS, CHUNK, CHUNK), CHUNK ** -0.5),
        'b_spatial': 1.0 + nrm(ks[11], (DEPTH, N_GMLP_GROUPS, CHUNK), 0.02),
        'w_branch_a': nrm(ks[12], (DEPTH, Q_W, D_MODEL), Q_W ** -0.5),
        'w_branch_b': nrm(ks[13], (DEPTH, GMLP_WIDTH, D_MODEL), GMLP_WIDTH ** -0.5),
        'w_out': nrm(ks[14], (DEPTH, D_MODEL, D_MODEL), DEEPNORM_BETA * D_MODEL ** -0.5),
        'ln1_g': 1.0 + nrm(ks[15], (DEPTH, D_MODEL), 0.02),
        'ln1_b': nrm(ks[16], (DEPTH, D_MODEL), 0.02),
        'w_ffn_in': nrm(ks[17], (DEPTH, D_MODEL, 2 * FFN_HIDDEN), D_MODEL ** -0.5),
        'w_ffn_out': nrm(ks[18], (DEPTH, FFN_HIDDEN, D_MODEL), DEEPNORM_BETA * FFN_HIDDEN ** -0.5),
        'ln2_g': 1.0 + nrm(ks[19], (DEPTH, D_MODEL), 0.02),
        'ln2_b': nrm(ks[20], (DEPTH, D_MODEL), 0.02),
    }


def reference(x, c, ctx, c_ctx, w_ada, b_ada, w_in, attn_sink, gmlp_ln_g, gmlp_ln_b,
              w_spatial, b_spatial, w_branch_a, w_branch_b, w_out, ln1_g, ln1_b,
              w_ffn_in, w_ffn_out, ln2_g, ln2_b):
    L = x.shape[1]
    n_rows = L // GRID_W
    rows = jnp.repeat(jnp.arange(n_rows, dtype=jnp.int32), GRID_W)
    cols = jnp.tile(jnp.arange(GRID_W, dtype=jnp.int32), n_rows)

    for layer in range(DEPTH):
        mod_x = jnp.split(jax.nn.silu(c) @ w_ada[layer] + b_ada[layer], 6, axis=-1)
        mod_c = jnp.split(jax.nn.silu(c_ctx) @ w_ada[layer] + b_ada[layer], 6, axis=-1)
        mix_params = (w_in[layer], attn_sink[layer], gmlp_ln_g[layer], gmlp_ln_b[layer],
                      w_spatial[layer], b_spatial[layer], w_branch_a[layer], w_branch_b[layer],
                      w_out[layer])

        hc = modulate(layer_norm(ctx), mod_c[0], mod_c[1])
        kc, vc = context_kv(hc, w_in[layer])

        h = modulate(layer_norm(x), mod_x[0], mod_x[1])
        mix = token_mixer(h, *mix_params, kc, vc, (rows, cols))
        x_mid = post_norm(x, mod_x[2][:, None, :] * mix, ln1_g[layer], ln1_b[layer])
        h2 = modulate(layer_norm(x_mid), mod_x[3], mod_x[4])
        x_new = post_norm(x_mid, mod_x[5][:, None, :] * swiglu(h2, w_ffn_in[layer], w_ffn_out[layer]),
                          ln2_g[layer], ln2_b[layer])

        if layer < DEPTH - 1:
            mix_c = token_mixer(hc, *mix_params, None, None, None)
            ctx_mid = post_norm(ctx, mod_c[2] * mix_c, ln1_g[layer], ln1_b[layer])
            h2c = modulate(layer_norm(ctx_mid), mod_c[3], mod_c[4])
            ctx = post_norm(ctx_mid, mod_c[5] * swiglu(h2c, w_ffn_in[layer], w_ffn_out[layer]),
                            ln2_g[layer], ln2_b[layer])
        x = x_new
    return x
```

```python
import numpy as np
from contextlib import ExitStack
import concourse.bass as bass
import concourse.mybir as mybir
from concourse.bass_utils import run_bass_kernel_spmd

F32 = mybir.dt.float32
BF16 = mybir.dt.bfloat16
AF = mybir.ActivationFunctionType
ALU = mybir.AluOpType

NCORES = 8
D = 1024
T = 2048
NT = 16
TH = 2304
NTH = 18
CTX = 256
FFH = 2816
NF = 22
FG = 2
NR = NF // FG
ALPHA = 2.0 ** 0.25
EPS = 1e-5
SEQ = 8192
IN_SPLITS = (512, 640, 768, 1280, 1792, 2816, 3840)

DEBUG_STOP = None


class Sched:
    ENG = ('pe', 'act', 'dve', 'pool', 'sp')

    def __init__(self):
        self.q = {e: [] for e in self.ENG}
        self.cnt = {e: 0 for e in self.ENG}
        self.waited = {e: {} for e in self.ENG}
        self.dmacnt = {}

    def _resolve(self, eng, deps):
        ws = []
        stack = [deps]
        flat = []
        while stack:
            d = stack.pop()
            if d is None:
                continue
            if isinstance(d, tuple) and len(d) == 2 and isinstance(d[0], str):
                flat.append(d)
            else:
                stack.extend(list(d))
        for (p, t) in flat:
            if self.waited[eng].get(p, 0) >= t:
                continue
            self.waited[eng][p] = t
            ws.append((p, t))
        return ws

    def op(self, eng, fn, deps=(), sig=True):
        ws = self._resolve(eng, deps)
        tick = None
        if sig:
            self.cnt[eng] += 1
            tick = (eng, self.cnt[eng])
        self.q[eng].append((ws, fn, eng if sig else None, 1))
        return tick

    def dma(self, eng, fn, key, deps=()):
        ws = self._resolve(eng, deps)
        self.dmacnt[key] = self.dmacnt.get(key, 0) + 16
        self.q[eng].append((ws, fn, 'dma:' + key, 16))
        return ('dma:' + key, self.dmacnt[key])

    def barrier(self):
        ticks = [(e, c) for e, c in self.cnt.items() if c > 0]
        ticks += [('dma:' + k, c) for k, c in self.dmacnt.items()]
        for e in self.ENG:
            ws = self._resolve(e, ticks)
            if ws:
                self.q[e].append((ws, None, None, 0))

    def emit(self, nc, es):
        sems = {}
        for e in self.ENG:
            sems[e] = es.enter_context(nc.semaphore("s_" + e))
        for k in self.dmacnt:
            sems['dma:' + k] = es.enter_context(nc.semaphore("d_" + k))
        block = es.enter_context(nc.Block())
        reg = {'pe': block.tensor, 'act': block.scalar, 'dve': block.vector,
               'pool': block.gpsimd, 'sp': block.sync}
        for e in self.ENG:
            items = self.q[e]

            def body(engine, items=items):
                for (ws, fn, sigkey, inc) in items:
                    for (p, t) in ws:
                        engine.wait_ge(sems[p], t)
                    if fn is None:
                        continue
                    ins = fn(engine)
                    if sigkey is not None:
                        ins.then_inc(sems[sigkey], inc)
            reg[e](body)


class Arena:
    def __init__(self, nc, es, nbytes):
        self.t = es.enter_context(nc.sbuf_tensor("arena", [128, nbytes // 2], BF16))
        self.nbytes = nbytes
        self.top = 0
        self.marks = {}

    def alloc(self, dtype, shape, at=None):
        esz = 4 if dtype == F32 else 2
        n = int(np.prod(shape))
        nb = (n * esz + 63) // 64 * 64
        if at is None:
            off = self.top
            self.top += nb
        else:
            off = at
        assert off + nb <= self.nbytes, ("arena overflow", off, nb, self.nbytes)
        a = self.t[:, off // 2: off // 2 + n * esz // 2]
        if dtype == F32:
            a = a.bitcast(F32)
        if len(shape) == 2:
            a = a.rearrange('p (a b) -> p a b', a=shape[0])
        elif len(shape) == 3:
            a = a.rearrange('p (a b c) -> p a b c', a=shape[0], b=shape[1])
        return a, off, nb


def run_pipeline(stages, n_items, hook=None):
    K = len(stages)
    for it in range(n_items + K - 1):
        for k in reversed(range(K)):
            i = it - k
            if 0 <= i < n_items:
                stages[k](i)
        if hook is not None:
            hook(it)


def build_nc(debug_stop=None):
    nc = bass.Bass("TRN2", target_bir_lowering=False)
    es = ExitStack()
    S = Sched()

    def din(name, shape):
        return nc.dram_tensor(name, list(shape), F32, kind="ExternalInput").ap()

    xh = din("xh", [TH, D])
    ctxb = din("ctxb", [CTX, D])
    cvec = din("cvec", [128, 16])
    wada = din("wada", [12, 128, 8 * 512])
    badac_d = din("badac", [128, 64])
    badabc_d = din("badabc", [128, 2048])
    wqk_d = din("wqk", [128, 8 * 640])
    wu_d = din("wu", [128, 8 * 512])
    wvvb_d = din("wvvb", [128, 8 * 640])
    wc1_d = din("wc1", [8, 128, 24 * 128])
    wout_d = din("wout", [128, 8 * 1024])
    wst_d = din("wst", [128, 8 * 128])
    bsbc_d = din("bsbc", [128, 512])
    wff1_d = din("wff1", [NR, 128, 8 * 512])
    wff2_d = din("wff2", [NR, 128, FG * 1024])
    ropec_d = din("ropec", [128, TH])
    ropes_d = din("ropes", [128, TH])
    masks_d = din("masks", [128, 4 * 128])
    ident_d = din("ident", [128, 128])
    perm_d = din("perm", [128, 128])
    esink_d = din("esink", [128, 8])
    glg_d = din("glg", [128, 512])
    glb_d = din("glb", [128, 512])
    ln1g_d = din("ln1g", [128, D])
    ln1b_d = din("ln1b", [128, D])
    ln2g_d = din("ln2g", [128, D])
    ln2b_d = din("ln2b", [128, D])
    out_d = nc.dram_tensor("out", [T, D], F32, kind="ExternalOutput").ap()
    dbg = {}

    ARENA_BYTES = 207 * 1024
    A = Arena(nc, es, ARENA_BYTES)
    banks = [es.enter_context(nc.psum_tensor("bank%d" % i, [128, 512], F32)) for i in range(8)]

    def bk(i):
        return banks[i][:, :]

    def bkb(i):
        return banks[i][:, :].bitcast(BF16)

    ident, _, _ = A.alloc(BF16, [128])
    perm, _, _ = A.alloc(BF16, [128])
    masks, _, _ = A.alloc(BF16, [4, 128])
    esink, _, _ = A.alloc(F32, [8])
    modc, _, _ = A.alloc(F32, [64])
    sc, _, _ = A.alloc(BF16, [16])
    small, _, _ = A.alloc(F32, [64])
    onesf, _, _ = A.alloc(F32, [128])
    neghalf, _, _ = A.alloc(F32, [32])
    bsbc, _, _ = A.alloc(F32, [512])
    g2bc, _, _ = A.alloc(F32, [1024])
    stt, _, _ = A.alloc(F32, [8, 16])
    mvr, _, _ = A.alloc(F32, [8, 8])
    st_free = [None] * 8
    epst, _, _ = A.alloc(F32, [16])
    CONST_END = A.top

    RA = A.top
    hT, _, nbA = A.alloc(BF16, [8, TH])
    RB = A.top
    RB_SIZE = 64 * 1024
    A.top = RB + RB_SIZE
    RC = A.top
    RC_SIZE = 36 * 1024
    A.top = RC + RC_SIZE
    RD = A.top
    RD_SIZE = 32 * 1024
    A.top = RD + RD_SIZE
    RE = A.top
    RE_SIZE = ARENA_BYTES - RE
    assert RE_SIZE >= 24 * 1024, RE_SIZE

    o = RB
    qT, _, nb = A.alloc(BF16, [4, T], at=o); o += nb
    kT, _, nb = A.alloc(BF16, [TH], at=o); o += nb
    kcT, _, nb = A.alloc(BF16, [CTX], at=o); o += nb
    vaug, _, nb = A.alloc(BF16, [NTH, 2, 66], at=o); o += nb
    vcaug, _, nb = A.alloc(BF16, [2, 2, 66], at=o); o += nb
    guT, _, nb = A.alloc(BF16, [4, T], at=o); o += nb
    vn, _, nb = A.alloc(BF16, [NT, 512], at=o); o += nb
    assert o <= RB + RB_SIZE, (o - RB)
    xmid, _, _ = A.alloc(F32, [NT, D], at=RB)
    h2T, _, _ = A.alloc(BF16, [8, T], at=RA)
    hcT_off = None
    yaT, _, nb1 = A.alloc(BF16, [4, T], at=RC)
    ybT, _, nb2 = A.alloc(BF16, [4, T], at=RC + nb1)
    merged, _, _ = A.alloc(BF16, [8, T], at=RD)

    cdeps = []

    def sp_load(dst, src, key='const'):
        return S.dma('sp', lambda e, dst=dst, src=src: e.dma_start(out=dst, in_=src), key)

    def pool_cast_load(dst, src, key, deps=()):
        return S.dma('pool', lambda e, dst=dst, src=src: e.dma_start(out=dst, in_=src, max_dma_last_dim=4096), key, deps)

    cvec_f, _, _ = A.alloc(F32, [16], at=RE)
    badac, _, _ = A.alloc(F32, [64], at=RE + 64)
    t_cvec = sp_load(cvec_f, cvec, 'cvec')
    t_badac = sp_load(badac, badac_d, 'badac')
    t_c = [sp_load(esink, esink_d), sp_load(bsbc, bsbc_d), ]
    t_cb = [pool_cast_load(ident, ident_d, 'cb'), pool_cast_load(perm, perm_d, 'cb'),
            pool_cast_load(masks.rearrange('p a b -> p (a b)'), masks_d, 'cb')]
    t_m1 = S.op('pool', lambda e: e.memset(onesf, 1.0))
    t_m2 = S.op('pool', lambda e: e.memset(neghalf, -0.5))
    t_m3 = S.op('pool', lambda e: e.memset(epst, EPS))
    T_CONST = [t_c[-1], t_cb[-1], t_m1, t_m2]

    o = RC
    wqk, _, nb = A.alloc(BF16, [8, 640], at=o); o += nb
    wu, _, nb = A.alloc(BF16, [8, 512], at=o); o += nb
    wvvb, _, nb = A.alloc(BF16, [8, 640], at=o); o += nb
    hcT, _, nb = A.alloc(BF16, [8, CTX], at=o); o += nb
    scb, _, nb = A.alloc(BF16, [8, 128], at=o); o += nb
    bslice, _, nb = A.alloc(F32, [512], at=o); o += nb
    assert o <= RC + RC_SIZE, (o - RC)

    o = RB
    wring0 = []
    for i in range(4):
        w_, _, nb = A.alloc(BF16, [8, 512], at=o); o += nb
        wring0.append(w_)
    assert o <= RB + RB_SIZE, o
    g1bc, _, nb = A.alloc(F32, [1024], at=RE + 384)
    RE_FREE = RE + 384 + nb
    wring1 = []
    o = RE_FREE + 8192
    for i in range(2):
        w_, _, nb = A.alloc(BF16, [8, 512], at=o); o += nb
        wring1.append(w_)
    assert o <= ARENA_BYTES, (o, ARENA_BYTES)

    t_sc = S.op('act', lambda e: e.activation(out=sc, in_=cvec_f, func=AF.Silu), deps=[t_cvec])
    t_scb = S.op('dve', lambda e: e.tensor_copy(out=scb, in_=sc[:, 0:8].unsqueeze(2).to_broadcast([128, 8, 128])),
                 deps=[t_sc])
    colkind = {0: 0, 1: 0, 2: 1, 3: 1, 6: 2, 7: 2, 8: 3, 9: 3}
    bcform = {4: (0, 0), 5: (0, 1), 10: (1, 0), 11: (1, 1)}
    t_g = {}
    modc4 = modc.rearrange('p (a b c) -> p a b c', a=4, b=8)

    def p0_col(blk, wb_, t_w, pm):
        kind = colkind[blk]
        t_l = None
        for s in range(4):
            chunk = (blk % 2) * 4 + s
            col = (kind * 8 + chunk) * 2
            for k in range(8):
                t_l = S.op('pe', lambda e, k=k, s=s, col=col, wb_=wb_, pm=pm: e.matmul(
                    pm[:, col:col + 2], lhsT=wb_[:, k, s * 128:(s + 1) * 128], rhs=sc[:, k:16:8],
                    start=(k == 0), stop=(k == 7)), deps=[t_w, t_sc] if k == 0 else (),
                    sig=(k == 7 and s == 3))
        return t_l

    def modc_evac(lo, hi, plus1, pm, dep):
        t0_ = S.op('dve', lambda e: e.tensor_tensor(out=modc[:, lo:hi], in0=pm[:, lo:hi], in1=badac[:, lo:hi], op=ALU.add),
                   deps=[dep, t_badac])
        if plus1:
            t0_ = S.op('dve', lambda e: e.tensor_scalar(out=modc[:, lo:hi], in0=modc[:, lo:hi], scalar1=1.0,
                                                        scalar2=None, op0=ALU.add), deps=[t0_])
        return t0_

    t_l = None
    t_modc01_A, t_modc01_B = [], []
    for blk in (0, 2, 1, 3):
        t_w = pool_cast_load(wring0[blk].rearrange('p a b -> p (a b)'), wada[blk], 'wada_e%d' % blk)
        pm_ = bk(0) if blk in (0, 2) else bk(1)
        t_l = p0_col(blk, wring0[blk], t_w, pm_)
        if blk == 2:
            t_modc01_A = [modc_evac(0, 8, False, bk(0), t_l), modc_evac(16, 24, True, bk(0), t_l)]
        if blk == 3:
            t_modc01_B = [modc_evac(8, 16, False, bk(1), t_l), modc_evac(24, 32, True, bk(1), t_l)]
    t_modc01 = t_modc01_A + t_modc01_B
    t_wqk = pool_cast_load(wqk.rearrange('p a b -> p (a b)'), wqk_d, 'wqk')
    t_wvvb = pool_cast_load(wvvb.rearrange('p a b -> p (a b)'), wvvb_d, 'wvvb')
    t_wu = pool_cast_load(wu.rearrange('p a b -> p (a b)'), wu_d, 'wu')

    def mcol(kind, chunk, which):
        return modc4[:, kind, chunk, which:which + 1]

    def stats_part(src, slot, deps, n=1024):
        st = stt[:, slot]
        mv = mvr[:, slot]
        if n == 1024:
            S.op('dve', lambda e: e.bn_stats(out=st[:, 0:6], in_=src[:, 0:512]), deps=[deps, st_free[slot]], sig=False)
            t1 = S.op('dve', lambda e: e.bn_stats(out=st[:, 6:12], in_=src[:, 512:1024]))
            t2 = S.op('dve', lambda e: e.bn_aggr(out=mv[:, 0:2], in_=st[:, 0:12]), deps=[t1])
        else:
            t1 = S.op('dve', lambda e: e.bn_stats(out=st[:, 0:6], in_=src), deps=[deps, st_free[slot]])
            t2 = S.op('dve', lambda e: e.bn_aggr(out=mv[:, 0:2], in_=st[:, 0:6]), deps=[t1])
        return S.op('act', lambda e: e.activation(out=mv[:, 4:5], in_=mv[:, 1:2], func=AF.Sqrt, bias=epst[:, 0:1], scale=1.0),
                    deps=[t2, t_m3])

    def rstd_part(slot, deps):
        mv = mvr[:, slot]
        t4 = S.op('dve', lambda e: e.reciprocal(out=mv[:, 2:3], in_=mv[:, 4:5]), deps=[deps])
        t5 = S.op('dve', lambda e: e.scalar_tensor_tensor(out=mv[:, 3:4], in0=mv[:, 0:1], scalar=-1.0, in1=mv[:, 2:3],
                                                          op0=ALU.mult, op1=ALU.mult), deps=[t4])
        return mv[:, 0:1], mv[:, 2:3], mv[:, 3:4], t5

    o = RD
    NXR = 6
    xr = []
    for i in range(NXR):
        x_, _, nb = A.alloc(F32, [D], at=o); o += nb
        xr.append(x_)
    xnb = []
    for i in range(2):
        x_, _, nb = A.alloc(BF16, [D], at=o); o += nb
        xnb.append(x_)
    assert o <= RD + RD_SIZE, (o - RD)

    xr_free = [None] * NXR
    xnb_free = [None, None]
    bankA_free = [None, None]
    bankB_free = [None, None]
    t_hT = [None] * NTH
    t_hcT = [None] * 2
    tiles = [('x', i) for i in range(NTH)] + [('c', i) for i in range(2)]
    NA1 = len(tiles)
    a1 = {}
    for it in range(NA1 + 2):
        if 0 <= it - 2 < NA1:
            n = it - 2
            kindt, ti = tiles[n]
            bs_ = n % 2
            ptA = bkb(4 + 2 * bs_)
            ptB = bkb(5 + 2 * bs_)
            which = 0 if kindt == 'x' else 1
            for c in range(8):
                dst = hT[:, c, ti * 128:(ti + 1) * 128] if kindt == 'x' else hcT[:, c, ti * 128:(ti + 1) * 128]
                if c < 4:
                    t_evA = S.op('act', lambda e, c=c, dst=dst, ptA=ptA, which=which: e.activation(
                        out=dst, in_=ptA[:, c * 128:(c + 1) * 128], func=AF.Identity,
                        scale=mcol(1, c, which), bias=mcol(0, c, which)),
                        deps=[a1[n]['trA'], t_modc01_A] if c == 0 else (), sig=(c == 3))
                else:
                    t_evB = S.op('dve', lambda e, c=c, dst=dst, ptB=ptB, which=which: e.tensor_scalar(
                        out=dst, in0=ptB[:, (c - 4) * 128:(c - 3) * 128], scalar1=mcol(1, c, which),
                        scalar2=mcol(0, c, which), op0=ALU.mult, op1=ALU.add),
                        deps=[a1[n]['trB'], t_modc01_B] if c == 4 else (), sig=(c == 7))
            bankA_free[bs_] = t_evA
            bankB_free[bs_] = t_evB
            if kindt == 'x':
                t_hT[ti] = [t_evA, t_evB]
            else:
                t_hcT[ti] = [t_evA, t_evB]
        if 0 <= it - 1 < NA1:
            n = it - 1
            kindt, ti = tiles[n]
            xs = n % NXR
            bs_ = n % 2
            mean, rstd, nmr, t_r = rstd_part(n % 4, [a1[n]['sq']])
            t_xn = S.op('act', lambda e, dst=xnb[bs_], src=xr[xs], rstd=rstd, nmr=nmr: e.activation(
                out=dst, in_=src, func=AF.Identity, scale=rstd, bias=nmr), deps=[t_r, xnb_free[bs_]])
            xr_free[xs] = t_xn
            st_free[n % 4] = t_xn
            ptA = bkb(4 + 2 * bs_)
            ptB = bkb(5 + 2 * bs_)
            for c in range(8):
                pt = ptA if c < 4 else ptB
                t_tr = S.op('pe', lambda e, c=c, pt=pt, src=xnb[bs_]: e.transpose(
                    pt[:, (c % 4) * 128:(c % 4 + 1) * 128], src[:, c * 128:(c + 1) * 128], ident),
                    deps=[t_xn, bankA_free[bs_], bankB_free[bs_], T_CONST] if c == 0 else (), sig=(c == 3 or c == 7))
                if c == 3:
                    a1[n]['trA'] = t_tr
            a1[n]['trB'] = t_tr
            xnb_free[bs_] = t_tr
        if it < NA1:
            n = it
            kindt, ti = tiles[n]
            xs = n % NXR
            src = xh[ti * 128:(ti + 1) * 128, :] if kindt == 'x' else ctxb[ti * 128:(ti + 1) * 128, :]
            t_ld = S.dma('sp', lambda e, dst=xr[xs], src=src: e.dma_start(out=dst, in_=src), 'xr%d' % xs,
                         deps=[xr_free[xs]])
            a1[n] = {'sq': stats_part(xr[xs], n % 4, [t_ld])}
    if debug_stop == 'A1':
        dbg['hT'] = (hT.rearrange('p a b -> p (a b)'), [128, 8 * TH], BF16)
        dbg['hcT'] = (hcT.rearrange('p a b -> p (a b)'), [128, 8 * CTX], BF16)
        return finish(nc, es, S, dbg, t_hT + t_hcT)

    S.barrier()
    o = RD
    ropec, _, nb = A.alloc(F32, [TH], at=o); o += nb
    ropes, _, nb = A.alloc(F32, [TH], at=o); o += nb
    glg, _, nb = A.alloc(F32, [512], at=o); o += nb
    glb, _, nb = A.alloc(F32, [512], at=o); o += nb
    qraw = []
    for i in range(2):
        q_, _, nb = A.alloc(BF16, [512], at=o); o += nb
        qraw.append(q_)
    rt1 = []
    rt2 = []
    for i in range(2):
        q_, _, nb = A.alloc(F32, [512], at=o); o += nb
        rt1.append(q_)
        q_, _, nb = A.alloc(F32, [512], at=o); o += nb
        rt2.append(q_)
    assert o <= RD + RD_SIZE, (o - RD)
    o = RE_FREE
    gvb = []
    for i in range(4):
        q_, _, nb = A.alloc(F32, [512], at=o); o += nb
        gvb.append(q_)
    assert o <= RE_FREE + 8192, (o, RE_FREE)
    t_rope = [sp_load(ropec, ropec_d, 'rope'), sp_load(ropes, ropes_d, 'rope')][-1]
    t_gl = [sp_load(glg, glg_d, 'gl'), sp_load(glb, glb_d, 'gl')][-1]
    t_vones = S.op('pool', lambda e: e.memset(vaug[:, :, :, 64:66], 1.0))
    t_vcones = S.op('pool', lambda e: e.memset(vcaug[:, :, :, 64:66], 1.0))

    fb_free = [None] * 8
    qraw_free = [None, None]
    rt_free = [None, None]

    dq = {'blocks': [4, 5, 6, 7, 8, 9, 10, 11], 'issued': [], 'n_issued': 0, 'n_done': 0, 'bs_free': None, 'pb': 0}
    ring1_free = [None, None]
    t_modc23 = []

    def p0_issue():
        if dq['n_issued'] >= len(dq['blocks']):
            return
        blk = dq['blocks'][dq['n_issued']]
        slot = dq['n_issued'] % 2
        dq['n_issued'] += 1
        t_w = pool_cast_load(wring1[slot].rearrange('p a b -> p (a b)'), wada[blk], 'wada_d%d' % slot,
                             deps=[ring1_free[slot]])
        dq['issued'].append((blk, slot, t_w))

    def p0_compute(col_bank=6, bc_bank=7):
        if dq['n_done'] >= len(dq['blocks']):
            return
        blk, slot, t_w = dq['issued'][dq['n_done']]
        dq['n_done'] += 1
        wb_ = wring1[slot]
        if blk in colkind:
            pm = bk(col_bank)
            kind = colkind[blk]
            t_l = None
            for s in range(4):
                chunk = (blk % 2) * 4 + s
                col = (kind * 8 + chunk) * 2
                for k in range(8):
                    t_l = S.op('pe', lambda e, k=k, s=s, col=col, wb_=wb_, pm=pm: e.matmul(
                        pm[:, col:col + 2], lhsT=wb_[:, k, s * 128:(s + 1) * 128], rhs=sc[:, k:16:8],
                        start=(k == 0), stop=(k == 7)), deps=[t_w, t_sc, fb_free[col_bank]] if k == 0 else (),
                        sig=(k == 7 and s == 3))
            ring1_free[slot] = t_l
            lo = (kind * 8 + (blk % 2) * 4) * 2
            t_e = modc_evac(lo, lo + 8, kind in (1, 3), pm, t_l)
            fb_free[col_bank] = t_e
            t_modc23.append(t_e)
        else:
            which, half = bcform[blk]
            bank = bc_bank
            pb = bk(bank)
            gi_ = which * 2 + half
            t_bs = S.dma('sp', lambda e, gi_=gi_: e.dma_start(out=bslice, in_=badabc_d[:, gi_ * 512:(gi_ + 1) * 512]),
                         'bslice', deps=[dq['bs_free']])
            for k in range(8):
                t_mm = S.op('pe', lambda e, k=k, pb=pb, wb_=wb_: e.matmul(
                    pb, lhsT=scb[:, k, :], rhs=wb_[:, k, :], start=(k == 0), stop=(k == 7)),
                    deps=[t_w, t_scb, fb_free[bank]] if k == 0 else (), sig=(k == 7))
            ring1_free[slot] = t_mm
            dst = (g1bc if which == 0 else g2bc)[:, half * 512:(half + 1) * 512]
            t_e = S.op('dve', lambda e, dst=dst, pb=pb: e.tensor_tensor(out=dst, in0=pb, in1=bslice, op=ALU.add),
                       deps=[t_mm, t_bs])
            dq['bs_free'] = t_e
            fb_free[bank] = t_e
            t_g[(which, half)] = t_e
        p0_issue()

    p0_issue()
    p0_issue()
    units = []
    for m in range(4):
        for tg in range(4):
            units.append(('q', m, 128 + tg * 512, 512, tg * 512))
    for tg in range(5):
        n_ = 512 if tg < 4 else 256
        units.append(('k', 4, tg * 512, n_, tg * 512))
    t_q = {}
    t_qk_all = []
    qk = [dict() for _ in units]

    def qk_part1(ui):
        kd, m, hcol, n_, ocol = units[ui]
        b0 = (ui % 3) * 2
        pq = bk(b0)[:, 0:n_]
        for k in range(8):
            t_mm = S.op('pe', lambda e, k=k, pq=pq, m=m, hcol=hcol, n_=n_: e.matmul(
                pq, lhsT=wqk[:, k, m * 128:(m + 1) * 128], rhs=hT[:, k, hcol:hcol + n_],
                start=(k == 0), stop=(k == 7)),
                deps=[t_wqk, fb_free[b0], fb_free[b0 + 1]] if k == 0 else (), sig=(k == 7))
        r = ui % 2
        t_raw = S.op('act', lambda e, dst=qraw[r][:, 0:n_], pq=pq: e.activation(out=dst, in_=pq, func=AF.Copy),
                     deps=[t_mm, qraw_free[r]])
        qk[ui]['mm'] = t_mm
        qk[ui]['raw'] = t_raw

    def qk_part2(ui):
        kd, m, hcol, n_, ocol = units[ui]
        b0 = (ui % 3) * 2
        pq, ps_ = bk(b0)[:, 0:n_], bk(b0 + 1)[:, 0:n_]
        r = ui % 2
        t_mm, t_raw = qk[ui]['mm'], qk[ui]['raw']
        t_pm = S.op('pe', lambda e, ps_=ps_, src=qraw[r][:, 0:n_]: e.matmul(ps_, lhsT=perm, rhs=src, start=True, stop=True),
                    deps=[t_raw, T_CONST])
        qraw_free[r] = t_pm
        t_1 = S.op('dve', lambda e, dst=rt1[r][:, 0:n_], pq=pq, hcol=hcol, n_=n_: e.tensor_tensor(
            out=dst, in0=pq, in1=ropec[:, hcol:hcol + n_], op=ALU.mult), deps=[t_mm, t_raw, t_rope, rt_free[r]])
        t_2 = S.op('dve', lambda e, dst=rt2[r][:, 0:n_], ps_=ps_, hcol=hcol, n_=n_: e.tensor_tensor(
            out=dst, in0=ps_, in1=ropes[:, hcol:hcol + n_], op=ALU.mult), deps=[t_pm])
        fb_free[b0] = t_2
        fb_free[b0 + 1] = t_2
        dst = qT[:, m, ocol:ocol + n_] if kd == 'q' else kT[:, ocol:ocol + n_]
        t_3 = S.op('pool', lambda e, dst=dst, a=rt1[r][:, 0:n_], b=rt2[r][:, 0:n_]: e.tensor_tensor(
            out=dst, in0=a, in1=b, op=ALU.add), deps=[t_1, t_2])
        rt_free[r] = t_3
        t_qk_all.append(t_3)

    for ui in range(len(units)):
        qk_part1(ui)
        if ui >= 1:
            qk_part2(ui - 1)
        if ui in (6, 13, 20):
            p0_compute()
    qk_part2(len(units) - 1)
    if debug_stop == 'A2a':
        dbg['qT'] = (qT.rearrange('p a b -> p (a b)'), [128, 4 * T], BF16)
        dbg['kT'] = (kT, [128, TH], BF16)
        return finish(nc, es, S, dbg, t_qk_all)
    pq = bk(0)[:, 0:CTX]
    for k in range(8):
        t_mm = S.op('pe', lambda e, k=k, pq=pq: e.matmul(pq, lhsT=wqk[:, k, 512:640], rhs=hcT[:, k, :],
                                                         start=(k == 0), stop=(k == 7)),
                    deps=[fb_free[0], t_hcT] if k == 0 else (), sig=(k == 7))
    t_kc = S.op('act', lambda e, pq=pq: e.activation(out=kcT, in_=pq, func=AF.Copy), deps=[t_mm])
    fb_free[0] = t_kc
    t_gu = []
    ui = 0
    for m in range(4):
        for tg in range(4):
            b0 = 1 + (ui % 3); ui += 1
            pq = bk(b0)
            for k in range(8):
                t_mm = S.op('pe', lambda e, k=k, pq=pq, m=m, tg=tg: e.matmul(
                    pq, lhsT=wu[:, k, m * 128:(m + 1) * 128], rhs=hT[:, k, 128 + tg * 512:128 + (tg + 1) * 512],
                    start=(k == 0), stop=(k == 7)), deps=[t_wu, fb_free[b0]] if k == 0 else (), sig=(k == 7))
            t_e = S.op('act', lambda e, pq=pq, m=m, tg=tg: e.activation(
                out=guT[:, m, tg * 512:(tg + 1) * 512], in_=pq, func=AF.Gelu_apprx_tanh), deps=[t_mm])
            fb_free[b0] = t_e
            t_gu.append(t_e)
            if ui in (5, 10, 15):
                p0_compute()
    if debug_stop == 'A2b':
        dbg['kcT'] = (kcT, [128, CTX], BF16)
        dbg['guT'] = (guT.rearrange('p a b -> p (a b)'), [128, 4 * T], BF16)
        return finish(nc, es, S, dbg, [t_kc] + t_gu)
    gvb_free = [None] * 4
    t_v = [None] * NTH
    t_vc = [None] * 2
    t_vn = [None] * NT
    vb_ = [dict() for _ in tiles]

    def is_main(n):
        kindt, ti = tiles[n]
        return kindt == 'x' and 1 <= ti <= NT

    def v_s0(n):
        kindt, ti = tiles[n]
        bv = 4 + (n % 2)
        pv = bk(bv)[:, 0:128]
        for k in range(8):
            lh = hT[:, k, ti * 128:(ti + 1) * 128] if kindt == 'x' else hcT[:, k, ti * 128:(ti + 1) * 128]
            t_mm = S.op('pe', lambda e, k=k, pv=pv, lh=lh: e.matmul(pv, lhsT=lh, rhs=wvvb[:, k, 0:128],
                                                                   start=(k == 0), stop=(k == 7)),
                        deps=[t_wvvb, fb_free[bv]] if k == 0 else (), sig=(k == 7))
        vb_[n]['v'] = t_mm
        if is_main(n):
            mt = ti - 1
            bvb = 6 + (mt % 2)
            pvb = bk(bvb)
            for k in range(8):
                t_mm = S.op('pe', lambda e, k=k, pvb=pvb, ti=ti: e.matmul(
                    pvb, lhsT=hT[:, k, ti * 128:(ti + 1) * 128], rhs=wvvb[:, k, 128:640],
                    start=(k == 0), stop=(k == 7)), deps=[fb_free[bvb]] if k == 0 else (), sig=(k == 7))
            vb_[n]['vb'] = t_mm

    def v_s1(n):
        kindt, ti = tiles[n]
        bv = 4 + (n % 2)
        pv = bk(bv)[:, 0:128]
        dstv = (vaug[:, ti, :, 0:64] if kindt == 'x' else vcaug[:, ti, :, 0:64])
        t_e = S.op('act', lambda e, dstv=dstv, pv=pv: e.activation(
            out=dstv, in_=pv.rearrange('p (a b) -> p a b', a=2), func=AF.Copy), deps=[vb_[n]['v'], t_vones, t_vcones])
        fb_free[bv] = t_e
        if kindt == 'x':
            t_v[ti] = t_e
        else:
            t_vc[ti] = t_e
        if is_main(n):
            mt = ti - 1
            bvb = 6 + (mt % 2)
            g_ = gvb[mt % 4]
            t_ge = S.op('act', lambda e, g_=g_, pvb=bk(bvb): e.activation(out=g_, in_=pvb, func=AF.Gelu_apprx_tanh),
                        deps=[vb_[n]['vb'], gvb_free[mt % 4]])
            fb_free[bvb] = t_ge
            vb_[n]['ge'] = t_ge

    def v_s2(n):
        if not is_main(n):
            return
        mt = tiles[n][1] - 1
        sl = mt % 4
        st = stt[:, sl]; mv = mvr[:, sl]
        g_ = gvb[mt % 4]
        t1 = S.op('dve', lambda e: e.bn_stats(out=st[:, 0:6], in_=g_), deps=[vb_[n]['ge'], st_free[sl]])
        vb_[n]['ag'] = S.op('dve', lambda e: e.bn_aggr(out=mv[:, 0:2], in_=st[:, 0:6]), deps=[t1])

    def v_s3(n):
        if not is_main(n):
            return
        mt = tiles[n][1] - 1
        mv = mvr[:, mt % 4]
        t3 = S.op('pool', lambda e: e.tensor_scalar(out=mv[:, 4:5], in0=mv[:, 1:2], scalar1=EPS, scalar2=None,
                                                    op0=ALU.add), deps=[vb_[n]['ag']])
        vb_[n]['rs'] = S.op('pool', lambda e: e.tensor_tensor(out=mv[:, 2:3], in0=mv[:, 4:5], in1=neghalf[:, 0:1],
                                                              op=ALU.pow), deps=[t3, t_m2])

    def v_s4(n):
        if not is_main(n):
            return
        mt = tiles[n][1] - 1
        mv = mvr[:, mt % 4]
        g_ = gvb[mt % 4]
        tmp_ = rt1[mt % 2]
        t5 = S.op('dve', lambda e: e.scalar_tensor_tensor(
            out=tmp_, in0=g_, scalar=mv[:, 0:1], in1=glg, op0=ALU.subtract, op1=ALU.mult),
            deps=[vb_[n]['rs'], t_gl, rt_free[mt % 2]])
        t6 = S.op('dve', lambda e: e.scalar_tensor_tensor(
            out=vn[:, mt, :], in0=tmp_, scalar=mv[:, 2:3], in1=glb, op0=ALU.mult, op1=ALU.add), deps=[t5])
        rt_free[mt % 2] = t6
        gvb_free[mt % 4] = t6
        st_free[mt % 4] = t6
        t_vn[mt] = t6

    def v_hook(it):
        if it in (5, 12):
            p0_compute(col_bank=0, bc_bank=1)

    run_pipeline([v_s0, v_s1, v_s2, v_s3, v_s4], len(tiles), hook=v_hook)
    while dq['n_done'] < len(dq['blocks']):
        p0_compute(col_bank=0, bc_bank=1)
    t_modc = t_modc01 + t_modc23
    t_g1 = [t_g[(0, 0)], t_g[(0, 1)]]
    t_g2 = [t_g[(1, 0)], t_g[(1, 1)]]

    if debug_stop == 'A2':
        dbg['qT'] = (qT.rearrange('p a b -> p (a b)'), [128, 4 * T], BF16)
        dbg['kT'] = (kT, [128, TH], BF16)
        dbg['kcT'] = (kcT, [128, CTX], BF16)
        dbg['vaug'] = (vaug.rearrange('p a b c -> p (a b c)'), [128, NTH * 2 * 66], BF16)
        dbg['vcaug'] = (vcaug.rearrange('p a b c -> p (a b c)'), [128, 2 * 2 * 66], BF16)
        dbg['guT'] = (guT.rearrange('p a b -> p (a b)'), [128, 4 * T], BF16)
        dbg['vn'] = (vn.rearrange('p a b -> p (a b)'), [128, NT * 512], BF16)
        return finish(nc, es, S, dbg, t_qk_all + [t_kc] + t_gu + t_v + t_vc + t_vn)

    S.barrier()
    o = RD
    wst, _, nb = A.alloc(BF16, [8, 128], at=o); o += nb
    NPT = 20
    PT = []
    for i in range(NPT):
        p_, _, nb = A.alloc(BF16, [4, 128], at=o); o += nb
        PT.append(p_)
    yatm = []
    for i in range(2):
        p_, _, nb = A.alloc(BF16, [512], at=o); o += nb
        yatm.append(p_)
    sbt = []
    for i in range(2):
        p_, _, nb = A.alloc(F32, [512], at=o); o += nb
        sbt.append(p_)
    dens, _, nb = A.alloc(F32, [2, 16], at=o); o += nb
    esk, _, nb = A.alloc(F32, [8], at=o); o += nb
    assert o <= RD + RD_SIZE, (o - RD)
    o = RE_FREE
    woutb, _, nb = A.alloc(BF16, [8, 1024], at=o); o += nb
    wstage = []
    for i in range(2):
        p_, _, nb = A.alloc(F32, [1024], at=o); o += nb
        wstage.append(p_)
    assert o <= ARENA_BYTES, (o, ARENA_BYTES)

    t_wst = pool_cast_load(wst.rearrange('p a b -> p (a b)'), wst_d, 'wst')
    t_esk = S.op('act', lambda e: e.activation(out=esk, in_=esink, func=AF.Exp), deps=[T_CONST])
    ws_free = [None, None]
    wo = {'t': None}

    def wout_step(k):
        r = k % 2
        t_l = S.dma('sp', lambda e, dst=wstage[r], k=k: e.dma_start(out=dst, in_=wout_d[:, k * 1024:(k + 1) * 1024]),
                    'wos%d' % r, deps=[ws_free[r]])
        wo['t'] = S.op('pool', lambda e, k=k, src=wstage[r]: e.tensor_tensor(out=woutb[:, k, :], in0=src, in1=g1bc, op=ALU.mult),
                       deps=[t_l] + t_g1)
        ws_free[r] = wo['t']

    fb_free = [None] * 8
    sbt_free = [None, None]
    t_yb = [None] * NT
    sc_st = {'i': 0}

    def gmlp_tile(t):
        b0 = sc_st['i'] % 4; sc_st['i'] += 1
        pf = bk(b0)
        for c in range(4):
            for gi in range(2):
                g = 2 * c + gi
                outp = pf[gi * 64:(gi + 1) * 64, c * 128:(c + 1) * 128]
                t_mm = S.op('pe', lambda e, outp=outp, g=g, t=t: e.matmul(
                    outp, lhsT=vn[:, t, g * 64:(g + 1) * 64], rhs=wst[:, g, :], start=True, stop=True),
                    deps=[t_wst, fb_free[b0], t_vn[t]] if (c == 0 and gi == 0) else (), sig=(c == 3 and gi == 1))
        r = t % 2
        t_sb = S.op('dve', lambda e, pf=pf, dst=sbt[r]: e.tensor_tensor(out=dst, in0=pf, in1=bsbc, op=ALU.add),
                    deps=[t_mm, T_CONST, sbt_free[r]])
        fb_free[b0] = t_sb
        t_yb[t] = S.op('dve', lambda e, src=sbt[r], t=t: e.tensor_tensor(
            out=ybT[:, :, t * 128:(t + 1) * 128], in0=src.rearrange('p (a b) -> p a b', a=4),
            in1=guT[:, :, t * 128:(t + 1) * 128], op=ALU.mult), deps=[t_sb])
        sbt_free[r] = t_yb[t]

    pt_free = [None] * NPT
    yatm_free = [None, None]
    o_free = [None] * 4
    t_ya = [None] * NT
    att = [dict() for _ in range(NT)]

    def att_srcs(j):
        srcs = []
        for s in range(3):
            kt = j + s
            srcs.append((kT[:, kt * 128:(kt + 1) * 128], vaug[:, kt], [t_v[kt]]))
        for s in range(2):
            srcs.append((kcT[:, s * 128:(s + 1) * 128], vcaug[:, s], [t_vc[s]]))
        return srcs

    def att_front(j):
        base = (j % 2) * 10
        srcs = att_srcs(j)
        t_p = {}
        att[j]['p'] = t_p
        for s, (ksrc, vsrc, vdep) in enumerate(srcs):
            for kv in range(2):
                bi = sc_st['i'] % 4; sc_st['i'] += 1
                psb = bk(bi)
                pt = PT[base + s * 2 + kv]
                t_mm = S.op('pe', lambda e, psb=psb, ksrc=ksrc, kv=kv, j=j: e.matmul(
                    psb, lhsT=ksrc[kv * 64:(kv + 1) * 64, :], rhs=qT[kv * 64:(kv + 1) * 64, :, j * 128:(j + 1) * 128],
                    start=True, stop=True), deps=[fb_free[bi]])
                t_e = S.op('act', lambda e, pt=pt, psb=psb: e.activation(
                    out=pt.rearrange('p a b -> p (a b)'), in_=psb, func=AF.Exp, scale=0.125),
                    deps=[t_mm, pt_free[base + s * 2 + kv]])
                fb_free[bi] = t_e
                if s == 0 or s == 2:
                    mi = (0 if j == 0 else 1) if s == 0 else (3 if j == NT - 1 else 2)
                    t_e = S.op('pool', lambda e, pt=pt, mi=mi: e.tensor_tensor(
                        out=pt, in0=pt, in1=masks[:, mi:mi + 1, :].to_broadcast([128, 4, 128]), op=ALU.mult),
                        deps=[t_e, T_CONST])
                t_p[(s, kv)] = t_e

    def att_back(j):
        base = (j % 2) * 10
        srcs = att_srcs(j)
        t_p = att[j]['p']
        t_o = [None, None]
        for kv in range(2):
            ob = bk(4 + 2 * (j % 2) + kv).rearrange('p (a b) -> p a b', a=4)
            for g in range(4):
                for s, (ksrc, vsrc, vdep) in enumerate(srcs):
                    pt = PT[base + s * 2 + kv]
                    t_mm = S.op('pe', lambda e, ob=ob, g=g, pt=pt, vsrc=vsrc, kv=kv, s=s: e.matmul(
                        ob[:, g, 0:65], lhsT=pt[:, g, :], rhs=vsrc[:, kv, 0:65], start=(s == 0), stop=(s == 4)),
                        deps=[t_p[(s, kv)], o_free[2 * (j % 2) + kv]] + vdep, sig=(g == 3 and s == 4))
            t_o[kv] = t_mm
        for s in range(5):
            for kv in range(2):
                pt_free[base + s * 2 + kv] = t_o[kv]
        r = j % 2
        dn = dens[:, r]
        ya = yatm[r].rearrange('p (k g d) -> p k g d', k=2, g=4)
        t_n = None
        for kv in range(2):
            ob = bk(4 + 2 * (j % 2) + kv).rearrange('p (a b) -> p a b', a=4)
            t_a = S.op('dve', lambda e, ob=ob, dn=dn, kv=kv: e.tensor_tensor(
                out=dn[:, kv * 4:(kv + 1) * 4], in0=ob[:, :, 64], in1=esk[:, kv * 4:(kv + 1) * 4], op=ALU.add),
                deps=[t_o[kv], t_esk, yatm_free[r]])
            t_b = S.op('dve', lambda e, dn=dn, kv=kv: e.reciprocal(out=dn[:, 8 + kv * 4:8 + (kv + 1) * 4],
                                                                  in_=dn[:, kv * 4:(kv + 1) * 4]), deps=[t_a])
            t_n = S.op('dve', lambda e, ob=ob, dn=dn, kv=kv, ya=ya: e.tensor_tensor(
                out=ya[:, kv], in0=ob[:, :, 0:64],
                in1=dn[:, 8 + kv * 4:8 + (kv + 1) * 4].unsqueeze(2).to_broadcast([128, 4, 64]), op=ALU.mult),
                deps=[t_b])
            o_free[2 * (j % 2) + kv] = t_n
        att[j]['n'] = t_n

    def att_tail(j):
        r = j % 2
        t_n = att[j]['n']
        ptb = bkb(4 + 2 * (j % 2))
        for c in range(4):
            t_tr = S.op('pe', lambda e, c=c, ptb=ptb, src=yatm[r]: e.transpose(
                ptb[:, c * 128:(c + 1) * 128], src[:, c * 128:(c + 1) * 128], ident),
                deps=[t_n] if c == 0 else (), sig=(c == 3))
        yatm_free[r] = t_tr
        t_ya[j] = S.op('dve', lambda e, ptb=ptb, j=j: e.tensor_copy(
            out=yaT[:, :, j * 128:(j + 1) * 128], in_=ptb[:, 0:512].rearrange('p (a b) -> p a b', a=4)),
            deps=[t_tr])
        o_free[2 * (j % 2)] = t_ya[j]

    att_front(0)
    for j in range(NT + 1):
        if j + 1 < NT:
            att_front(j + 1)
        if j < NT:
            gmlp_tile(j)
            att_back(j)
            if 2 <= j < 10:
                wout_step(j - 2)
        if j >= 1:
            att_tail(j - 1)

    t_wout = wo['t']
    if debug_stop == 'B':
        dbg['yaT'] = (yaT.rearrange('p a b -> p (a b)'), [128, 4 * T], BF16)
        dbg['ybT'] = (ybT.rearrange('p a b -> p (a b)'), [128, 4 * T], BF16)
        return finish(nc, es, S, dbg, t_ya + t_yb + [t_wout])

    S.barrier()
    o = RB
    wc1 = []
    t_wc1 = [None] * 8
    for m in range(8):
        p_, _, nb = A.alloc(BF16, [24, 128], at=o); o += nb
        wc1.append(p_)
        t_wc1[m] = pool_cast_load(p_.rearrange('p a b -> p (a b)'), wc1_d[m], 'wc1_%d' % m)
    outpre = {}
    sga = []
    for i in range(4):
        p_, _, nb = A.alloc(F32, [512], at=o); o += nb
        sga.append(p_)
    mt1 = []
    for i in range(4):
        p_, _, nb = A.alloc(F32, [512], at=o); o += nb
        mt1.append(p_)
    assert o <= RB + RB_SIZE, (o - RB)
    fb_free = [None] * 8
    sga_free = [None] * 4
    mt_free = [None] * 4
    t_merged = {}
    ui = 0
    for m in range(8):
        if m == 4:
            for t_ in range(NT):
                outpre['t'] = S.dma('sp', lambda e, t_=t_: e.dma_start(out=out_d[t_ * 128:(t_ + 1) * 128, :], in_=ln2b_d),
                                    'outpre')
        wm = wc1[m]
        t_w = t_wc1[m]
        for tg in range(4):
            bs_ = (ui % 2) * 4
            r2 = (ui % 2) * 2
            ui += 1
            pa, pb_, pga, pgb = bk(bs_), bk(bs_ + 1), bk(bs_ + 2), bk(bs_ + 3)
            tok = slice(tg * 512, (tg + 1) * 512)
            htok = slice(128 + tg * 512, 128 + (tg + 1) * 512)
            for k in range(8):
                t_ga = S.op('pe', lambda e, k=k, pga=pga, wm=wm, htok=htok: e.matmul(
                    pga, lhsT=wm[:, 8 + k, :], rhs=hT[:, k, htok], start=(k == 0), stop=(k == 7)),
                    deps=[t_w, fb_free[bs_ + 2]] if k == 0 else (), sig=(k == 7))
            for k in range(8):
                t_gb = S.op('pe', lambda e, k=k, pgb=pgb, wm=wm, htok=htok: e.matmul(
                    pgb, lhsT=wm[:, 16 + k, :], rhs=hT[:, k, htok], start=(k == 0), stop=(k == 7)),
                    deps=[fb_free[bs_ + 3]] if k == 0 else (), sig=(k == 7))
            for k in range(4):
                t_a = S.op('pe', lambda e, k=k, pa=pa, wm=wm, tok=tok: e.matmul(
                    pa, lhsT=wm[:, k, :], rhs=yaT[:, k, tok], start=(k == 0), stop=(k == 3)),
                    deps=[fb_free[bs_]] if k == 0 else (), sig=(k == 3))
            for k in range(4):
                t_b = S.op('pe', lambda e, k=k, pb_=pb_, wm=wm, tok=tok: e.matmul(
                    pb_, lhsT=wm[:, 4 + k, :], rhs=ybT[:, k, tok], start=(k == 0), stop=(k == 3)),
                    deps=[fb_free[bs_ + 1]] if k == 0 else (), sig=(k == 3))
            t_sa = S.op('act', lambda e, dst=sga[r2], pga=pga: e.activation(out=dst, in_=pga, func=AF.Sigmoid),
                        deps=[t_ga, sga_free[r2]])
            t_sb = S.op('act', lambda e, dst=sga[r2 + 1], pgb=pgb: e.activation(out=dst, in_=pgb, func=AF.Sigmoid),
                        deps=[t_gb, sga_free[r2 + 1]])
            fb_free[bs_ + 2] = t_sa
            fb_free[bs_ + 3] = t_sb
            t_1 = S.op('dve', lambda e, dst=mt1[r2], pa=pa, sa=sga[r2]: e.tensor_tensor(out=dst, in0=pa, in1=sa, op=ALU.mult),
                       deps=[t_a, t_sa, mt_free[r2]])
            t_2 = S.op('dve', lambda e, dst=mt1[r2 + 1], pb_=pb_, sb=sga[r2 + 1]: e.tensor_tensor(out=dst, in0=pb_, in1=sb, op=ALU.mult),
                       deps=[t_b, t_sb, mt_free[r2 + 1]])
            fb_free[bs_] = t_1
            fb_free[bs_ + 1] = t_2
            sga_free[r2] = t_1
            sga_free[r2 + 1] = t_2
            t_3 = S.op('pool', lambda e, m=m, tok=tok, a=mt1[r2], b=mt1[r2 + 1]: e.tensor_tensor(
                out=merged[:, m, tok], in0=a, in1=b, op=ALU.add), deps=[t_1, t_2])
            mt_free[r2] = t_3
            mt_free[r2 + 1] = t_3
            t_merged[(m, tg)] = t_3

    if debug_stop == 'C1':
        dbg['merged'] = (merged.rearrange('p a b -> p (a b)'), [128, 8 * T], BF16)
        return finish(nc, es, S, dbg, list(t_merged.values()) + [t_wout])

    S.barrier()
    o = RC
    NXR2, NWK = 2, 4
    xr = []
    for i in range(NXR2):
        x_, _, nb = A.alloc(F32, [D], at=o); o += nb
        xr.append(x_)
    wk = []
    for i in range(NWK):
        x_, _, nb = A.alloc(F32, [D], at=o); o += nb
        wk.append(x_)
    xnb = []
    for i in range(2):
        x_, _, nb = A.alloc(BF16, [D], at=o); o += nb
        xnb.append(x_)
    assert o <= RC + 28 * 1024, (o - RC)
    NWR = 3
    w1 = [None] * NWR
    w2 = [None] * NWR
    w1[0], _, _ = A.alloc(BF16, [8, 512], at=RC + 28 * 1024)
    t_w1_pre = pool_cast_load(w1[0].rearrange('p a b -> p (a b)'), wff1_d[0], 'wf1_0')
    ln1g, ln1b = wstage[0], wstage[1]
    t_ln1 = [sp_load(ln1g, ln1g_d, 'ln1'), sp_load(ln1b, ln1b_d, 'ln1')][-1]

    fb_free = [None] * 8
    xr_free = [None] * NXR2
    wk_free = [None] * NWK
    xnb_free = [None, None]
    t_h2 = [None] * NT
    t_xmid = [None] * NT
    c2 = [dict() for _ in range(NT)]

    def c2_s0(t):
        r = t % NXR2
        c2[t]['x'] = S.dma('sp', lambda e, dst=xr[r], t=t: e.dma_start(out=dst, in_=xh[(t + 1) * 128:(t + 2) * 128, :]),
                           'xr%d' % r, deps=[xr_free[r]])
        b0 = (t % 2) * 2
        c2[t]['mm'] = []
        for half in range(2):
            pm_ = bk(b0 + half)
            for k in range(8):
                t_mm = S.op('pe', lambda e, k=k, pm_=pm_, t=t, half=half: e.matmul(
                    pm_, lhsT=merged[:, k, t * 128:(t + 1) * 128], rhs=woutb[:, k, half * 512:(half + 1) * 512],
                    start=(k == 0), stop=(k == 7)),
                    deps=[t_wout, fb_free[b0 + half]] + [t_merged[(kk, t // 4)] for kk in range(8)] if k == 0 else (),
                    sig=(k == 7))
            c2[t]['mm'].append(t_mm)

    def c2_s1(t):
        r = t % NXR2
        w = wk[t % NWK]
        b0 = (t % 2) * 2
        for half in range(2):
            pm_ = bk(b0 + half)
            t_pre = S.op('dve', lambda e, pm_=pm_, r=r, half=half, w=w: e.scalar_tensor_tensor(
                out=w[:, half * 512:(half + 1) * 512], in0=xr[r][:, half * 512:(half + 1) * 512],
                scalar=ALPHA, in1=pm_, op0=ALU.mult, op1=ALU.add),
                deps=[c2[t]['mm'][half], c2[t]['x'], wk_free[t % NWK]])
            fb_free[b0 + half] = t_pre
        xr_free[r] = t_pre
        sl = t % 4
        st = stt[:, sl]; mv = mvr[:, sl]
        S.op('dve', lambda e: e.bn_stats(out=st[:, 0:6], in_=w[:, 0:512]), deps=[t_pre, st_free[sl]], sig=False)
        t1 = S.op('dve', lambda e: e.bn_stats(out=st[:, 6:12], in_=w[:, 512:1024]))
        c2[t]['ag1'] = S.op('dve', lambda e: e.bn_aggr(out=mv[:, 0:2], in_=st[:, 0:12]), deps=[t1])

    def c2_s2(t):
        mv = mvr[:, t % 4]
        c2[t]['sq1'] = S.op('act', lambda e: e.activation(out=mv[:, 4:5], in_=mv[:, 1:2], func=AF.Sqrt, bias=epst[:, 0:1],
                                                          scale=1.0), deps=[c2[t]['ag1'], t_m3])

    def c2_s3(t):
        _, _, _, c2[t]['r1'] = rstd_part(t % 4, [c2[t]['sq1']])

    def c2_s4(t):
        mv = mvr[:, t % 4]
        w = wk[t % NWK]
        c2[t]['n1'] = S.op('act', lambda e: e.activation(out=w, in_=w, func=AF.Identity, scale=mv[:, 2:3], bias=mv[:, 3:4]),
                           deps=[c2[t]['r1']])
        st_free[t % 4] = c2[t]['n1']

    def c2_s5(t):
        w = wk[t % NWK]
        t_g_ = S.op('pool', lambda e: e.tensor_tensor(out=xmid[:, t, :], in0=w, in1=ln1g, op=ALU.mult),
                    deps=[c2[t]['n1'], t_ln1])
        wk_free[t % NWK] = t_g_
        t_xmid[t] = S.dma('pool', lambda e: e.dma_start(out=xmid[:, t, :], in_=ln1b, accum_op=ALU.add), 'xb%d' % t,
                          deps=[t_g_, t_ln1])

    def c2_s5w(t):
        pass

    def c2_s6(t):
        sl = 4 + t % 4
        st = stt[:, sl]; mv = mvr[:, sl]
        src = xmid[:, t, :]
        S.op('dve', lambda e: e.bn_stats(out=st[:, 0:6], in_=src[:, 0:512]), deps=[t_xmid[t], st_free[sl]], sig=False)
        t1 = S.op('dve', lambda e: e.bn_stats(out=st[:, 6:12], in_=src[:, 512:1024]))
        c2[t]['ag2'] = S.op('dve', lambda e: e.bn_aggr(out=mv[:, 0:2], in_=st[:, 0:12]), deps=[t1])

    def c2_s7(t):
        mv = mvr[:, 4 + t % 4]
        c2[t]['sq2'] = S.op('act', lambda e: e.activation(out=mv[:, 4:5], in_=mv[:, 1:2], func=AF.Sqrt, bias=epst[:, 0:1],
                                                          scale=1.0), deps=[c2[t]['ag2']])

    def c2_s8(t):
        _, _, _, c2[t]['r2'] = rstd_part(4 + t % 4, [c2[t]['sq2']])

    def c2_s9(t):
        mv = mvr[:, 4 + t % 4]
        r = t % 2
        c2[t]['n2'] = S.op('act', lambda e: e.activation(out=xnb[r], in_=xmid[:, t, :], func=AF.Identity,
                                                         scale=mv[:, 2:3], bias=mv[:, 3:4]), deps=[c2[t]['r2'], xnb_free[r]])
        st_free[4 + t % 4] = c2[t]['n2']

    def c2_s10(t):
        r = t % 2
        ptA = bkb(4 + 2 * r)
        ptB = bkb(5 + 2 * r)
        for c in range(8):
            pt = ptA if c < 4 else ptB
            t_tr = S.op('pe', lambda e, c=c, pt=pt, src=xnb[r]: e.transpose(
                pt[:, (c % 4) * 128:(c % 4 + 1) * 128], src[:, c * 128:(c + 1) * 128], ident),
                deps=[c2[t]['n2'], fb_free[4 + 2 * r], fb_free[5 + 2 * r]] if c == 0 else (), sig=(c == 3 or c == 7))
            if c == 3:
                c2[t]['trA'] = t_tr
        c2[t]['trB'] = t_tr
        xnb_free[r] = t_tr

    def c2_s11(t):
        r = t % 2
        ptA = bkb(4 + 2 * r)
        ptB = bkb(5 + 2 * r)
        for c in range(8):
            dst = h2T[:, c, t * 128:(t + 1) * 128]
            pt = ptA if c < 4 else ptB
            t_ev = S.op('act', lambda e, c=c, dst=dst, pt=pt: e.activation(
                out=dst, in_=pt[:, (c % 4) * 128:(c % 4 + 1) * 128], func=AF.Identity,
                scale=mcol(3, c, 0), bias=mcol(2, c, 0)),
                deps=[c2[t]['trA'], c2[t]['trB'], t_modc23] if c == 0 else (), sig=(c == 3 or c == 7))
            if c == 3:
                t_evA = t_ev
        t_evB = t_ev
        fb_free[4 + 2 * r] = t_evA
        fb_free[5 + 2 * r] = t_evB
        t_h2[t] = [t_evA, t_evB]

    run_pipeline([c2_s0, c2_s1, c2_s2, c2_s3, c2_s4, c2_s5, c2_s5w, c2_s6, c2_s7, c2_s8, c2_s9, c2_s10, c2_s11], NT)

    if debug_stop == 'C2':
        dbg['xmid'] = (xmid.rearrange('p a b -> p (a b)'), [128, NT * D], F32)
        dbg['h2T'] = (h2T.rearrange('p a b -> p (a b)'), [128, 8 * T], BF16)
        return finish(nc, es, S, dbg, t_h2 + t_xmid)

    S.barrier()
    o = RC
    for i in range(1, NWR):
        w1[i], _, nb = A.alloc(BF16, [8, 512], at=o); o += nb
    for i in range(NWR):
        w2[i], _, nb = A.alloc(BF16, [FG, 1024], at=o); o += nb
    assert o <= RC + 28 * 1024, (o - RC)
    o = RD
    actb = []
    for i in range(2):
        p_, _, nb = A.alloc(BF16, [FG, T], at=o); o += nb
        actb.append(p_)
    w2st = []
    for i in range(2):
        p_, _, nb = A.alloc(F32, [FG, 1024], at=o); o += nb
        w2st.append(p_)
    assert o <= RD + RD_SIZE, (o - RD)
    o = RE + 64
    sgb = []
    for i in range(2):
        p_, _, nb = A.alloc(F32, [512], at=o); o += nb
        sgb.append(p_)
    ln2g, _, nb = A.alloc(F32, [D], at=o); o += nb
    ln2b, _, nb = A.alloc(F32, [D], at=o); o += nb
    y0 = []
    for i in range(3):
        p_, _, nb = A.alloc(F32, [D], at=o); o += nb
        y0.append(p_)
    assert o <= ARENA_BYTES, (o, ARENA_BYTES)
    t_ln2 = [sp_load(ln2g, ln2g_d, 'ln2'), sp_load(ln2b, ln2b_d, 'ln2')][-1]

    w_free = [None] * NWR
    w2st_free = [None, None]
    t_w1 = [None] * NR
    t_w2 = [None] * NR

    def load_round(r):
        slot = r % NWR
        if r == 0:
            t_w1[r] = t_w1_pre
        else:
            t_w1[r] = pool_cast_load(w1[slot].rearrange('p a b -> p (a b)'), wff1_d[r], 'wf1_%d' % slot, deps=[w_free[slot]])
        s2 = r % 2
        t_l = S.dma('sp', lambda e, dst=w2st[s2], r=r: e.dma_start(out=dst.rearrange('p a b -> p (a b)'), in_=wff2_d[r]),
                    'wf2_%d' % s2, deps=[w2st_free[s2]])
        t_w2[r] = S.op('pool', lambda e, slot=slot, s2=s2: e.tensor_tensor(
            out=w2[slot], in0=w2st[s2], in1=g2bc.unsqueeze(1).to_broadcast([128, FG, 1024]), op=ALU.mult),
            deps=[t_l, w_free[slot]] + t_g2)
        w2st_free[s2] = t_w2[r]

    fb_free = [None] * 8
    sg_free = [None, None]
    act_free = [None, None]
    t_act = {}
    t_acc = [None] * NT
    gu_i = 0

    def emit_gu_unit(r, tg, fi):
        nonlocal gu_i
        slot = r % NWR
        ab = actb[r % 2]
        b0 = (gu_i % 2) * 2
        s_ = gu_i % 2
        gu_i += 1
        pg, pu = bk(b0), bk(b0 + 1)
        tok = slice(tg * 512, (tg + 1) * 512)
        for k in range(8):
            t_g_ = S.op('pe', lambda e, k=k, pg=pg, slot=slot, fi=fi, tok=tok: e.matmul(
                pg, lhsT=w1[slot][:, k, fi * 128:(fi + 1) * 128], rhs=h2T[:, k, tok], start=(k == 0), stop=(k == 7)),
                deps=[t_w1[r], fb_free[b0]] if k == 0 else (), sig=(k == 7))
        for k in range(8):
            t_u_ = S.op('pe', lambda e, k=k, pu=pu, slot=slot, fi=fi, tok=tok: e.matmul(
                pu, lhsT=w1[slot][:, k, 256 + fi * 128:256 + (fi + 1) * 128], rhs=h2T[:, k, tok],
                start=(k == 0), stop=(k == 7)), deps=[fb_free[b0 + 1]] if k == 0 else (), sig=(k == 7))
        t_s = S.op('act', lambda e, dst=sgb[s_], pg=pg: e.activation(out=dst, in_=pg, func=AF.Silu),
                   deps=[t_g_, sg_free[s_]])
        fb_free[b0] = t_s
        t_m = S.op('dve', lambda e, ab=ab, fi=fi, tok=tok, pu=pu, sg=sgb[s_]: e.tensor_tensor(
            out=ab[:, fi, tok], in0=pu, in1=sg, op=ALU.mult), deps=[t_u_, t_s, act_free[r % 2]])
        fb_free[b0 + 1] = t_m
        sg_free[s_] = t_m
        t_act[(r, fi, tg)] = t_m

    def emit_gu_tg(r, tg):
        for fi in range(FG):
            emit_gu_unit(r, tg, fi)

    def emit_gu(r):
        for tg in range(4):
            emit_gu_tg(r, tg)

    def emit_dn_tile(rounds, t):
        b0 = 4 + (t % 2) * 2
        last_mm = None
        for half in range(2):
            pd = bk(b0 + half)
            n_mm = len(rounds) * FG
            i_mm = 0
            for r in rounds:
                slot = r % NWR
                ab = actb[r % 2]
                for fi in range(FG):
                    t_mm = S.op('pe', lambda e, pd=pd, fi=fi, t=t, half=half, ab=ab, slot=slot, i_mm=i_mm, n_mm=n_mm: e.matmul(
                        pd, lhsT=ab[:, fi, t * 128:(t + 1) * 128], rhs=w2[slot][:, fi, half * 512:(half + 1) * 512],
                        start=(i_mm == 0), stop=(i_mm == n_mm - 1)),
                        deps=[t_w2[r], fb_free[b0 + half], t_act[(r, fi, t // 4)]], sig=(i_mm == n_mm - 1))
                    i_mm += 1
            accv = xmid[:, t, half * 512:(half + 1) * 512]
            if rounds[0] == 0:
                t_ad = S.op('dve', lambda e, accv=accv, pd=pd: e.scalar_tensor_tensor(
                    out=accv, in0=accv, scalar=ALPHA, in1=pd, op0=ALU.mult, op1=ALU.add), deps=[t_mm, t_acc[t]])
            else:
                t_ad = S.op('dve', lambda e, accv=accv, pd=pd: e.tensor_tensor(
                    out=accv, in0=pd, in1=accv, op=ALU.add), deps=[t_mm, t_acc[t]])
            fb_free[b0 + half] = t_ad
            t_acc[t] = t_ad
            last_mm = t_mm
        return last_mm

    def emit_dn(r):
        last_mm = None
        for t in range(NT):
            last_mm = emit_dn_tile([r], t)
        act_free[r % 2] = last_mm
        w_free[r % NWR] = last_mm

    y_free = [None] * 3
    t_out = []
    tl = [dict() for _ in range(NT)]

    def tail_s1(t):
        sl = t % 4
        st = stt[:, sl]; mv = mvr[:, sl]
        src = xmid[:, t, :]
        S.op('dve', lambda e: e.bn_stats(out=st[:, 0:6], in_=src[:, 0:512]), deps=[t_acc[t], st_free[sl]], sig=False)
        t1 = S.op('dve', lambda e: e.bn_stats(out=st[:, 6:12], in_=src[:, 512:1024]))
        tl[t]['ag'] = S.op('dve', lambda e: e.bn_aggr(out=mv[:, 0:2], in_=st[:, 0:12]), deps=[t1])

    def tail_s2(t):
        mv = mvr[:, t % 4]
        tl[t]['sq'] = S.op('act', lambda e: e.activation(out=mv[:, 4:5], in_=mv[:, 1:2], func=AF.Sqrt, bias=epst[:, 0:1],
                                                         scale=1.0), deps=[tl[t]['ag']])

    def tail_s3(t):
        _, _, _, tl[t]['r'] = rstd_part(t % 4, [tl[t]['sq']])

    def tail_s4(t):
        mv = mvr[:, t % 4]
        r3 = t % 3
        tl[t]['n'] = S.op('act', lambda e: e.activation(out=y0[r3], in_=xmid[:, t, :], func=AF.Identity,
                                                        scale=mv[:, 2:3], bias=mv[:, 3:4]), deps=[tl[t]['r'], y_free[r3]])
        st_free[t % 4] = tl[t]['n']

    def tail_s5(t):
        r3 = t % 3
        tl[t]['g'] = S.op('pool', lambda e: e.tensor_tensor(out=y0[r3], in0=y0[r3], in1=ln2g, op=ALU.mult),
                          deps=[tl[t]['n'], t_ln2])

    def tail_s6(t):
        r3 = t % 3
        t_st_ = S.dma('pool', lambda e, t=t: e.dma_start(out=out_d[t * 128:(t + 1) * 128, :], in_=y0[r3], accum_op=ALU.add),
                      'out%d' % r3, deps=[tl[t]['g'], outpre['t']])
        y_free[r3] = t_st_
        t_out.append(t_st_)

    tail_stages = [tail_s1, tail_s2, tail_s3, tail_s4, tail_s5, tail_s6]
    tail_state = {'n': 0}

    def tail_step(t_new):
        it = tail_state['n']; tail_state['n'] += 1
        K = len(tail_stages)
        for k in reversed(range(K)):
            i = it - k
            if 0 <= i < NT and (t_new is not None or True):
                if i <= (t_new if t_new is not None else NT - 1):
                    tail_stages[k](i)

    for r in range(min(NWR, NR)):
        load_round(r)
    R1, R2 = NR - 2, NR - 1
    emit_gu(0)
    for r in range(NR - 2):
        if r + 1 < NR - 2:
            gu_units = [(tg, fi) for tg in range(4) for fi in range(FG)]
            per = NT // len(gu_units)
            last_mm = None
            for i_, (tg, fi) in enumerate(gu_units):
                emit_gu_unit(r + 1, tg, fi)
                for t in range(i_ * per, (i_ + 1) * per):
                    last_mm = emit_dn_tile([r], t)
            act_free[r % 2] = last_mm
            w_free[r % NWR] = last_mm
        else:
            emit_gu_tg(R1, 0)
            emit_dn(r)
        if r + NWR < NR:
            load_round(r + NWR)
    emit_gu_tg(R2, 0)
    order = [('GU', 1), ('DN', 0), ('GU', 2), ('DN', 1), ('GU', 3), ('DN', 2), ('DN', 3)]
    for kind_, tg in order:
        if kind_ == 'GU':
            emit_gu_tg(R1, tg)
            emit_gu_tg(R2, tg)
        else:
            for t in range(4 * tg, 4 * tg + 4):
                emit_dn_tile([R1, R2], t)
                tail_step(t)
    for _ in range(len(tail_stages)):
        tail_step(None)
    assert len(t_out) == NT
    return finish(nc, es, S, dbg, t_out)


def finish(nc, es, S, dbg, final_ticks):
    dbg_ticks = []
    for name, spec in dbg.items():
        ap, shape = spec[0], spec[1]
        dt_ = spec[2] if len(spec) > 2 else F32
        d = nc.dram_tensor("dbg_" + name, list(shape), dt_, kind="ExternalOutput").ap()
        dbg_ticks.append(S.dma('sp', lambda e, d=d, ap=ap: e.dma_start(out=d, in_=ap), 'dbg', deps=final_ticks))
    S.barrier()
    S.emit(nc, es)
    es.close()
    return nc


def _rope_tables(start):
    pos = np.arange(start - 128, start - 128 + TH)
    rows = (pos // 64).astype(np.float64)
    cols = (pos % 64).astype(np.float64)
    inv = 10000.0 ** (-np.arange(16, dtype=np.float64) / 16)
    C = np.zeros((64, TH), np.float64)
    Sg = np.zeros((64, TH), np.float64)
    for d in range(64):
        p_ = rows if d < 32 else cols
        dd = d % 32
        i = dd % 16
        ang = p_ * inv[i]
        C[d] = np.cos(ang)
        Sg[d] = -np.sin(ang) if dd < 16 else np.sin(ang)
    C = C.astype(np.float32)
    Sg = Sg.astype(np.float32)
    return np.concatenate([C, C], 0), np.concatenate([Sg, Sg], 0)


def _perm_matrix():
    P = np.zeros((128, 128), np.float32)
    for m in range(128):
        d = m % 64
        dd = d % 32
        partner = d + 16 if dd < 16 else d - 16
        P[(m // 64) * 64 + partner, m] = 1.0
    return P


def prep_inputs(inp):
    f = lambda a: np.ascontiguousarray(np.asarray(a, dtype=np.float32))
    x = f(inp['x']); c = f(inp['c']); ctx = f(inp['ctx']); c_ctx = f(inp['c_ctx'])
    w_ada = f(inp['w_ada'])[0]; b_ada = f(inp['b_ada'])[0]; w_in = f(inp['w_in'])[0]
    sink = f(inp['attn_sink'])[0]
    glg = f(inp['gmlp_ln_g'])[0]; glb = f(inp['gmlp_ln_b'])[0]
    w_s = f(inp['w_spatial'])[0]; b_s = f(inp['b_spatial'])[0]
    w_a = f(inp['w_branch_a'])[0]; w_b = f(inp['w_branch_b'])[0]; w_out = f(inp['w_out'])[0]
    ln1g = f(inp['ln1_g'])[0]; ln1b = f(inp['ln1_b'])[0]; ln2g = f(inp['ln2_g'])[0]; ln2b = f(inp['ln2_b'])[0]
    w_ffn_in = f(inp['w_ffn_in'])[0]; w_ffn_out = f(inp['w_ffn_out'])[0]

    def ktile(w):
        n = w.shape[1]
        return np.ascontiguousarray(w.reshape(8, 128, n).transpose(1, 0, 2)).reshape(128, 8 * n)

    wada_t = np.ascontiguousarray(w_ada.reshape(8, 128, 12, 512).transpose(2, 1, 0, 3)).reshape(12, 128, 4096)
    qcols = []
    for cc in range(4):
        qcols += list(range(cc * 64, cc * 64 + 64)) + list(range((4 + cc) * 64, (4 + cc) * 64 + 64))
    qkcols = qcols + list(range(512, 640))
    wqk = ktile(w_in[:, qkcols])
    wu = ktile(w_in[:, 768:1280])
    wvvb = ktile(w_in[:, list(range(640, 768)) + list(range(1280, 1792))])
    wga = w_in[:, 1792:2816]
    wgb = w_in[:, 2816:3840]
    wc1 = np.zeros((8, 128, 24, 128), np.float32)
    for m in range(8):
        cs = slice(m * 128, (m + 1) * 128)
        wc1[m, :, 0:4] = w_a[:, cs].reshape(4, 128, 128).transpose(1, 0, 2)
        wc1[m, :, 4:8] = w_b[:, cs].reshape(4, 128, 128).transpose(1, 0, 2)
        wc1[m, :, 8:16] = wga[:, cs].reshape(8, 128, 128).transpose(1, 0, 2)
        wc1[m, :, 16:24] = wgb[:, cs].reshape(8, 128, 128).transpose(1, 0, 2)
    wc1 = wc1.reshape(8, 128, 24 * 128)
    wout_t = ktile(w_out)
    wst = np.ascontiguousarray(w_s.transpose(2, 0, 1)).reshape(128, 8 * 128)
    bsbc = np.zeros((128, 4, 128), np.float32)
    for cc in range(4):
        for gi in range(2):
            bsbc[gi * 64:(gi + 1) * 64, cc, :] = b_s[2 * cc + gi][None, :]
    bsbc = bsbc.reshape(128, 512)
    badac = np.zeros((128, 4, 8, 2), np.float32)
    for kind, off in enumerate((0, 1024, 3072, 4096)):
        badac[:, kind, :, :] = b_ada[off:off + 1024].reshape(8, 128).T[:, :, None]
    badac = badac.reshape(128, 64)
    badabc = np.ascontiguousarray(np.broadcast_to(
        np.concatenate([b_ada[2048:3072], b_ada[5120:6144]])[None, :], (128, 2048)))
    wff1 = np.zeros((NR, 128, 8, 512), np.float32)
    wff2 = np.zeros((NR, 128, FG, 1024), np.float32)
    for r in range(NR):
        for fi in range(FG):
            fch = r * FG + fi
            wff1[r, :, :, fi * 128:(fi + 1) * 128] = w_ffn_in[:, fch * 128:(fch + 1) * 128].reshape(8, 128, 128).transpose(1, 0, 2)
            wff1[r, :, :, 256 + fi * 128:256 + (fi + 1) * 128] = \
                w_ffn_in[:, FFH + fch * 128:FFH + (fch + 1) * 128].reshape(8, 128, 128).transpose(1, 0, 2)
            wff2[r, :, fi, :] = w_ffn_out[fch * 128:(fch + 1) * 128, :]
    wff1 = wff1.reshape(NR, 128, 4096)
    wff2 = wff2.reshape(NR, 128, FG * 1024)
    ident = np.eye(128, dtype=np.float32)
    perm = _perm_matrix()
    ki = np.arange(128)[:, None]
    qi = np.arange(128)[None, :]
    maskP = (ki >= qi).astype(np.float32)
    maskN = (ki <= qi).astype(np.float32)
    zero = np.zeros((128, 128), np.float32)
    bc = lambda v: np.ascontiguousarray(np.broadcast_to(v[None, :], (128, v.shape[0])))
    shared = dict(wada=wada_t, badac=badac, badabc=badabc, wqk=wqk, wu=wu, wvvb=wvvb, wc1=wc1, wout=wout_t,
                  wst=wst, bsbc=bsbc, wff1=wff1, wff2=wff2, ident=ident, perm=perm, esink=bc(sink),
                  glg=bc(glg), glb=bc(glb), ln1g=bc(ln1g), ln1b=bc(ln1b), ln2g=bc(ln2g), ln2b=bc(ln2b))
    in_maps = []
    for core in range(NCORES):
        b = core // 4
        seg = core % 4
        start = seg * T
        xhalo = np.zeros((TH, D), np.float32)
        lo = max(start - 128, 0)
        hi = min(start + T + 128, SEQ)
        xhalo[lo - (start - 128): hi - (start - 128)] = x[b, lo:hi]
        cv = np.zeros((128, 16), np.float32)
        cv[:, 0:8] = c[b].reshape(8, 128).T
        cv[:, 8:16] = c_ctx.reshape(8, 128).T
        rc, rs = _rope_tables(start)
        mk = np.stack([zero if seg == 0 else maskP, maskP, maskN, zero if seg == 3 else maskN], 1).reshape(128, 512)
        m = dict(shared)
        m.update(xh=xhalo, ctxb=np.ascontiguousarray(ctx[b]), cvec=cv, ropec=rc, ropes=rs, masks=np.ascontiguousarray(mk))
        in_maps.append(m)
    return in_maps


_NC_CACHE = {}


def kernel(**inputs):
    in_maps = prep_inputs(inputs)
    if 'nc' not in _NC_CACHE:
        _NC_CACHE['nc'] = build_nc(None)
    nc = _NC_CACHE['nc']
    res = run_bass_kernel_spmd(nc, in_maps, core_ids=list(range(NCORES)))
    out = np.zeros((2, SEQ, D), np.float32)
    for core in range(NCORES):
        b = core // 4
        seg = core % 4
        out[b, seg * T:(seg + 1) * T] = res.results[core]["out"]
    return out
```

```python
import numpy as np
from contextlib import ExitStack
import concourse.bass as bass
import concourse.mybir as mybir
from concourse.bass_utils import run_bass_kernel_spmd

F32 = mybir.dt.float32
BF16 = mybir.dt.bfloat16
AF = mybir.ActivationFunctionType
ALU = mybir.AluOpType

NCORES = 8
D = 1024
T = 2048
NT = 16
TH = 2304
NTH = 18
CTX = 256
FFH = 2816
NF = 22
FG = 2
NR = NF // FG
ALPHA = 2.0 ** 0.25
EPS = 1e-5
SEQ = 8192
IN_SPLITS = (512, 640, 768, 1280, 1792, 2816, 3840)

DEBUG_STOP = None


class Sched:
    ENG = ('pe', 'act', 'dve', 'pool', 'sp')

    def __init__(self):
        self.q = {e: [] for e in self.ENG}
        self.cnt = {e: 0 for e in self.ENG}
        self.waited = {e: {} for e in self.ENG}
        self.dmacnt = {}

    def _resolve(self, eng, deps):
        ws = []
        stack = [deps]
        flat = []
        while stack:
            d = stack.pop()
            if d is None:
                continue
            if isinstance(d, tuple) and len(d) == 2 and isinstance(d[0], str):
                flat.append(d)
            else:
                stack.extend(list(d))
        for (p, t) in flat:
            if self.waited[eng].get(p, 0) >= t:
                continue
            self.waited[eng][p] = t
            ws.append((p, t))
        return ws

    def op(self, eng, fn, deps=(), sig=True):
        ws = self._resolve(eng, deps)
        tick = None
        if sig:
            self.cnt[eng] += 1
            tick = (eng, self.cnt[eng])
        self.q[eng].append((ws, fn, eng if sig else None, 1))
        return tick

    def dma(self, eng, fn, key, deps=()):
        ws = self._resolve(eng, deps)
        self.dmacnt[key] = self.dmacnt.get(key, 0) + 16
        self.q[eng].append((ws, fn, 'dma:' + key, 16))
        return ('dma:' + key, self.dmacnt[key])

    def barrier(self):
        ticks = [(e, c) for e, c in self.cnt.items() if c > 0]
        ticks += [('dma:' + k, c) for k, c in self.dmacnt.items()]
        for e in self.ENG:
            ws = self._resolve(e, ticks)
            if ws:
                self.q[e].append((ws, None, None, 0))

    def emit(self, nc, es):
        sems = {}
        for e in self.ENG:
            sems[e] = es.enter_context(nc.semaphore("s_" + e))
        for k in self.dmacnt:
            sems['dma:' + k] = es.enter_context(nc.semaphore("d_" + k))
        block = es.enter_context(nc.Block())
        reg = {'pe': block.tensor, 'act': block.scalar, 'dve': block.vector,
               'pool': block.gpsimd, 'sp': block.sync}
        for e in self.ENG:
            items = self.q[e]

            def body(engine, items=items):
                for (ws, fn, sigkey, inc) in items:
                    for (p, t) in ws:
                        engine.wait_ge(sems[p], t)
                    if fn is None:
                        continue
                    ins = fn(engine)
                    if sigkey is not None:
                        ins.then_inc(sems[sigkey], inc)
            reg[e](body)


class Arena:
    def __init__(self, nc, es, nbytes):
        self.t = es.enter_context(nc.sbuf_tensor("arena", [128, nbytes // 2], BF16))
        self.nbytes = nbytes
        self.top = 0
        self.marks = {}

    def alloc(self, dtype, shape, at=None):
        esz = 4 if dtype == F32 else 2
        n = int(np.prod(shape))
        nb = (n * esz + 63) // 64 * 64
        if at is None:
            off = self.top
            self.top += nb
        else:
            off = at
        assert off + nb <= self.nbytes, ("arena overflow", off, nb, self.nbytes)
        a = self.t[:, off // 2: off // 2 + n * esz // 2]
        if dtype == F32:
            a = a.bitcast(F32)
        if len(shape) == 2:
            a = a.rearrange('p (a b) -> p a b', a=shape[0])
        elif len(shape) == 3:
            a = a.rearrange('p (a b c) -> p a b c', a=shape[0], b=shape[1])
        return a, off, nb


def run_pipeline(stages, n_items, hook=None):
    K = len(stages)
    for it in range(n_items + K - 1):
        for k in reversed(range(K)):
            i = it - k
            if 0 <= i < n_items:
                stages[k](i)
        if hook is not None:
            hook(it)


def build_nc(debug_stop=None):
    nc = bass.Bass("TRN2", target_bir_lowering=False)
    es = ExitStack()
    S = Sched()

    def din(name, shape):
        return nc.dram_tensor(name, list(shape), F32, kind="ExternalInput").ap()

    xh = din("xh", [TH, D])
    ctxb = din("ctxb", [CTX, D])
    cvec = din("cvec", [128, 16])
    wada = din("wada", [12, 128, 8 * 512])
    badac_d = din("badac", [128, 64])
    badabc_d = din("badabc", [128, 2048])
    wqk_d = din("wqk", [128, 8 * 640])
    wu_d = din("wu", [128, 8 * 512])
    wvvb_d = din("wvvb", [128, 8 * 640])
    wc1_d = din("wc1", [8, 128, 24 * 128])
    wout_d = din("wout", [128, 8 * 1024])
    wst_d = din("wst", [128, 8 * 128])
    bsbc_d = din("bsbc", [128, 512])
    wff1_d = din("wff1", [NR, 128, 8 * 512])
    wff2_d = din("wff2", [NR, 128, FG * 1024])
    ropec_d = din("ropec", [128, TH])
    ropes_d = din("ropes", [128, TH])
    masks_d = din("masks", [128, 4 * 128])
    ident_d = din("ident", [128, 128])
    perm_d = din("perm", [128, 128])
    esink_d = din("esink", [128, 8])
    glg_d = din("glg", [128, 512])
    glb_d = din("glb", [128, 512])
    ln1g_d = din("ln1g", [128, D])
    ln1b_d = din("ln1b", [128, D])
    ln2g_d = din("ln2g", [128, D])
    ln2b_d = din("ln2b", [128, D])
    out_d = nc.dram_tensor("out", [T, D], F32, kind="ExternalOutput").ap()
    dbg = {}

    ARENA_BYTES = 207 * 1024
    A = Arena(nc, es, ARENA_BYTES)
    banks = [es.enter_context(nc.psum_tensor("bank%d" % i, [128, 512], F32)) for i in range(8)]

    def bk(i):
        return banks[i][:, :]

    def bkb(i):
        return banks[i][:, :].bitcast(BF16)

    ident, _, _ = A.alloc(BF16, [128])
    perm, _, _ = A.alloc(BF16, [128])
    masks, _, _ = A.alloc(BF16, [4, 128])
    esink, _, _ = A.alloc(F32, [8])
    modc, _, _ = A.alloc(F32, [64])
    sc, _, _ = A.alloc(BF16, [16])
    small, _, _ = A.alloc(F32, [64])
    onesf, _, _ = A.alloc(F32, [128])
    neghalf, _, _ = A.alloc(F32, [32])
    bsbc, _, _ = A.alloc(F32, [512])
    g2bc, _, _ = A.alloc(F32, [1024])
    stt, _, _ = A.alloc(F32, [8, 16])
    mvr, _, _ = A.alloc(F32, [8, 8])
    st_free = [None] * 8
    epst, _, _ = A.alloc(F32, [16])
    CONST_END = A.top

    RA = A.top
    hT, _, nbA = A.alloc(BF16, [8, TH])
    RB = A.top
    RB_SIZE = 64 * 1024
    A.top = RB + RB_SIZE
    RC = A.top
    RC_SIZE = 36 * 1024
    A.top = RC + RC_SIZE
    RD = A.top
    RD_SIZE = 32 * 1024
    A.top = RD + RD_SIZE
    RE = A.top
    RE_SIZE = ARENA_BYTES - RE
    assert RE_SIZE >= 24 * 1024, RE_SIZE

    o = RB
    qT, _, nb = A.alloc(BF16, [4, T], at=o); o += nb
    kT, _, nb = A.alloc(BF16, [TH], at=o); o += nb
    kcT, _, nb = A.alloc(BF16, [CTX], at=o); o += nb
    vaug, _, nb = A.alloc(BF16, [NTH, 2, 66], at=o); o += nb
    vcaug, _, nb = A.alloc(BF16, [2, 2, 66], at=o); o += nb
    guT, _, nb = A.alloc(BF16, [4, T], at=o); o += nb
    vn, _, nb = A.alloc(BF16, [NT, 512], at=o); o += nb
    assert o <= RB + RB_SIZE, (o - RB)
    xmid, _, _ = A.alloc(F32, [NT, D], at=RB)
    h2T, _, _ = A.alloc(BF16, [8, T], at=RA)
    hcT_off = None
    yaT, _, nb1 = A.alloc(BF16, [4, T], at=RC)
    ybT, _, nb2 = A.alloc(BF16, [4, T], at=RC + nb1)
    merged, _, _ = A.alloc(BF16, [8, T], at=RD)

    cdeps = []

    def sp_load(dst, src, key='const'):
        return S.dma('sp', lambda e, dst=dst, src=src: e.dma_start(out=dst, in_=src), key)

    def pool_cast_load(dst, src, key, deps=()):
        return S.dma('pool', lambda e, dst=dst, src=src: e.dma_start(out=dst, in_=src, max_dma_last_dim=4096), key, deps)

    cvec_f, _, _ = A.alloc(F32, [16], at=RE)
    badac, _, _ = A.alloc(F32, [64], at=RE + 64)
    t_cvec = sp_load(cvec_f, cvec, 'cvec')
    t_badac = sp_load(badac, badac_d, 'badac')
    t_c = [sp_load(esink, esink_d), sp_load(bsbc, bsbc_d), ]
    t_cb = [pool_cast_load(ident, ident_d, 'cb'), pool_cast_load(perm, perm_d, 'cb'),
            pool_cast_load(masks.rearrange('p a b -> p (a b)'), masks_d, 'cb')]
    t_m1 = S.op('pool', lambda e: e.memset(onesf, 1.0))
    t_m2 = S.op('pool', lambda e: e.memset(neghalf, -0.5))
    t_m3 = S.op('pool', lambda e: e.memset(epst, EPS))
    T_CONST = [t_c[-1], t_cb[-1], t_m1, t_m2]

    o = RC
    wqk, _, nb = A.alloc(BF16, [8, 640], at=o); o += nb
    wu, _, nb = A.alloc(BF16, [8, 512], at=o); o += nb
    wvvb, _, nb = A.alloc(BF16, [8, 640], at=o); o += nb
    hcT, _, nb = A.alloc(BF16, [8, CTX], at=o); o += nb
    scb, _, nb = A.alloc(BF16, [8, 128], at=o); o += nb
    bslice, _, nb = A.alloc(F32, [512], at=o); o += nb
    assert o <= RC + RC_SIZE, (o - RC)

    o = RB
    wring0 = []
    for i in range(4):
        w_, _, nb = A.alloc(BF16, [8, 512], at=o); o += nb
        wring0.append(w_)
    assert o <= RB + RB_SIZE, o
    g1bc, _, nb = A.alloc(F32, [1024], at=RE + 384)
    RE_FREE = RE + 384 + nb
    wring1 = []
    o = RE_FREE + 8192
    for i in range(2):
        w_, _, nb = A.alloc(BF16, [8, 512], at=o); o += nb
        wring1.append(w_)
    assert o <= ARENA_BYTES, (o, ARENA_BYTES)

    t_sc = S.op('act', lambda e: e.activation(out=sc, in_=cvec_f, func=AF.Silu), deps=[t_cvec])
    t_scb = S.op('dve', lambda e: e.tensor_copy(out=scb, in_=sc[:, 0:8].unsqueeze(2).to_broadcast([128, 8, 128])),
                 deps=[t_sc])
    colkind = {0: 0, 1: 0, 2: 1, 3: 1, 6: 2, 7: 2, 8: 3, 9: 3}
    bcform = {4: (0, 0), 5: (0, 1), 10: (1, 0), 11: (1, 1)}
    t_g = {}
    modc4 = modc.rearrange('p (a b c) -> p a b c', a=4, b=8)

    def p0_col(blk, wb_, t_w, pm):
        kind = colkind[blk]
        t_l = None
        for s in range(4):
            chunk = (blk % 2) * 4 + s
            col = (kind * 8 + chunk) * 2
            for k in range(8):
                t_l = S.op('pe', lambda e, k=k, s=s, col=col, wb_=wb_, pm=pm: e.matmul(
                    pm[:, col:col + 2], lhsT=wb_[:, k, s * 128:(s + 1) * 128], rhs=sc[:, k:16:8],
                    start=(k == 0), stop=(k == 7)), deps=[t_w, t_sc] if k == 0 else (),
                    sig=(k == 7 and s == 3))
        return t_l

    def modc_evac(lo, hi, plus1, pm, dep):
        t0_ = S.op('dve', lambda e: e.tensor_tensor(out=modc[:, lo:hi], in0=pm[:, lo:hi], in1=badac[:, lo:hi], op=ALU.add),
                   deps=[dep, t_badac])
        if plus1:
            t0_ = S.op('dve', lambda e: e.tensor_scalar(out=modc[:, lo:hi], in0=modc[:, lo:hi], scalar1=1.0,
                                                        scalar2=None, op0=ALU.add), deps=[t0_])
        return t0_

    t_l = None
    t_modc01_A, t_modc01_B = [], []
    for blk in (0, 2, 1, 3):
        t_w = pool_cast_load(wring0[blk].rearrange('p a b -> p (a b)'), wada[blk], 'wada_e%d' % blk)
        pm_ = bk(0) if blk in (0, 2) else bk(1)
        t_l = p0_col(blk, wring0[blk], t_w, pm_)
        if blk == 2:
            t_modc01_A = [modc_evac(0, 8, False, bk(0), t_l), modc_evac(16, 24, True, bk(0), t_l)]
        if blk == 3:
            t_modc01_B = [modc_evac(8, 16, False, bk(1), t_l), modc_evac(24, 32, True, bk(1), t_l)]
    t_modc01 = t_modc01_A + t_modc01_B
    t_wqk = pool_cast_load(wqk.rearrange('p a b -> p (a b)'), wqk_d, 'wqk')
    t_wvvb = pool_cast_load(wvvb.rearrange('p a b -> p (a b)'), wvvb_d, 'wvvb')
    t_wu = pool_cast_load(wu.rearrange('p a b -> p (a b)'), wu_d, 'wu')

    def mcol(kind, chunk, which):
        return modc4[:, kind, chunk, which:which + 1]

    def stats_part(src, slot, deps, n=1024):
        st = stt[:, slot]
        mv = mvr[:, slot]
        if n == 1024:
            S.op('dve', lambda e: e.bn_stats(out=st[:, 0:6], in_=src[:, 0:512]), deps=[deps, st_free[slot]], sig=False)
            t1 = S.op('dve', lambda e: e.bn_stats(out=st[:, 6:12], in_=src[:, 512:1024]))
            t2 = S.op('dve', lambda e: e.bn_aggr(out=mv[:, 0:2], in_=st[:, 0:12]), deps=[t1])
        else:
            t1 = S.op('dve', lambda e: e.bn_stats(out=st[:, 0:6], in_=src), deps=[deps, st_free[slot]])
            t2 = S.op('dve', lambda e: e.bn_aggr(out=mv[:, 0:2], in_=st[:, 0:6]), deps=[t1])
        return S.op('act', lambda e: e.activation(out=mv[:, 4:5], in_=mv[:, 1:2], func=AF.Sqrt, bias=epst[:, 0:1], scale=1.0),
                    deps=[t2, t_m3])

    def rstd_part(slot, deps):
        mv = mvr[:, slot]
        t4 = S.op('dve', lambda e: e.reciprocal(out=mv[:, 2:3], in_=mv[:, 4:5]), deps=[deps])
        t5 = S.op('dve', lambda e: e.scalar_tensor_tensor(out=mv[:, 3:4], in0=mv[:, 0:1], scalar=-1.0, in1=mv[:, 2:3],
                                                          op0=ALU.mult, op1=ALU.mult), deps=[t4])
        return mv[:, 0:1], mv[:, 2:3], mv[:, 3:4], t5

    o = RD
    NXR = 6
    xr = []
    for i in range(NXR):
        x_, _, nb = A.alloc(F32, [D], at=o); o += nb
        xr.append(x_)
    xnb = []
    for i in range(2):
        x_, _, nb = A.alloc(BF16, [D], at=o); o += nb
        xnb.append(x_)
    assert o <= RD + RD_SIZE, (o - RD)

    xr_free = [None] * NXR
    xnb_free = [None, None]
    bankA_free = [None, None]
    bankB_free = [None, None]
    t_hT = [None] * NTH
    t_hcT = [None] * 2
    tiles = [('x', i) for i in range(NTH)] + [('c', i) for i in range(2)]
    NA1 = len(tiles)
    a1 = {}
    for it in range(NA1 + 2):
        if it < NA1:
            n = it
            kindt, ti = tiles[n]
            xs = n % NXR
            src = xh[ti * 128:(ti + 1) * 128, :] if kindt == 'x' else ctxb[ti * 128:(ti + 1) * 128, :]
            t_ld = S.dma('sp', lambda e, dst=xr[xs], src=src: e.dma_start(out=dst, in_=src), 'xr%d' % xs,
                         deps=[xr_free[xs]])
            a1[n] = {'sq': stats_part(xr[xs], n % 4, [t_ld])}
        if 0 <= it - 1 < NA1:
            n = it - 1
            kindt, ti = tiles[n]
            xs = n % NXR
            bs_ = n % 2
            mean, rstd, nmr, t_r = rstd_part(n % 4, [a1[n]['sq']])
            t_xn = S.op('act', lambda e, dst=xnb[bs_], src=xr[xs], rstd=rstd, nmr=nmr: e.activation(
                out=dst, in_=src, func=AF.Identity, scale=rstd, bias=nmr), deps=[t_r, xnb_free[bs_]])
            xr_free[xs] = t_xn
            st_free[n % 4] = t_xn
            ptA = bkb(4 + 2 * bs_)
            ptB = bkb(5 + 2 * bs_)
            for c in range(8):
                pt = ptA if c < 4 else ptB
                t_tr = S.op('pe', lambda e, c=c, pt=pt, src=xnb[bs_]: e.transpose(
                    pt[:, (c % 4) * 128:(c % 4 + 1) * 128], src[:, c * 128:(c + 1) * 128], ident),
                    deps=[t_xn, bankA_free[bs_], bankB_free[bs_], T_CONST] if c == 0 else (), sig=(c == 3 or c == 7))
                if c == 3:
                    a1[n]['trA'] = t_tr
            a1[n]['trB'] = t_tr
            xnb_free[bs_] = t_tr
        if 0 <= it - 2 < NA1:
            n = it - 2
            kindt, ti = tiles[n]
            bs_ = n % 2
            ptA = bkb(4 + 2 * bs_)
            ptB = bkb(5 + 2 * bs_)
            which = 0 if kindt == 'x' else 1
            for c in range(8):
                dst = hT[:, c, ti * 128:(ti + 1) * 128] if kindt == 'x' else hcT[:, c, ti * 128:(ti + 1) * 128]
                if c < 4:
                    t_evA = S.op('act', lambda e, c=c, dst=dst, ptA=ptA, which=which: e.activation(
                        out=dst, in_=ptA[:, c * 128:(c + 1) * 128], func=AF.Identity,
                        scale=mcol(1, c, which), bias=mcol(0, c, which)),
                        deps=[a1[n]['trA'], t_modc01_A] if c == 0 else (), sig=(c == 3))
                else:
                    t_evB = S.op('dve', lambda e, c=c, dst=dst, ptB=ptB, which=which: e.tensor_scalar(
                        out=dst, in0=ptB[:, (c - 4) * 128:(c - 3) * 128], scalar1=mcol(1, c, which),
                        scalar2=mcol(0, c, which), op0=ALU.mult, op1=ALU.add),
                        deps=[a1[n]['trB'], t_modc01_B] if c == 4 else (), sig=(c == 7))
            bankA_free[bs_] = t_evA
            bankB_free[bs_] = t_evB
            if kindt == 'x':
                t_hT[ti] = [t_evA, t_evB]
            else:
                t_hcT[ti] = [t_evA, t_evB]
    if debug_stop == 'A1':
        dbg['hT'] = (hT.rearrange('p a b -> p (a b)'), [128, 8 * TH], BF16)
        dbg['hcT'] = (hcT.rearrange('p a b -> p (a b)'), [128, 8 * CTX], BF16)
        return finish(nc, es, S, dbg, t_hT + t_hcT)

    S.barrier()
    o = RD
    ropec, _, nb = A.alloc(F32, [TH], at=o); o += nb
    ropes, _, nb = A.alloc(F32, [TH], at=o); o += nb
    glg, _, nb = A.alloc(F32, [512], at=o); o += nb
    glb, _, nb = A.alloc(F32, [512], at=o); o += nb
    qraw = []
    for i in range(2):
        q_, _, nb = A.alloc(BF16, [512], at=o); o += nb
        qraw.append(q_)
    rt1 = []
    rt2 = []
    for i in range(2):
        q_, _, nb = A.alloc(F32, [512], at=o); o += nb
        rt1.append(q_)
        q_, _, nb = A.alloc(F32, [512], at=o); o += nb
        rt2.append(q_)
    assert o <= RD + RD_SIZE, (o - RD)
    o = RE_FREE
    gvb = []
    for i in range(4):
        q_, _, nb = A.alloc(F32, [512], at=o); o += nb
        gvb.append(q_)
    assert o <= RE_FREE + 8192, (o, RE_FREE)
    t_rope = [sp_load(ropec, ropec_d, 'rope'), sp_load(ropes, ropes_d, 'rope')][-1]
    t_gl = [sp_load(glg, glg_d, 'gl'), sp_load(glb, glb_d, 'gl')][-1]
    t_vones = S.op('pool', lambda e: e.memset(vaug[:, :, :, 64:66], 1.0))
    t_vcones = S.op('pool', lambda e: e.memset(vcaug[:, :, :, 64:66], 1.0))

    fb_free = [None] * 8
    qraw_free = [None, None]
    rt_free = [None, None]

    dq = {'blocks': [4, 5, 6, 7, 8, 9, 10, 11], 'issued': [], 'n_issued': 0, 'n_done': 0, 'bs_free': None, 'pb': 0}
    ring1_free = [None, None]
    t_modc23 = []

    def p0_issue():
        if dq['n_issued'] >= len(dq['blocks']):
            return
        blk = dq['blocks'][dq['n_issued']]
        slot = dq['n_issued'] % 2
        dq['n_issued'] += 1
        t_w = pool_cast_load(wring1[slot].rearrange('p a b -> p (a b)'), wada[blk], 'wada_d%d' % slot,
                             deps=[ring1_free[slot]])
        dq['issued'].append((blk, slot, t_w))

    def p0_compute(col_bank=6, bc_bank=7):
        if dq['n_done'] >= len(dq['blocks']):
            return
        blk, slot, t_w = dq['issued'][dq['n_done']]
        dq['n_done'] += 1
        wb_ = wring1[slot]
        if blk in colkind:
            pm = bk(col_bank)
            kind = colkind[blk]
            t_l = None
            for s in range(4):
                chunk = (blk % 2) * 4 + s
                col = (kind * 8 + chunk) * 2
                for k in range(8):
                    t_l = S.op('pe', lambda e, k=k, s=s, col=col, wb_=wb_, pm=pm: e.matmul(
                        pm[:, col:col + 2], lhsT=wb_[:, k, s * 128:(s + 1) * 128], rhs=sc[:, k:16:8],
                        start=(k == 0), stop=(k == 7)), deps=[t_w, t_sc, fb_free[col_bank]] if k == 0 else (),
                        sig=(k == 7 and s == 3))
            ring1_free[slot] = t_l
            lo = (kind * 8 + (blk % 2) * 4) * 2
            t_e = modc_evac(lo, lo + 8, kind in (1, 3), pm, t_l)
            fb_free[col_bank] = t_e
            t_modc23.append(t_e)
        else:
            which, half = bcform[blk]
            bank = bc_bank
            pb = bk(bank)
            gi_ = which * 2 + half
            t_bs = S.dma('sp', lambda e, gi_=gi_: e.dma_start(out=bslice, in_=badabc_d[:, gi_ * 512:(gi_ + 1) * 512]),
                         'bslice', deps=[dq['bs_free']])
            for k in range(8):
                t_mm = S.op('pe', lambda e, k=k, pb=pb, wb_=wb_: e.matmul(
                    pb, lhsT=scb[:, k, :], rhs=wb_[:, k, :], start=(k == 0), stop=(k == 7)),
                    deps=[t_w, t_scb, fb_free[bank]] if k == 0 else (), sig=(k == 7))
            ring1_free[slot] = t_mm
            dst = (g1bc if which == 0 else g2bc)[:, half * 512:(half + 1) * 512]
            t_e = S.op('dve', lambda e, dst=dst, pb=pb: e.tensor_tensor(out=dst, in0=pb, in1=bslice, op=ALU.add),
                       deps=[t_mm, t_bs])
            dq['bs_free'] = t_e
            fb_free[bank] = t_e
            t_g[(which, half)] = t_e
        p0_issue()

    p0_issue()
    p0_issue()
    units = []
    for m in range(4):
        for tg in range(4):
            units.append(('q', m, 128 + tg * 512, 512, tg * 512))
    for tg in range(5):
        n_ = 512 if tg < 4 else 256
        units.append(('k', 4, tg * 512, n_, tg * 512))
    t_q = {}
    t_qk_all = []
    qk = [dict() for _ in units]

    def qk_part1(ui):
        kd, m, hcol, n_, ocol = units[ui]
        b0 = (ui % 3) * 2
        pq = bk(b0)[:, 0:n_]
        for k in range(8):
            t_mm = S.op('pe', lambda e, k=k, pq=pq, m=m, hcol=hcol, n_=n_: e.matmul(
                pq, lhsT=wqk[:, k, m * 128:(m + 1) * 128], rhs=hT[:, k, hcol:hcol + n_],
                start=(k == 0), stop=(k == 7)),
                deps=[t_wqk, fb_free[b0], fb_free[b0 + 1]] if k == 0 else (), sig=(k == 7))
        r = ui % 2
        t_raw = S.op('act', lambda e, dst=qraw[r][:, 0:n_], pq=pq: e.activation(out=dst, in_=pq, func=AF.Copy),
                     deps=[t_mm, qraw_free[r]])
        qk[ui]['mm'] = t_mm
        qk[ui]['raw'] = t_raw

    def qk_part2(ui):
        kd, m, hcol, n_, ocol = units[ui]
        b0 = (ui % 3) * 2
        pq, ps_ = bk(b0)[:, 0:n_], bk(b0 + 1)[:, 0:n_]
        r = ui % 2
        t_mm, t_raw = qk[ui]['mm'], qk[ui]['raw']
        t_pm = S.op('pe', lambda e, ps_=ps_, src=qraw[r][:, 0:n_]: e.matmul(ps_, lhsT=perm, rhs=src, start=True, stop=True),
                    deps=[t_raw, T_CONST])
        qraw_free[r] = t_pm
        t_1 = S.op('dve', lambda e, dst=rt1[r][:, 0:n_], pq=pq, hcol=hcol, n_=n_: e.tensor_tensor(
            out=dst, in0=pq, in1=ropec[:, hcol:hcol + n_], op=ALU.mult), deps=[t_mm, t_raw, t_rope, rt_free[r]])
        t_2 = S.op('dve', lambda e, dst=rt2[r][:, 0:n_], ps_=ps_, hcol=hcol, n_=n_: e.tensor_tensor(
            out=dst, in0=ps_, in1=ropes[:, hcol:hcol + n_], op=ALU.mult), deps=[t_pm])
        fb_free[b0] = t_2
        fb_free[b0 + 1] = t_2
        dst = qT[:, m, ocol:ocol + n_] if kd == 'q' else kT[:, ocol:ocol + n_]
        t_3 = S.op('pool', lambda e, dst=dst, a=rt1[r][:, 0:n_], b=rt2[r][:, 0:n_]: e.tensor_tensor(
            out=dst, in0=a, in1=b, op=ALU.add), deps=[t_1, t_2])
        rt_free[r] = t_3
        t_qk_all.append(t_3)

    for ui in range(len(units)):
        qk_part1(ui)
        if ui >= 1:
            qk_part2(ui - 1)
        if ui in (6, 13, 20):
            p0_compute()
    qk_part2(len(units) - 1)
    if debug_stop == 'A2a':
        dbg['qT'] = (qT.rearrange('p a b -> p (a b)'), [128, 4 * T], BF16)
        dbg['kT'] = (kT, [128, TH], BF16)
        return finish(nc, es, S, dbg, t_qk_all)
    pq = bk(0)[:, 0:CTX]
    for k in range(8):
        t_mm = S.op('pe', lambda e, k=k, pq=pq: e.matmul(pq, lhsT=wqk[:, k, 512:640], rhs=hcT[:, k, :],
                                                         start=(k == 0), stop=(k == 7)),
                    deps=[fb_free[0], t_hcT] if k == 0 else (), sig=(k == 7))
    t_kc = S.op('act', lambda e, pq=pq: e.activation(out=kcT, in_=pq, func=AF.Copy), deps=[t_mm])
    fb_free[0] = t_kc
    t_gu = []
    ui = 0
    for m in range(4):
        for tg in range(4):
            b0 = 1 + (ui % 3); ui += 1
            pq = bk(b0)
            for k in range(8):
                t_mm = S.op('pe', lambda e, k=k, pq=pq, m=m, tg=tg: e.matmul(
                    pq, lhsT=wu[:, k, m * 128:(m + 1) * 128], rhs=hT[:, k, 128 + tg * 512:128 + (tg + 1) * 512],
                    start=(k == 0), stop=(k == 7)), deps=[t_wu, fb_free[b0]] if k == 0 else (), sig=(k == 7))
            t_e = S.op('act', lambda e, pq=pq, m=m, tg=tg: e.activation(
                out=guT[:, m, tg * 512:(tg + 1) * 512], in_=pq, func=AF.Gelu_apprx_tanh), deps=[t_mm])
            fb_free[b0] = t_e
            t_gu.append(t_e)
            if ui in (5, 10, 15):
                p0_compute()
    if debug_stop == 'A2b':
        dbg['kcT'] = (kcT, [128, CTX], BF16)
        dbg['guT'] = (guT.rearrange('p a b -> p (a b)'), [128, 4 * T], BF16)
        return finish(nc, es, S, dbg, [t_kc] + t_gu)
    gvb_free = [None] * 4
    t_v = [None] * NTH
    t_vc = [None] * 2
    t_vn = [None] * NT
    vb_ = [dict() for _ in tiles]

    def is_main(n):
        kindt, ti = tiles[n]
        return kindt == 'x' and 1 <= ti <= NT

    def v_s0(n):
        kindt, ti = tiles[n]
        bv = 4 + (n % 2)
        pv = bk(bv)[:, 0:128]
        for k in range(8):
            lh = hT[:, k, ti * 128:(ti + 1) * 128] if kindt == 'x' else hcT[:, k, ti * 128:(ti + 1) * 128]
            t_mm = S.op('pe', lambda e, k=k, pv=pv, lh=lh: e.matmul(pv, lhsT=lh, rhs=wvvb[:, k, 0:128],
                                                                   start=(k == 0), stop=(k == 7)),
                        deps=[t_wvvb, fb_free[bv]] if k == 0 else (), sig=(k == 7))
        vb_[n]['v'] = t_mm
        if is_main(n):
            mt = ti - 1
            bvb = 6 + (mt % 2)
            pvb = bk(bvb)
            for k in range(8):
                t_mm = S.op('pe', lambda e, k=k, pvb=pvb, ti=ti: e.matmul(
                    pvb, lhsT=hT[:, k, ti * 128:(ti + 1) * 128], rhs=wvvb[:, k, 128:640],
                    start=(k == 0), stop=(k == 7)), deps=[fb_free[bvb]] if k == 0 else (), sig=(k == 7))
            vb_[n]['vb'] = t_mm

    def v_s1(n):
        kindt, ti = tiles[n]
        bv = 4 + (n % 2)
        pv = bk(bv)[:, 0:128]
        dstv = (vaug[:, ti, :, 0:64] if kindt == 'x' else vcaug[:, ti, :, 0:64])
        t_e = S.op('act', lambda e, dstv=dstv, pv=pv: e.activation(
            out=dstv, in_=pv.rearrange('p (a b) -> p a b', a=2), func=AF.Copy), deps=[vb_[n]['v'], t_vones, t_vcones])
        fb_free[bv] = t_e
        if kindt == 'x':
            t_v[ti] = t_e
        else:
            t_vc[ti] = t_e
        if is_main(n):
            mt = ti - 1
            bvb = 6 + (mt % 2)
            g_ = gvb[mt % 4]
            t_ge = S.op('act', lambda e, g_=g_, pvb=bk(bvb): e.activation(out=g_, in_=pvb, func=AF.Gelu_apprx_tanh),
                        deps=[vb_[n]['vb'], gvb_free[mt % 4]])
            fb_free[bvb] = t_ge
            vb_[n]['ge'] = t_ge

    def v_s2(n):
        if not is_main(n):
            return
        mt = tiles[n][1] - 1
        sl = mt % 4
        st = stt[:, sl]; mv = mvr[:, sl]
        g_ = gvb[mt % 4]
        t1 = S.op('dve', lambda e: e.bn_stats(out=st[:, 0:6], in_=g_), deps=[vb_[n]['ge'], st_free[sl]])
        vb_[n]['ag'] = S.op('dve', lambda e: e.bn_aggr(out=mv[:, 0:2], in_=st[:, 0:6]), deps=[t1])

    def v_s3(n):
        if not is_main(n):
            return
        mt = tiles[n][1] - 1
        mv = mvr[:, mt % 4]
        t3 = S.op('pool', lambda e: e.tensor_scalar(out=mv[:, 4:5], in0=mv[:, 1:2], scalar1=EPS, scalar2=None,
                                                    op0=ALU.add), deps=[vb_[n]['ag']])
        vb_[n]['rs'] = S.op('pool', lambda e: e.tensor_tensor(out=mv[:, 2:3], in0=mv[:, 4:5], in1=neghalf[:, 0:1],
                                                              op=ALU.pow), deps=[t3, t_m2])

    def v_s4(n):
        if not is_main(n):
            return
        mt = tiles[n][1] - 1
        mv = mvr[:, mt % 4]
        g_ = gvb[mt % 4]
        tmp_ = rt1[mt % 2]
        t5 = S.op('dve', lambda e: e.scalar_tensor_tensor(
            out=tmp_, in0=g_, scalar=mv[:, 0:1], in1=glg, op0=ALU.subtract, op1=ALU.mult),
            deps=[vb_[n]['rs'], t_gl, rt_free[mt % 2]])
        t6 = S.op('dve', lambda e: e.scalar_tensor_tensor(
            out=vn[:, mt, :], in0=tmp_, scalar=mv[:, 2:3], in1=glb, op0=ALU.mult, op1=ALU.add), deps=[t5])
        rt_free[mt % 2] = t6
        gvb_free[mt % 4] = t6
        st_free[mt % 4] = t6
        t_vn[mt] = t6

    def v_hook(it):
        if it in (5, 12):
            p0_compute(col_bank=0, bc_bank=1)

    run_pipeline([v_s0, v_s1, v_s2, v_s3, v_s4], len(tiles), hook=v_hook)
    while dq['n_done'] < len(dq['blocks']):
        p0_compute(col_bank=0, bc_bank=1)
    t_modc = t_modc01 + t_modc23
    t_g1 = [t_g[(0, 0)], t_g[(0, 1)]]
    t_g2 = [t_g[(1, 0)], t_g[(1, 1)]]

    if debug_stop == 'A2':
        dbg['qT'] = (qT.rearrange('p a b -> p (a b)'), [128, 4 * T], BF16)
        dbg['kT'] = (kT, [128, TH], BF16)
        dbg['kcT'] = (kcT, [128, CTX], BF16)
        dbg['vaug'] = (vaug.rearrange('p a b c -> p (a b c)'), [128, NTH * 2 * 66], BF16)
        dbg['vcaug'] = (vcaug.rearrange('p a b c -> p (a b c)'), [128, 2 * 2 * 66], BF16)
        dbg['guT'] = (guT.rearrange('p a b -> p (a b)'), [128, 4 * T], BF16)
        dbg['vn'] = (vn.rearrange('p a b -> p (a b)'), [128, NT * 512], BF16)
        return finish(nc, es, S, dbg, t_qk_all + [t_kc] + t_gu + t_v + t_vc + t_vn)

    S.barrier()
    o = RD
    wst, _, nb = A.alloc(BF16, [8, 128], at=o); o += nb
    NPT = 20
    PT = []
    for i in range(NPT):
        p_, _, nb = A.alloc(BF16, [4, 128], at=o); o += nb
        PT.append(p_)
    yatm = []
    for i in range(2):
        p_, _, nb = A.alloc(BF16, [512], at=o); o += nb
        yatm.append(p_)
    sbt = []
    for i in range(2):
        p_, _, nb = A.alloc(F32, [512], at=o); o += nb
        sbt.append(p_)
    dens, _, nb = A.alloc(F32, [2, 16], at=o); o += nb
    esk, _, nb = A.alloc(F32, [8], at=o); o += nb
    assert o <= RD + RD_SIZE, (o - RD)
    o = RE_FREE
    woutb, _, nb = A.alloc(BF16, [8, 1024], at=o); o += nb
    wstage = []
    for i in range(2):
        p_, _, nb = A.alloc(F32, [1024], at=o); o += nb
        wstage.append(p_)
    assert o <= ARENA_BYTES, (o, ARENA_BYTES)

    t_wst = pool_cast_load(wst.rearrange('p a b -> p (a b)'), wst_d, 'wst')
    t_esk = S.op('act', lambda e: e.activation(out=esk, in_=esink, func=AF.Exp), deps=[T_CONST])
    ws_free = [None, None]
    wo = {'t': None}

    def wout_step(k):
        r = k % 2
        t_l = S.dma('sp', lambda e, dst=wstage[r], k=k: e.dma_start(out=dst, in_=wout_d[:, k * 1024:(k + 1) * 1024]),
                    'wos%d' % r, deps=[ws_free[r]])
        wo['t'] = S.op('pool', lambda e, k=k, src=wstage[r]: e.tensor_tensor(out=woutb[:, k, :], in0=src, in1=g1bc, op=ALU.mult),
                       deps=[t_l] + t_g1)
        ws_free[r] = wo['t']

    fb_free = [None] * 8
    sbt_free = [None, None]
    t_yb = [None] * NT
    sc_st = {'i': 0}

    def gmlp_tile(t):
        b0 = sc_st['i'] % 4; sc_st['i'] += 1
        pf = bk(b0)
        for c in range(4):
            for gi in range(2):
                g = 2 * c + gi
                outp = pf[gi * 64:(gi + 1) * 64, c * 128:(c + 1) * 128]
                t_mm = S.op('pe', lambda e, outp=outp, g=g, t=t: e.matmul(
                    outp, lhsT=vn[:, t, g * 64:(g + 1) * 64], rhs=wst[:, g, :], start=True, stop=True),
                    deps=[t_wst, fb_free[b0], t_vn[t]] if (c == 0 and gi == 0) else (), sig=(c == 3 and gi == 1))
        r = t % 2
        t_sb = S.op('dve', lambda e, pf=pf, dst=sbt[r]: e.tensor_tensor(out=dst, in0=pf, in1=bsbc, op=ALU.add),
                    deps=[t_mm, T_CONST, sbt_free[r]])
        fb_free[b0] = t_sb
        t_yb[t] = S.op('dve', lambda e, src=sbt[r], t=t: e.tensor_tensor(
            out=ybT[:, :, t * 128:(t + 1) * 128], in0=src.rearrange('p (a b) -> p a b', a=4),
            in1=guT[:, :, t * 128:(t + 1) * 128], op=ALU.mult), deps=[t_sb])
        sbt_free[r] = t_yb[t]

    pt_free = [None] * NPT
    yatm_free = [None, None]
    o_free = [None] * 4
    t_ya = [None] * NT
    att = [dict() for _ in range(NT)]

    def att_srcs(j):
        srcs = []
        for s in range(3):
            kt = j + s
            srcs.append((kT[:, kt * 128:(kt + 1) * 128], vaug[:, kt], [t_v[kt]]))
        for s in range(2):
            srcs.append((kcT[:, s * 128:(s + 1) * 128], vcaug[:, s], [t_vc[s]]))
        return srcs

    def att_front(j):
        base = (j % 2) * 10
        srcs = att_srcs(j)
        t_p = {}
        att[j]['p'] = t_p
        for s, (ksrc, vsrc, vdep) in enumerate(srcs):
            for kv in range(2):
                bi = sc_st['i'] % 4; sc_st['i'] += 1
                psb = bk(bi)
                pt = PT[base + s * 2 + kv]
                t_mm = S.op('pe', lambda e, psb=psb, ksrc=ksrc, kv=kv, j=j: e.matmul(
                    psb, lhsT=ksrc[kv * 64:(kv + 1) * 64, :], rhs=qT[kv * 64:(kv + 1) * 64, :, j * 128:(j + 1) * 128],
                    start=True, stop=True), deps=[fb_free[bi]])
                t_e = S.op('act', lambda e, pt=pt, psb=psb: e.activation(
                    out=pt.rearrange('p a b -> p (a b)'), in_=psb, func=AF.Exp, scale=0.125),
                    deps=[t_mm, pt_free[base + s * 2 + kv]])
                fb_free[bi] = t_e
                if s == 0 or s == 2:
                    mi = (0 if j == 0 else 1) if s == 0 else (3 if j == NT - 1 else 2)
                    t_e = S.op('pool', lambda e, pt=pt, mi=mi: e.tensor_tensor(
                        out=pt, in0=pt, in1=masks[:, mi:mi + 1, :].to_broadcast([128, 4, 128]), op=ALU.mult),
                        deps=[t_e, T_CONST])
                t_p[(s, kv)] = t_e

    def att_back(j):
        base = (j % 2) * 10
        srcs = att_srcs(j)
        t_p = att[j]['p']
        t_o = [None, None]
        for kv in range(2):
            ob = bk(4 + 2 * (j % 2) + kv).rearrange('p (a b) -> p a b', a=4)
            for g in range(4):
                for s, (ksrc, vsrc, vdep) in enumerate(srcs):
                    pt = PT[base + s * 2 + kv]
                    t_mm = S.op('pe', lambda e, ob=ob, g=g, pt=pt, vsrc=vsrc, kv=kv, s=s: e.matmul(
                        ob[:, g, 0:65], lhsT=pt[:, g, :], rhs=vsrc[:, kv, 0:65], start=(s == 0), stop=(s == 4)),
                        deps=[t_p[(s, kv)], o_free[2 * (j % 2) + kv]] + vdep, sig=(g == 3 and s == 4))
            t_o[kv] = t_mm
        for s in range(5):
            for kv in range(2):
                pt_free[base + s * 2 + kv] = t_o[kv]
        r = j % 2
        dn = dens[:, r]
        ya = yatm[r].rearrange('p (k g d) -> p k g d', k=2, g=4)
        t_n = None
        for kv in range(2):
            ob = bk(4 + 2 * (j % 2) + kv).rearrange('p (a b) -> p a b', a=4)
            t_a = S.op('dve', lambda e, ob=ob, dn=dn, kv=kv: e.tensor_tensor(
                out=dn[:, kv * 4:(kv + 1) * 4], in0=ob[:, :, 64], in1=esk[:, kv * 4:(kv + 1) * 4], op=ALU.add),
                deps=[t_o[kv], t_esk, yatm_free[r]])
            t_b = S.op('dve', lambda e, dn=dn, kv=kv: e.reciprocal(out=dn[:, 8 + kv * 4:8 + (kv + 1) * 4],
                                                                  in_=dn[:, kv * 4:(kv + 1) * 4]), deps=[t_a])
            t_n = S.op('dve', lambda e, ob=ob, dn=dn, kv=kv, ya=ya: e.tensor_tensor(
                out=ya[:, kv], in0=ob[:, :, 0:64],
                in1=dn[:, 8 + kv * 4:8 + (kv + 1) * 4].unsqueeze(2).to_broadcast([128, 4, 64]), op=ALU.mult),
                deps=[t_b])
            o_free[2 * (j % 2) + kv] = t_n
        att[j]['n'] = t_n

    def att_tail(j):
        r = j % 2
        t_n = att[j]['n']
        ptb = bkb(4 + 2 * (j % 2))
        for c in range(4):
            t_tr = S.op('pe', lambda e, c=c, ptb=ptb, src=yatm[r]: e.transpose(
                ptb[:, c * 128:(c + 1) * 128], src[:, c * 128:(c + 1) * 128], ident),
                deps=[t_n] if c == 0 else (), sig=(c == 3))
        yatm_free[r] = t_tr
        t_ya[j] = S.op('dve', lambda e, ptb=ptb, j=j: e.tensor_copy(
            out=yaT[:, :, j * 128:(j + 1) * 128], in_=ptb[:, 0:512].rearrange('p (a b) -> p a b', a=4)),
            deps=[t_tr])
        o_free[2 * (j % 2)] = t_ya[j]

    att_front(0)
    for j in range(NT + 1):
        if j + 1 < NT:
            att_front(j + 1)
        if j < NT:
            gmlp_tile(j)
            att_back(j)
            if 2 <= j < 10:
                wout_step(j - 2)
        if j >= 1:
            att_tail(j - 1)

    t_wout = wo['t']
    if debug_stop == 'B':
        dbg['yaT'] = (yaT.rearrange('p a b -> p (a b)'), [128, 4 * T], BF16)
        dbg['ybT'] = (ybT.rearrange('p a b -> p (a b)'), [128, 4 * T], BF16)
        return finish(nc, es, S, dbg, t_ya + t_yb + [t_wout])

    S.barrier()
    o = RB
    wc1 = []
    t_wc1 = [None] * 8
    for m in range(8):
        p_, _, nb = A.alloc(BF16, [24, 128], at=o); o += nb
        wc1.append(p_)
        t_wc1[m] = pool_cast_load(p_.rearrange('p a b -> p (a b)'), wc1_d[m], 'wc1_%d' % m)
    outpre = {}
    sga = []
    for i in range(4):
        p_, _, nb = A.alloc(F32, [512], at=o); o += nb
        sga.append(p_)
    mt1 = []
    for i in range(4):
        p_, _, nb = A.alloc(F32, [512], at=o); o += nb
        mt1.append(p_)
    assert o <= RB + RB_SIZE, (o - RB)
    fb_free = [None] * 8
    sga_free = [None] * 4
    mt_free = [None] * 4
    t_merged = {}
    ui = 0
    for m in range(8):
        if m == 4:
            for t_ in range(NT):
                outpre['t'] = S.dma('sp', lambda e, t_=t_: e.dma_start(out=out_d[t_ * 128:(t_ + 1) * 128, :], in_=ln2b_d),
                                    'outpre', deps=[t_wc1[7]])
        wm = wc1[m]
        t_w = t_wc1[m]
        for tg in range(4):
            bs_ = (ui % 2) * 4
            r2 = (ui % 2) * 2
            ui += 1
            pa, pb_, pga, pgb = bk(bs_), bk(bs_ + 1), bk(bs_ + 2), bk(bs_ + 3)
            tok = slice(tg * 512, (tg + 1) * 512)
            htok = slice(128 + tg * 512, 128 + (tg + 1) * 512)
            for k in range(8):
                t_ga = S.op('pe', lambda e, k=k, pga=pga, wm=wm, htok=htok: e.matmul(
                    pga, lhsT=wm[:, 8 + k, :], rhs=hT[:, k, htok], start=(k == 0), stop=(k == 7)),
                    deps=[t_w, fb_free[bs_ + 2]] if k == 0 else (), sig=(k == 7))
            for k in range(8):
                t_gb = S.op('pe', lambda e, k=k, pgb=pgb, wm=wm, htok=htok: e.matmul(
                    pgb, lhsT=wm[:, 16 + k, :], rhs=hT[:, k, htok], start=(k == 0), stop=(k == 7)),
                    deps=[fb_free[bs_ + 3]] if k == 0 else (), sig=(k == 7))
            for k in range(4):
                t_a = S.op('pe', lambda e, k=k, pa=pa, wm=wm, tok=tok: e.matmul(
                    pa, lhsT=wm[:, k, :], rhs=yaT[:, k, tok], start=(k == 0), stop=(k == 3)),
                    deps=[fb_free[bs_]] if k == 0 else (), sig=(k == 3))
            for k in range(4):
                t_b = S.op('pe', lambda e, k=k, pb_=pb_, wm=wm, tok=tok: e.matmul(
                    pb_, lhsT=wm[:, 4 + k, :], rhs=ybT[:, k, tok], start=(k == 0), stop=(k == 3)),
                    deps=[fb_free[bs_ + 1]] if k == 0 else (), sig=(k == 3))
            t_sa = S.op('act', lambda e, dst=sga[r2], pga=pga: e.activation(out=dst, in_=pga, func=AF.Sigmoid),
                        deps=[t_ga, sga_free[r2]])
            t_sb = S.op('act', lambda e, dst=sga[r2 + 1], pgb=pgb: e.activation(out=dst, in_=pgb, func=AF.Sigmoid),
                        deps=[t_gb, sga_free[r2 + 1]])
            fb_free[bs_ + 2] = t_sa
            fb_free[bs_ + 3] = t_sb
            t_1 = S.op('dve', lambda e, dst=mt1[r2], pa=pa, sa=sga[r2]: e.tensor_tensor(out=dst, in0=pa, in1=sa, op=ALU.mult),
                       deps=[t_a, t_sa, mt_free[r2]])
            t_2 = S.op('dve', lambda e, dst=mt1[r2 + 1], pb_=pb_, sb=sga[r2 + 1]: e.tensor_tensor(out=dst, in0=pb_, in1=sb, op=ALU.mult),
                       deps=[t_b, t_sb, mt_free[r2 + 1]])
            fb_free[bs_] = t_1
            fb_free[bs_ + 1] = t_2
            sga_free[r2] = t_1
            sga_free[r2 + 1] = t_2
            t_3 = S.op('pool', lambda e, m=m, tok=tok, a=mt1[r2], b=mt1[r2 + 1]: e.tensor_tensor(
                out=merged[:, m, tok], in0=a, in1=b, op=ALU.add), deps=[t_1, t_2])
            mt_free[r2] = t_3
            mt_free[r2 + 1] = t_3
            t_merged[(m, tg)] = t_3

    if debug_stop == 'C1':
        dbg['merged'] = (merged.rearrange('p a b -> p (a b)'), [128, 8 * T], BF16)
        return finish(nc, es, S, dbg, list(t_merged.values()) + [t_wout])

    S.barrier()
    o = RC
    NXR2, NWK = 2, 4
    xr = []
    for i in range(NXR2):
        x_, _, nb = A.alloc(F32, [D], at=o); o += nb
        xr.append(x_)
    wk = []
    for i in range(NWK):
        x_, _, nb = A.alloc(F32, [D], at=o); o += nb
        wk.append(x_)
    xnb = []
    for i in range(2):
        x_, _, nb = A.alloc(BF16, [D], at=o); o += nb
        xnb.append(x_)
    assert o <= RC + 28 * 1024, (o - RC)
    NWR = 3
    w1 = [None] * NWR
    w2 = [None] * NWR
    w1[0], _, _ = A.alloc(BF16, [8, 512], at=RC + 28 * 1024)
    t_w1_pre = pool_cast_load(w1[0].rearrange('p a b -> p (a b)'), wff1_d[0], 'wf1_0')
    ln1g, ln1b = wstage[0], wstage[1]
    t_ln1 = [sp_load(ln1g, ln1g_d, 'ln1'), sp_load(ln1b, ln1b_d, 'ln1')][-1]

    fb_free = [None] * 8
    xr_free = [None] * NXR2
    wk_free = [None] * NWK
    xnb_free = [None, None]
    t_h2 = [None] * NT
    t_xmid = [None] * NT
    c2 = [dict() for _ in range(NT)]

    def c2_s0(t):
        r = t % NXR2
        c2[t]['x'] = S.dma('sp', lambda e, dst=xr[r], t=t: e.dma_start(out=dst, in_=xh[(t + 1) * 128:(t + 2) * 128, :]),
                           'xr%d' % r, deps=[xr_free[r]])
        b0 = (t % 2) * 2
        c2[t]['mm'] = []
        for half in range(2):
            pm_ = bk(b0 + half)
            for k in range(8):
                t_mm = S.op('pe', lambda e, k=k, pm_=pm_, t=t, half=half: e.matmul(
                    pm_, lhsT=merged[:, k, t * 128:(t + 1) * 128], rhs=woutb[:, k, half * 512:(half + 1) * 512],
                    start=(k == 0), stop=(k == 7)),
                    deps=[t_wout, fb_free[b0 + half]] + [t_merged[(kk, t // 4)] for kk in range(8)] if k == 0 else (),
                    sig=(k == 7))
            c2[t]['mm'].append(t_mm)

    def c2_s1(t):
        r = t % NXR2
        w = wk[t % NWK]
        b0 = (t % 2) * 2
        for half in range(2):
            pm_ = bk(b0 + half)
            t_pre = S.op('dve', lambda e, pm_=pm_, r=r, half=half, w=w: e.scalar_tensor_tensor(
                out=w[:, half * 512:(half + 1) * 512], in0=xr[r][:, half * 512:(half + 1) * 512],
                scalar=ALPHA, in1=pm_, op0=ALU.mult, op1=ALU.add),
                deps=[c2[t]['mm'][half], c2[t]['x'], wk_free[t % NWK]])
            fb_free[b0 + half] = t_pre
        xr_free[r] = t_pre
        sl = t % 4
        st = stt[:, sl]; mv = mvr[:, sl]
        S.op('dve', lambda e: e.bn_stats(out=st[:, 0:6], in_=w[:, 0:512]), deps=[t_pre, st_free[sl]], sig=False)
        t1 = S.op('dve', lambda e: e.bn_stats(out=st[:, 6:12], in_=w[:, 512:1024]))
        c2[t]['ag1'] = S.op('dve', lambda e: e.bn_aggr(out=mv[:, 0:2], in_=st[:, 0:12]), deps=[t1])

    def c2_s2(t):
        mv = mvr[:, t % 4]
        c2[t]['sq1'] = S.op('act', lambda e: e.activation(out=mv[:, 4:5], in_=mv[:, 1:2], func=AF.Sqrt, bias=epst[:, 0:1],
                                                          scale=1.0), deps=[c2[t]['ag1'], t_m3])

    def c2_s3(t):
        _, _, _, c2[t]['r1'] = rstd_part(t % 4, [c2[t]['sq1']])

    def c2_s4(t):
        mv = mvr[:, t % 4]
        w = wk[t % NWK]
        c2[t]['n1'] = S.op('act', lambda e: e.activation(out=w, in_=w, func=AF.Identity, scale=mv[:, 2:3], bias=mv[:, 3:4]),
                           deps=[c2[t]['r1']])
        st_free[t % 4] = c2[t]['n1']

    def c2_s5(t):
        w = wk[t % NWK]
        t_g_ = S.op('pool', lambda e: e.tensor_tensor(out=xmid[:, t, :], in0=w, in1=ln1g, op=ALU.mult),
                    deps=[c2[t]['n1'], t_ln1])
        wk_free[t % NWK] = t_g_
        t_xmid[t] = S.dma('pool', lambda e: e.dma_start(out=xmid[:, t, :], in_=ln1b, accum_op=ALU.add), 'xb%d' % t,
                          deps=[t_g_, t_ln1])

    def c2_s5w(t):
        pass

    def c2_s6(t):
        sl = 4 + t % 4
        st = stt[:, sl]; mv = mvr[:, sl]
        src = xmid[:, t, :]
        S.op('dve', lambda e: e.bn_stats(out=st[:, 0:6], in_=src[:, 0:512]), deps=[t_xmid[t], st_free[sl]], sig=False)
        t1 = S.op('dve', lambda e: e.bn_stats(out=st[:, 6:12], in_=src[:, 512:1024]))
        c2[t]['ag2'] = S.op('dve', lambda e: e.bn_aggr(out=mv[:, 0:2], in_=st[:, 0:12]), deps=[t1])

    def c2_s7(t):
        mv = mvr[:, 4 + t % 4]
        c2[t]['sq2'] = S.op('act', lambda e: e.activation(out=mv[:, 4:5], in_=mv[:, 1:2], func=AF.Sqrt, bias=epst[:, 0:1],
                                                          scale=1.0), deps=[c2[t]['ag2']])

    def c2_s8(t):
        _, _, _, c2[t]['r2'] = rstd_part(4 + t % 4, [c2[t]['sq2']])

    def c2_s9(t):
        mv = mvr[:, 4 + t % 4]
        r = t % 2
        c2[t]['n2'] = S.op('act', lambda e: e.activation(out=xnb[r], in_=xmid[:, t, :], func=AF.Identity,
                                                         scale=mv[:, 2:3], bias=mv[:, 3:4]), deps=[c2[t]['r2'], xnb_free[r]])
        st_free[4 + t % 4] = c2[t]['n2']

    def c2_s10(t):
        r = t % 2
        ptA = bkb(4 + 2 * r)
        ptB = bkb(5 + 2 * r)
        for c in range(8):
            pt = ptA if c < 4 else ptB
            t_tr = S.op('pe', lambda e, c=c, pt=pt, src=xnb[r]: e.transpose(
                pt[:, (c % 4) * 128:(c % 4 + 1) * 128], src[:, c * 128:(c + 1) * 128], ident),
                deps=[c2[t]['n2'], fb_free[4 + 2 * r], fb_free[5 + 2 * r]] if c == 0 else (), sig=(c == 3 or c == 7))
            if c == 3:
                c2[t]['trA'] = t_tr
        c2[t]['trB'] = t_tr
        xnb_free[r] = t_tr

    def c2_s11(t):
        r = t % 2
        ptA = bkb(4 + 2 * r)
        ptB = bkb(5 + 2 * r)
        for c in range(8):
            dst = h2T[:, c, t * 128:(t + 1) * 128]
            pt = ptA if c < 4 else ptB
            t_ev = S.op('act', lambda e, c=c, dst=dst, pt=pt: e.activation(
                out=dst, in_=pt[:, (c % 4) * 128:(c % 4 + 1) * 128], func=AF.Identity,
                scale=mcol(3, c, 0), bias=mcol(2, c, 0)),
                deps=[c2[t]['trA'], c2[t]['trB'], t_modc23] if c == 0 else (), sig=(c == 3 or c == 7))
            if c == 3:
                t_evA = t_ev
        t_evB = t_ev
        fb_free[4 + 2 * r] = t_evA
        fb_free[5 + 2 * r] = t_evB
        t_h2[t] = [t_evA, t_evB]

    run_pipeline([c2_s0, c2_s1, c2_s2, c2_s3, c2_s4, c2_s5, c2_s5w, c2_s6, c2_s7, c2_s8, c2_s9, c2_s10, c2_s11], NT)

    if debug_stop == 'C2':
        dbg['xmid'] = (xmid.rearrange('p a b -> p (a b)'), [128, NT * D], F32)
        dbg['h2T'] = (h2T.rearrange('p a b -> p (a b)'), [128, 8 * T], BF16)
        return finish(nc, es, S, dbg, t_h2 + t_xmid)

    S.barrier()
    o = RC
    for i in range(1, NWR):
        w1[i], _, nb = A.alloc(BF16, [8, 512], at=o); o += nb
    for i in range(NWR):
        w2[i], _, nb = A.alloc(BF16, [FG, 1024], at=o); o += nb
    assert o <= RC + 28 * 1024, (o - RC)
    o = RD
    actb = []
    for i in range(2):
        p_, _, nb = A.alloc(BF16, [FG, T], at=o); o += nb
        actb.append(p_)
    w2st = []
    for i in range(2):
        p_, _, nb = A.alloc(F32, [FG, 1024], at=o); o += nb
        w2st.append(p_)
    assert o <= RD + RD_SIZE, (o - RD)
    o = RE + 64
    sgb = []
    for i in range(2):
        p_, _, nb = A.alloc(F32, [512], at=o); o += nb
        sgb.append(p_)
    ln2g, _, nb = A.alloc(F32, [D], at=o); o += nb
    ln2b, _, nb = A.alloc(F32, [D], at=o); o += nb
    y0 = []
    for i in range(3):
        p_, _, nb = A.alloc(F32, [D], at=o); o += nb
        y0.append(p_)
    assert o <= ARENA_BYTES, (o, ARENA_BYTES)
    t_ln2 = [sp_load(ln2g, ln2g_d, 'ln2'), sp_load(ln2b, ln2b_d, 'ln2')][-1]

    w_free = [None] * NWR
    w2st_free = [None, None]
    t_w1 = [None] * NR
    t_w2 = [None] * NR

    def load_round(r):
        slot = r % NWR
        if r == 0:
            t_w1[r] = t_w1_pre
        else:
            t_w1[r] = pool_cast_load(w1[slot].rearrange('p a b -> p (a b)'), wff1_d[r], 'wf1_%d' % slot, deps=[w_free[slot]])
        s2 = r % 2
        t_l = S.dma('sp', lambda e, dst=w2st[s2], r=r: e.dma_start(out=dst.rearrange('p a b -> p (a b)'), in_=wff2_d[r]),
                    'wf2_%d' % s2, deps=[w2st_free[s2]])
        t_w2[r] = S.op('pool', lambda e, slot=slot, s2=s2: e.tensor_tensor(
            out=w2[slot], in0=w2st[s2], in1=g2bc.unsqueeze(1).to_broadcast([128, FG, 1024]), op=ALU.mult),
            deps=[t_l, w_free[slot]] + t_g2)
        w2st_free[s2] = t_w2[r]

    fb_free = [None] * 8
    sg_free = [None, None]
    act_free = [None, None]
    t_act = {}
    t_acc = [None] * NT
    gu_i = 0

    def emit_gu_unit(r, tg, fi):
        nonlocal gu_i
        slot = r % NWR
        ab = actb[r % 2]
        b0 = (gu_i % 2) * 2
        s_ = gu_i % 2
        gu_i += 1
        pg, pu = bk(b0), bk(b0 + 1)
        tok = slice(tg * 512, (tg + 1) * 512)
        for k in range(8):
            t_g_ = S.op('pe', lambda e, k=k, pg=pg, slot=slot, fi=fi, tok=tok: e.matmul(
                pg, lhsT=w1[slot][:, k, fi * 128:(fi + 1) * 128], rhs=h2T[:, k, tok], start=(k == 0), stop=(k == 7)),
                deps=[t_w1[r], fb_free[b0]] if k == 0 else (), sig=(k == 7))
        for k in range(8):
            t_u_ = S.op('pe', lambda e, k=k, pu=pu, slot=slot, fi=fi, tok=tok: e.matmul(
                pu, lhsT=w1[slot][:, k, 256 + fi * 128:256 + (fi + 1) * 128], rhs=h2T[:, k, tok],
                start=(k == 0), stop=(k == 7)), deps=[fb_free[b0 + 1]] if k == 0 else (), sig=(k == 7))
        t_s = S.op('act', lambda e, dst=sgb[s_], pg=pg: e.activation(out=dst, in_=pg, func=AF.Silu),
                   deps=[t_g_, sg_free[s_]])
        fb_free[b0] = t_s
        t_m = S.op('dve', lambda e, ab=ab, fi=fi, tok=tok, pu=pu, sg=sgb[s_]: e.tensor_tensor(
            out=ab[:, fi, tok], in0=pu, in1=sg, op=ALU.mult), deps=[t_u_, t_s, act_free[r % 2]])
        fb_free[b0 + 1] = t_m
        sg_free[s_] = t_m
        t_act[(r, fi, tg)] = t_m

    def emit_gu_tg(r, tg):
        for fi in range(FG):
            emit_gu_unit(r, tg, fi)

    def emit_gu(r):
        for tg in range(4):
            emit_gu_tg(r, tg)

    def emit_dn_tile(rounds, t):
        b0 = 4 + (t % 2) * 2
        last_mm = None
        for half in range(2):
            pd = bk(b0 + half)
            n_mm = len(rounds) * FG
            i_mm = 0
            for r in rounds:
                slot = r % NWR
                ab = actb[r % 2]
                for fi in range(FG):
                    t_mm = S.op('pe', lambda e, pd=pd, fi=fi, t=t, half=half, ab=ab, slot=slot, i_mm=i_mm, n_mm=n_mm: e.matmul(
                        pd, lhsT=ab[:, fi, t * 128:(t + 1) * 128], rhs=w2[slot][:, fi, half * 512:(half + 1) * 512],
                        start=(i_mm == 0), stop=(i_mm == n_mm - 1)),
                        deps=[t_w2[r], fb_free[b0 + half], t_act[(r, fi, t // 4)]], sig=(i_mm == n_mm - 1))
                    i_mm += 1
            accv = xmid[:, t, half * 512:(half + 1) * 512]
            if rounds[0] == 0:
                t_ad = S.op('dve', lambda e, accv=accv, pd=pd: e.scalar_tensor_tensor(
                    out=accv, in0=accv, scalar=ALPHA, in1=pd, op0=ALU.mult, op1=ALU.add), deps=[t_mm, t_acc[t]])
            else:
                t_ad = S.op('dve', lambda e, accv=accv, pd=pd: e.tensor_tensor(
                    out=accv, in0=pd, in1=accv, op=ALU.add), deps=[t_mm, t_acc[t]])
            fb_free[b0 + half] = t_ad
            t_acc[t] = t_ad
            last_mm = t_mm
        return last_mm

    def emit_dn(r):
        last_mm = None
        for t in range(NT):
            last_mm = emit_dn_tile([r], t)
        act_free[r % 2] = last_mm
        w_free[r % NWR] = last_mm

    y_free = [None] * 3
    t_out = []
    tl = [dict() for _ in range(NT)]

    def tail_s1(t):
        sl = t % 4
        st = stt[:, sl]; mv = mvr[:, sl]
        src = xmid[:, t, :]
        S.op('dve', lambda e: e.bn_stats(out=st[:, 0:6], in_=src[:, 0:512]), deps=[t_acc[t], st_free[sl]], sig=False)
        t1 = S.op('dve', lambda e: e.bn_stats(out=st[:, 6:12], in_=src[:, 512:1024]))
        tl[t]['ag'] = S.op('dve', lambda e: e.bn_aggr(out=mv[:, 0:2], in_=st[:, 0:12]), deps=[t1])

    def tail_s2(t):
        mv = mvr[:, t % 4]
        tl[t]['sq'] = S.op('act', lambda e: e.activation(out=mv[:, 4:5], in_=mv[:, 1:2], func=AF.Sqrt, bias=epst[:, 0:1],
                                                         scale=1.0), deps=[tl[t]['ag']])

    def tail_s3(t):
        _, _, _, tl[t]['r'] = rstd_part(t % 4, [tl[t]['sq']])

    def tail_s4(t):
        mv = mvr[:, t % 4]
        r3 = t % 3
        tl[t]['n'] = S.op('act', lambda e: e.activation(out=y0[r3], in_=xmid[:, t, :], func=AF.Identity,
                                                        scale=mv[:, 2:3], bias=mv[:, 3:4]), deps=[tl[t]['r'], y_free[r3]])
        st_free[t % 4] = tl[t]['n']

    def tail_s5(t):
        r3 = t % 3
        tl[t]['g'] = S.op('pool', lambda e: e.tensor_tensor(out=y0[r3], in0=y0[r3], in1=ln2g, op=ALU.mult),
                          deps=[tl[t]['n'], t_ln2])

    def tail_s6(t):
        r3 = t % 3
        t_st_ = S.dma('pool', lambda e, t=t: e.dma_start(out=out_d[t * 128:(t + 1) * 128, :], in_=y0[r3], accum_op=ALU.add),
                      'out%d' % r3, deps=[tl[t]['g'], outpre['t']])
        y_free[r3] = t_st_
        t_out.append(t_st_)

    tail_stages = [tail_s1, tail_s2, tail_s3, tail_s4, tail_s5, tail_s6]
    tail_state = {'n': 0}

    def tail_step(t_new):
        it = tail_state['n']; tail_state['n'] += 1
        K = len(tail_stages)
        for k in reversed(range(K)):
            i = it - k
            if 0 <= i < NT and (t_new is not None or True):
                if i <= (t_new if t_new is not None else NT - 1):
                    tail_stages[k](i)

    for r in range(min(NWR, NR)):
        load_round(r)
    R1, R2 = NR - 2, NR - 1
    emit_gu(0)
    for r in range(NR - 2):
        if r + 1 < NR - 2:
            gu_units = [(tg, fi) for tg in range(4) for fi in range(FG)]
            per = NT // len(gu_units)
            last_mm = None
            for i_, (tg, fi) in enumerate(gu_units):
                emit_gu_unit(r + 1, tg, fi)
                for t in range(i_ * per, (i_ + 1) * per):
                    last_mm = emit_dn_tile([r], t)
            act_free[r % 2] = last_mm
            w_free[r % NWR] = last_mm
        else:
            emit_gu_tg(R1, 0)
            emit_dn(r)
        if r + NWR < NR:
            load_round(r + NWR)
    emit_gu_tg(R2, 0)
    order = [('GU', 1), ('DN', 0), ('GU', 2), ('DN', 1), ('GU', 3), ('DN', 2), ('DN', 3)]
    for kind_, tg in order:
        if kind_ == 'GU':
            emit_gu_tg(R1, tg)
            emit_gu_tg(R2, tg)
        else:
            for t in range(4 * tg, 4 * tg + 4):
                emit_dn_tile([R1, R2], t)
                tail_step(t)
    for _ in range(len(tail_stages)):
        tail_step(None)
    assert len(t_out) == NT
    return finish(nc, es, S, dbg, t_out)


def finish(nc, es, S, dbg, final_ticks):
    dbg_ticks = []
    for name, spec in dbg.items():
        ap, shape = spec[0], spec[1]
        dt_ = spec[2] if len(spec) > 2 else F32
        d = nc.dram_tensor("dbg_" + name, list(shape), dt_, kind="ExternalOutput").ap()
        dbg_ticks.append(S.dma('sp', lambda e, d=d, ap=ap: e.dma_start(out=d, in_=ap), 'dbg', deps=final_ticks))
    S.barrier()
    S.emit(nc, es)
    es.close()
    return nc


def _rope_tables(start):
    pos = np.arange(start - 128, start - 128 + TH)
    rows = (pos // 64).astype(np.float64)
    cols = (pos % 64).astype(np.float64)
    inv = 10000.0 ** (-np.arange(16, dtype=np.float64) / 16)
    C = np.zeros((64, TH), np.float64)
    Sg = np.zeros((64, TH), np.float64)
    for d in range(64):
        p_ = rows if d < 32 else cols
        dd = d % 32
        i = dd % 16
        ang = p_ * inv[i]
        C[d] = np.cos(ang)
        Sg[d] = -np.sin(ang) if dd < 16 else np.sin(ang)
    C = C.astype(np.float32)
    Sg = Sg.astype(np.float32)
    return np.concatenate([C, C], 0), np.concatenate([Sg, Sg], 0)


def _perm_matrix():
    P = np.zeros((128, 128), np.float32)
    for m in range(128):
        d = m % 64
        dd = d % 32
        partner = d + 16 if dd < 16 else d - 16
        P[(m // 64) * 64 + partner, m] = 1.0
    return P


def prep_inputs(inp):
    f = lambda a: np.ascontiguousarray(np.asarray(a, dtype=np.float32))
    x = f(inp['x']); c = f(inp['c']); ctx = f(inp['ctx']); c_ctx = f(inp['c_ctx'])
    w_ada = f(inp['w_ada'])[0]; b_ada = f(inp['b_ada'])[0]; w_in = f(inp['w_in'])[0]
    sink = f(inp['attn_sink'])[0]
    glg = f(inp['gmlp_ln_g'])[0]; glb = f(inp['gmlp_ln_b'])[0]
    w_s = f(inp['w_spatial'])[0]; b_s = f(inp['b_spatial'])[0]
    w_a = f(inp['w_branch_a'])[0]; w_b = f(inp['w_branch_b'])[0]; w_out = f(inp['w_out'])[0]
    ln1g = f(inp['ln1_g'])[0]; ln1b = f(inp['ln1_b'])[0]; ln2g = f(inp['ln2_g'])[0]; ln2b = f(inp['ln2_b'])[0]
    w_ffn_in = f(inp['w_ffn_in'])[0]; w_ffn_out = f(inp['w_ffn_out'])[0]

    def ktile(w):
        n = w.shape[1]
        return np.ascontiguousarray(w.reshape(8, 128, n).transpose(1, 0, 2)).reshape(128, 8 * n)

    wada_t = np.ascontiguousarray(w_ada.reshape(8, 128, 12, 512).transpose(2, 1, 0, 3)).reshape(12, 128, 4096)
    qcols = []
    for cc in range(4):
        qcols += list(range(cc * 64, cc * 64 + 64)) + list(range((4 + cc) * 64, (4 + cc) * 64 + 64))
    qkcols = qcols + list(range(512, 640))
    wqk = ktile(w_in[:, qkcols])
    wu = ktile(w_in[:, 768:1280])
    wvvb = ktile(w_in[:, list(range(640, 768)) + list(range(1280, 1792))])
    wga = w_in[:, 1792:2816]
    wgb = w_in[:, 2816:3840]
    wc1 = np.zeros((8, 128, 24, 128), np.float32)
    for m in range(8):
        cs = slice(m * 128, (m + 1) * 128)
        wc1[m, :, 0:4] = w_a[:, cs].reshape(4, 128, 128).transpose(1, 0, 2)
        wc1[m, :, 4:8] = w_b[:, cs].reshape(4, 128, 128).transpose(1, 0, 2)
        wc1[m, :, 8:16] = wga[:, cs].reshape(8, 128, 128).transpose(1, 0, 2)
        wc1[m, :, 16:24] = wgb[:, cs].reshape(8, 128, 128).transpose(1, 0, 2)
    wc1 = wc1.reshape(8, 128, 24 * 128)
    wout_t = ktile(w_out)
    wst = np.ascontiguousarray(w_s.transpose(2, 0, 1)).reshape(128, 8 * 128)
    bsbc = np.zeros((128, 4, 128), np.float32)
    for cc in range(4):
        for gi in range(2):
            bsbc[gi * 64:(gi + 1) * 64, cc, :] = b_s[2 * cc + gi][None, :]
    bsbc = bsbc.reshape(128, 512)
    badac = np.zeros((128, 4, 8, 2), np.float32)
    for kind, off in enumerate((0, 1024, 3072, 4096)):
        badac[:, kind, :, :] = b_ada[off:off + 1024].reshape(8, 128).T[:, :, None]
    badac = badac.reshape(128, 64)
    badabc = np.ascontiguousarray(np.broadcast_to(
        np.concatenate([b_ada[2048:3072], b_ada[5120:6144]])[None, :], (128, 2048)))
    wff1 = np.zeros((NR, 128, 8, 512), np.float32)
    wff2 = np.zeros((NR, 128, FG, 1024), np.float32)
    for r in range(NR):
        for fi in range(FG):
            fch = r * FG + fi
            wff1[r, :, :, fi * 128:(fi + 1) * 128] = w_ffn_in[:, fch * 128:(fch + 1) * 128].reshape(8, 128, 128).transpose(1, 0, 2)
            wff1[r, :, :, 256 + fi * 128:256 + (fi + 1) * 128] = \
                w_ffn_in[:, FFH + fch * 128:FFH + (fch + 1) * 128].reshape(8, 128, 128).transpose(1, 0, 2)
            wff2[r, :, fi, :] = w_ffn_out[fch * 128:(fch + 1) * 128, :]
    wff1 = wff1.reshape(NR, 128, 4096)
    wff2 = wff2.reshape(NR, 128, FG * 1024)
    ident = np.eye(128, dtype=np.float32)
    perm = _perm_matrix()
    ki = np.arange(128)[:, None]
    qi = np.arange(128)[None, :]
    maskP = (ki >= qi).astype(np.float32)
    maskN = (ki <= qi).astype(np.float32)
    zero = np.zeros((128, 128), np.float32)
    bc = lambda v: np.ascontiguousarray(np.broadcast_to(v[None, :], (128, v.shape[0])))
    shared = dict(wada=wada_t, badac=badac, badabc=badabc, wqk=wqk, wu=wu, wvvb=wvvb, wc1=wc1, wout=wout_t,
                  wst=wst, bsbc=bsbc, wff1=wff1, wff2=wff2, ident=ident, perm=perm, esink=bc(sink),
                  glg=bc(glg), glb=bc(glb), ln1g=bc(ln1g), ln1b=bc(ln1b), ln2g=bc(ln2g), ln2b=bc(ln2b))
    in_maps = []
    for core in range(NCORES):
        b = core // 4
        seg = core % 4
        start = seg * T
        xhalo = np.zeros((TH, D), np.float32)
        lo = max(start - 128, 0)
        hi = min(start + T + 128, SEQ)
        xhalo[lo - (start - 128): hi - (start - 128)] = x[b, lo:hi]
        cv = np.zeros((128, 16), np.float32)
        cv[:, 0:8] = c[b].reshape(8, 128).T
        cv[:, 8:16] = c_ctx.reshape(8, 128).T
        rc, rs = _rope_tables(start)
        mk = np.stack([zero if seg == 0 else maskP, maskP, maskN, zero if seg == 3 else maskN], 1).reshape(128, 512)
        m = dict(shared)
        m.update(xh=xhalo, ctxb=np.ascontiguousarray(ctx[b]), cvec=cv, ropec=rc, ropes=rs, masks=np.ascontiguousarray(mk))
        in_maps.append(m)
    return in_maps


_NC_CACHE = {}


def kernel(**inputs):
    in_maps = prep_inputs(inputs)
    if 'nc' not in _NC_CACHE:
        _NC_CACHE['nc'] = build_nc(None)
    nc = _NC_CACHE['nc']
    res = run_bass_kernel_spmd(nc, in_maps, core_ids=list(range(NCORES)))
    out = np.zeros((2, SEQ, D), np.float32)
    for core in range(NCORES):
        b = core // 4
        seg = core % 4
        out[b, seg * T:(seg + 1) * T] = res.results[core]["out"]
    return out
```

```python
import numpy as np
from contextlib import ExitStack
import concourse.bass as bass
import concourse.mybir as mybir
from concourse.bass_utils import run_bass_kernel_spmd

F32 = mybir.dt.float32
BF16 = mybir.dt.bfloat16
AF = mybir.ActivationFunctionType
ALU = mybir.AluOpType

NCORES = 8
D = 1024
T = 2048
NT = 16
TH = 2304
NTH = 18
CTX = 256
FFH = 2816
NF = 22
FG = 2
NR = NF // FG
ALPHA = 2.0 ** 0.25
EPS = 1e-5
SEQ = 8192
IN_SPLITS = (512, 640, 768, 1280, 1792, 2816, 3840)

DEBUG_STOP = None


class Sched:
    ENG = ('pe', 'act', 'dve', 'pool', 'sp')

    def __init__(self):
        self.q = {e: [] for e in self.ENG}
        self.cnt = {e: 0 for e in self.ENG}
        self.waited = {e: {} for e in self.ENG}
        self.dmacnt = {}

    def _resolve(self, eng, deps):
        ws = []
        stack = [deps]
        flat = []
        while stack:
            d = stack.pop()
            if d is None:
                continue
            if isinstance(d, tuple) and len(d) == 2 and isinstance(d[0], str):
                flat.append(d)
            else:
                stack.extend(list(d))
        for (p, t) in flat:
            if self.waited[eng].get(p, 0) >= t:
                continue
            self.waited[eng][p] = t
            ws.append((p, t))
        return ws

    def op(self, eng, fn, deps=(), sig=True):
        ws = self._resolve(eng, deps)
        tick = None
        if sig:
            self.cnt[eng] += 1
            tick = (eng, self.cnt[eng])
        self.q[eng].append((ws, fn, eng if sig else None, 1))
        return tick

    def dma(self, eng, fn, key, deps=()):
        ws = self._resolve(eng, deps)
        self.dmacnt[key] = self.dmacnt.get(key, 0) + 16
        self.q[eng].append((ws, fn, 'dma:' + key, 16))
        return ('dma:' + key, self.dmacnt[key])

    def barrier(self):
        ticks = [(e, c) for e, c in self.cnt.items() if c > 0]
        ticks += [('dma:' + k, c) for k, c in self.dmacnt.items()]
        for e in self.ENG:
            ws = self._resolve(e, ticks)
            if ws:
                self.q[e].append((ws, None, None, 0))

    def emit(self, nc, es):
        sems = {}
        for e in self.ENG:
            sems[e] = es.enter_context(nc.semaphore("s_" + e))
        for k in self.dmacnt:
            sems['dma:' + k] = es.enter_context(nc.semaphore("d_" + k))
        block = es.enter_context(nc.Block())
        reg = {'pe': block.tensor, 'act': block.scalar, 'dve': block.vector,
               'pool': block.gpsimd, 'sp': block.sync}
        for e in self.ENG:
            items = self.q[e]

            def body(engine, items=items):
                for (ws, fn, sigkey, inc) in items:
                    for (p, t) in ws:
                        engine.wait_ge(sems[p], t)
                    if fn is None:
                        continue
                    ins = fn(engine)
                    if sigkey is not None:
                        ins.then_inc(sems[sigkey], inc)
            reg[e](body)


class Arena:
    def __init__(self, nc, es, nbytes):
        self.t = es.enter_context(nc.sbuf_tensor("arena", [128, nbytes // 2], BF16))
        self.nbytes = nbytes
        self.top = 0
        self.marks = {}

    def alloc(self, dtype, shape, at=None):
        esz = 4 if dtype == F32 else 2
        n = int(np.prod(shape))
        nb = (n * esz + 63) // 64 * 64
        if at is None:
            off = self.top
            self.top += nb
        else:
            off = at
        assert off + nb <= self.nbytes, ("arena overflow", off, nb, self.nbytes)
        a = self.t[:, off // 2: off // 2 + n * esz // 2]
        if dtype == F32:
            a = a.bitcast(F32)
        if len(shape) == 2:
            a = a.rearrange('p (a b) -> p a b', a=shape[0])
        elif len(shape) == 3:
            a = a.rearrange('p (a b c) -> p a b c', a=shape[0], b=shape[1])
        return a, off, nb


def run_pipeline(stages, n_items, hook=None):
    K = len(stages)
    for it in range(n_items + K - 1):
        for k in reversed(range(K)):
            i = it - k
            if 0 <= i < n_items:
                stages[k](i)
        if hook is not None:
            hook(it)


def build_nc(debug_stop=None):
    nc = bass.Bass("TRN2", target_bir_lowering=False)
    es = ExitStack()
    S = Sched()

    def din(name, shape):
        return nc.dram_tensor(name, list(shape), F32, kind="ExternalInput").ap()

    xh = din("xh", [TH, D])
    ctxb = din("ctxb", [CTX, D])
    cvec = din("cvec", [128, 16])
    wada = din("wada", [12, 128, 8 * 512])
    badac_d = din("badac", [128, 64])
    badabc_d = din("badabc", [128, 2048])
    wqk_d = din("wqk", [128, 8 * 640])
    wu_d = din("wu", [128, 8 * 512])
    wvvb_d = din("wvvb", [128, 8 * 640])
    wc1_d = din("wc1", [8, 128, 24 * 128])
    wout_d = din("wout", [128, 8 * 1024])
    wst_d = din("wst", [128, 8 * 128])
    bsbc_d = din("bsbc", [128, 512])
    wff1_d = din("wff1", [NR, 128, 8 * 512])
    wff2_d = din("wff2", [NR, 128, FG * 1024])
    ropec_d = din("ropec", [128, TH])
    ropes_d = din("ropes", [128, TH])
    masks_d = din("masks", [128, 4 * 128])
    ident_d = din("ident", [128, 128])
    perm_d = din("perm", [128, 128])
    esink_d = din("esink", [128, 8])
    glg_d = din("glg", [128, 512])
    glb_d = din("glb", [128, 512])
    ln1g_d = din("ln1g", [128, D])
    ln1b_d = din("ln1b", [128, D])
    ln2g_d = din("ln2g", [128, D])
    ln2b_d = din("ln2b", [128, D])
    out_d = nc.dram_tensor("out", [T, D], F32, kind="ExternalOutput").ap()
    dbg = {}

    ARENA_BYTES = 207 * 1024
    A = Arena(nc, es, ARENA_BYTES)
    banks = [es.enter_context(nc.psum_tensor("bank%d" % i, [128, 512], F32)) for i in range(8)]

    def bk(i):
        return banks[i][:, :]

    def bkb(i):
        return banks[i][:, :].bitcast(BF16)

    ident, _, _ = A.alloc(BF16, [128])
    perm, _, _ = A.alloc(BF16, [128])
    masks, _, _ = A.alloc(BF16, [4, 128])
    esink, _, _ = A.alloc(F32, [8])
    modc, _, _ = A.alloc(F32, [64])
    sc, _, _ = A.alloc(BF16, [16])
    small, _, _ = A.alloc(F32, [64])
    onesf, _, _ = A.alloc(F32, [128])
    neghalf, _, _ = A.alloc(F32, [32])
    bsbc, _, _ = A.alloc(F32, [512])
    g2bc, _, _ = A.alloc(F32, [1024])
    stt, _, _ = A.alloc(F32, [8, 16])
    mvr, _, _ = A.alloc(F32, [8, 8])
    st_free = [None] * 8
    epst, _, _ = A.alloc(F32, [16])
    CONST_END = A.top

    RA = A.top
    hT, _, nbA = A.alloc(BF16, [8, TH])
    RB = A.top
    RB_SIZE = 64 * 1024
    A.top = RB + RB_SIZE
    RC = A.top
    RC_SIZE = 36 * 1024
    A.top = RC + RC_SIZE
    RD = A.top
    RD_SIZE = 32 * 1024
    A.top = RD + RD_SIZE
    RE = A.top
    RE_SIZE = ARENA_BYTES - RE
    assert RE_SIZE >= 24 * 1024, RE_SIZE

    o = RB
    qT, _, nb = A.alloc(BF16, [4, T], at=o); o += nb
    kT, _, nb = A.alloc(BF16, [TH], at=o); o += nb
    kcT, _, nb = A.alloc(BF16, [CTX], at=o); o += nb
    vaug, _, nb = A.alloc(BF16, [NTH, 2, 66], at=o); o += nb
    vcaug, _, nb = A.alloc(BF16, [2, 2, 66], at=o); o += nb
    guT, _, nb = A.alloc(BF16, [4, T], at=o); o += nb
    vn, _, nb = A.alloc(BF16, [NT, 512], at=o); o += nb
    assert o <= RB + RB_SIZE, (o - RB)
    xmid, _, _ = A.alloc(F32, [NT, D], at=RB)
    h2T, _, _ = A.alloc(BF16, [8, T], at=RA)
    hcT_off = None
    yaT, _, nb1 = A.alloc(BF16, [4, T], at=RC)
    ybT, _, nb2 = A.alloc(BF16, [4, T], at=RC + nb1)
    merged, _, _ = A.alloc(BF16, [8, T], at=RD)

    cdeps = []

    def sp_load(dst, src, key='const'):
        return S.dma('sp', lambda e, dst=dst, src=src: e.dma_start(out=dst, in_=src), key)

    def pool_cast_load(dst, src, key, deps=()):
        return S.dma('pool', lambda e, dst=dst, src=src: e.dma_start(out=dst, in_=src, max_dma_last_dim=4096), key, deps)

    cvec_f, _, _ = A.alloc(F32, [16], at=RE)
    badac, _, _ = A.alloc(F32, [64], at=RE + 64)
    t_cvec = sp_load(cvec_f, cvec, 'cvec')
    t_badac = sp_load(badac, badac_d, 'badac')
    t_c = [sp_load(esink, esink_d), sp_load(bsbc, bsbc_d), ]
    t_cb = [pool_cast_load(ident, ident_d, 'cb'), pool_cast_load(perm, perm_d, 'cb'),
            pool_cast_load(masks.rearrange('p a b -> p (a b)'), masks_d, 'cb')]
    t_m1 = S.op('pool', lambda e: e.memset(onesf, 1.0))
    t_m2 = S.op('pool', lambda e: e.memset(neghalf, -0.5))
    t_m3 = S.op('pool', lambda e: e.memset(epst, EPS))
    T_CONST = [t_c[-1], t_cb[-1], t_m1, t_m2]

    o = RC
    wqk, _, nb = A.alloc(BF16, [8, 640], at=o); o += nb
    wu, _, nb = A.alloc(BF16, [8, 512], at=o); o += nb
    wvvb, _, nb = A.alloc(BF16, [8, 640], at=o); o += nb
    hcT, _, nb = A.alloc(BF16, [8, CTX], at=o); o += nb
    scb, _, nb = A.alloc(BF16, [8, 128], at=o); o += nb
    bslice, _, nb = A.alloc(F32, [512], at=o); o += nb
    assert o <= RC + RC_SIZE, (o - RC)

    o = RB
    wring0 = []
    for i in range(4):
        w_, _, nb = A.alloc(BF16, [8, 512], at=o); o += nb
        wring0.append(w_)
    assert o <= RB + RB_SIZE, o
    g1bc, _, nb = A.alloc(F32, [1024], at=RE + 384)
    RE_FREE = RE + 384 + nb
    wring1 = []
    o = RE_FREE + 8192
    for i in range(2):
        w_, _, nb = A.alloc(BF16, [8, 512], at=o); o += nb
        wring1.append(w_)
    assert o <= ARENA_BYTES, (o, ARENA_BYTES)

    t_sc = S.op('act', lambda e: e.activation(out=sc, in_=cvec_f, func=AF.Silu), deps=[t_cvec])
    t_scb = S.op('dve', lambda e: e.tensor_copy(out=scb, in_=sc[:, 0:8].unsqueeze(2).to_broadcast([128, 8, 128])),
                 deps=[t_sc])
    colkind = {0: 0, 1: 0, 2: 1, 3: 1, 6: 2, 7: 2, 8: 3, 9: 3}
    bcform = {4: (0, 0), 5: (0, 1), 10: (1, 0), 11: (1, 1)}
    t_g = {}
    modc4 = modc.rearrange('p (a b c) -> p a b c', a=4, b=8)

    def p0_col(blk, wb_, t_w, pm):
        kind = colkind[blk]
        t_l = None
        for s in range(4):
            chunk = (blk % 2) * 4 + s
            col = (kind * 8 + chunk) * 2
            for k in range(8):
                t_l = S.op('pe', lambda e, k=k, s=s, col=col, wb_=wb_, pm=pm: e.matmul(
                    pm[:, col:col + 2], lhsT=wb_[:, k, s * 128:(s + 1) * 128], rhs=sc[:, k:16:8],
                    start=(k == 0), stop=(k == 7)), deps=[t_w, t_sc] if k == 0 else (),
                    sig=(k == 7 and s == 3))
        return t_l

    def modc_evac(lo, hi, plus1, pm, dep):
        t0_ = S.op('dve', lambda e: e.tensor_tensor(out=modc[:, lo:hi], in0=pm[:, lo:hi], in1=badac[:, lo:hi], op=ALU.add),
                   deps=[dep, t_badac])
        if plus1:
            t0_ = S.op('dve', lambda e: e.tensor_scalar(out=modc[:, lo:hi], in0=modc[:, lo:hi], scalar1=1.0,
                                                        scalar2=None, op0=ALU.add), deps=[t0_])
        return t0_

    t_l = None
    t_modc01_A, t_modc01_B = [], []
    for blk in (0, 2, 1, 3):
        t_w = pool_cast_load(wring0[blk].rearrange('p a b -> p (a b)'), wada[blk], 'wada_e%d' % blk)
        pm_ = bk(0) if blk in (0, 2) else bk(1)
        t_l = p0_col(blk, wring0[blk], t_w, pm_)
        if blk == 2:
            t_modc01_A = [modc_evac(0, 8, False, bk(0), t_l), modc_evac(16, 24, True, bk(0), t_l)]
        if blk == 3:
            t_modc01_B = [modc_evac(8, 16, False, bk(1), t_l), modc_evac(24, 32, True, bk(1), t_l)]
    t_modc01 = t_modc01_A + t_modc01_B
    t_wqk = pool_cast_load(wqk.rearrange('p a b -> p (a b)'), wqk_d, 'wqk')
    t_wvvb = pool_cast_load(wvvb.rearrange('p a b -> p (a b)'), wvvb_d, 'wvvb')
    t_wu = pool_cast_load(wu.rearrange('p a b -> p (a b)'), wu_d, 'wu')

    def mcol(kind, chunk, which):
        return modc4[:, kind, chunk, which:which + 1]

    def stats_part(src, slot, deps, n=1024):
        st = stt[:, slot]
        mv = mvr[:, slot]
        if n == 1024:
            S.op('dve', lambda e: e.bn_stats(out=st[:, 0:6], in_=src[:, 0:512]), deps=[deps, st_free[slot]], sig=False)
            t1 = S.op('dve', lambda e: e.bn_stats(out=st[:, 6:12], in_=src[:, 512:1024]))
            t2 = S.op('dve', lambda e: e.bn_aggr(out=mv[:, 0:2], in_=st[:, 0:12]), deps=[t1])
        else:
            t1 = S.op('dve', lambda e: e.bn_stats(out=st[:, 0:6], in_=src), deps=[deps, st_free[slot]])
            t2 = S.op('dve', lambda e: e.bn_aggr(out=mv[:, 0:2], in_=st[:, 0:6]), deps=[t1])
        return S.op('act', lambda e: e.activation(out=mv[:, 4:5], in_=mv[:, 1:2], func=AF.Sqrt, bias=epst[:, 0:1], scale=1.0),
                    deps=[t2, t_m3])

    def rstd_part(slot, deps):
        mv = mvr[:, slot]
        t4 = S.op('dve', lambda e: e.reciprocal(out=mv[:, 2:3], in_=mv[:, 4:5]), deps=[deps])
        t5 = S.op('dve', lambda e: e.scalar_tensor_tensor(out=mv[:, 3:4], in0=mv[:, 0:1], scalar=-1.0, in1=mv[:, 2:3],
                                                          op0=ALU.mult, op1=ALU.mult), deps=[t4])
        return mv[:, 0:1], mv[:, 2:3], mv[:, 3:4], t5

    o = RD
    NXR = 6
    xr = []
    for i in range(NXR):
        x_, _, nb = A.alloc(F32, [D], at=o); o += nb
        xr.append(x_)
    xnb = []
    for i in range(2):
        x_, _, nb = A.alloc(BF16, [D], at=o); o += nb
        xnb.append(x_)
    assert o <= RD + RD_SIZE, (o - RD)

    xr_free = [None] * NXR
    xnb_free = [None, None]
    bankA_free = [None, None]
    bankB_free = [None, None]
    t_hT = [None] * NTH
    t_hcT = [None] * 2
    tiles = [('x', i) for i in range(NTH)] + [('c', i) for i in range(2)]
    NA1 = len(tiles)
    a1 = {}
    for it in range(NA1 + 2):
        if it < NA1:
            n = it
            kindt, ti = tiles[n]
            xs = n % NXR
            src = xh[ti * 128:(ti + 1) * 128, :] if kindt == 'x' else ctxb[ti * 128:(ti + 1) * 128, :]
            t_ld = S.dma('sp', lambda e, dst=xr[xs], src=src: e.dma_start(out=dst, in_=src), 'xr%d' % xs,
                         deps=[xr_free[xs]])
            a1[n] = {'sq': stats_part(xr[xs], n % 4, [t_ld])}
        if 0 <= it - 1 < NA1:
            n = it - 1
            kindt, ti = tiles[n]
            xs = n % NXR
            bs_ = n % 2
            mean, rstd, nmr, t_r = rstd_part(n % 4, [a1[n]['sq']])
            t_xn = S.op('act', lambda e, dst=xnb[bs_], src=xr[xs], rstd=rstd, nmr=nmr: e.activation(
                out=dst, in_=src, func=AF.Identity, scale=rstd, bias=nmr), deps=[t_r, xnb_free[bs_]])
            xr_free[xs] = t_xn
            st_free[n % 4] = t_xn
            ptA = bkb(4 + 2 * bs_)
            ptB = bkb(5 + 2 * bs_)
            for c in range(8):
                pt = ptA if c < 4 else ptB
                t_tr = S.op('pe', lambda e, c=c, pt=pt, src=xnb[bs_]: e.transpose(
                    pt[:, (c % 4) * 128:(c % 4 + 1) * 128], src[:, c * 128:(c + 1) * 128], ident),
                    deps=[t_xn, bankA_free[bs_], bankB_free[bs_], T_CONST] if c == 0 else (), sig=(c == 3 or c == 7))
                if c == 3:
                    a1[n]['trA'] = t_tr
            a1[n]['trB'] = t_tr
            xnb_free[bs_] = t_tr
        if 0 <= it - 2 < NA1:
            n = it - 2
            kindt, ti = tiles[n]
            bs_ = n % 2
            ptA = bkb(4 + 2 * bs_)
            ptB = bkb(5 + 2 * bs_)
            which = 0 if kindt == 'x' else 1
            for c in range(8):
                dst = hT[:, c, ti * 128:(ti + 1) * 128] if kindt == 'x' else hcT[:, c, ti * 128:(ti + 1) * 128]
                if c < 4:
                    t_evA = S.op('act', lambda e, c=c, dst=dst, ptA=ptA, which=which: e.activation(
                        out=dst, in_=ptA[:, c * 128:(c + 1) * 128], func=AF.Identity,
                        scale=mcol(1, c, which), bias=mcol(0, c, which)),
                        deps=[a1[n]['trA'], t_modc01_A] if c == 0 else (), sig=(c == 3))
                else:
                    t_evB = S.op('dve', lambda e, c=c, dst=dst, ptB=ptB, which=which: e.tensor_scalar(
                        out=dst, in0=ptB[:, (c - 4) * 128:(c - 3) * 128], scalar1=mcol(1, c, which),
                        scalar2=mcol(0, c, which), op0=ALU.mult, op1=ALU.add),
                        deps=[a1[n]['trB'], t_modc01_B] if c == 4 else (), sig=(c == 7))
            bankA_free[bs_] = t_evA
            bankB_free[bs_] = t_evB
            if kindt == 'x':
                t_hT[ti] = [t_evA, t_evB]
            else:
                t_hcT[ti] = [t_evA, t_evB]
    if debug_stop == 'A1':
        dbg['hT'] = (hT.rearrange('p a b -> p (a b)'), [128, 8 * TH], BF16)
        dbg['hcT'] = (hcT.rearrange('p a b -> p (a b)'), [128, 8 * CTX], BF16)
        return finish(nc, es, S, dbg, t_hT + t_hcT)

    S.barrier()
    o = RD
    ropec, _, nb = A.alloc(F32, [TH], at=o); o += nb
    ropes, _, nb = A.alloc(F32, [TH], at=o); o += nb
    glg, _, nb = A.alloc(F32, [512], at=o); o += nb
    glb, _, nb = A.alloc(F32, [512], at=o); o += nb
    qraw = []
    for i in range(2):
        q_, _, nb = A.alloc(BF16, [512], at=o); o += nb
        qraw.append(q_)
    rt1 = []
    rt2 = []
    for i in range(2):
        q_, _, nb = A.alloc(F32, [512], at=o); o += nb
        rt1.append(q_)
        q_, _, nb = A.alloc(F32, [512], at=o); o += nb
        rt2.append(q_)
    assert o <= RD + RD_SIZE, (o - RD)
    o = RE_FREE
    gvb = []
    for i in range(4):
        q_, _, nb = A.alloc(F32, [512], at=o); o += nb
        gvb.append(q_)
    assert o <= RE_FREE + 8192, (o, RE_FREE)
    t_rope = [sp_load(ropec, ropec_d, 'rope'), sp_load(ropes, ropes_d, 'rope')][-1]
    t_gl = [sp_load(glg, glg_d, 'gl'), sp_load(glb, glb_d, 'gl')][-1]
    t_vones = S.op('pool', lambda e: e.memset(vaug[:, :, :, 64:66], 1.0))
    t_vcones = S.op('pool', lambda e: e.memset(vcaug[:, :, :, 64:66], 1.0))

    fb_free = [None] * 8
    qraw_free = [None, None]
    rt_free = [None, None]

    dq = {'blocks': [4, 5, 6, 7, 8, 9, 10, 11], 'issued': [], 'n_issued': 0, 'n_done': 0, 'bs_free': None, 'pb': 0}
    ring1_free = [None, None]
    t_modc23 = []

    def p0_issue():
        if dq['n_issued'] >= len(dq['blocks']):
            return
        blk = dq['blocks'][dq['n_issued']]
        slot = dq['n_issued'] % 2
        dq['n_issued'] += 1
        t_w = pool_cast_load(wring1[slot].rearrange('p a b -> p (a b)'), wada[blk], 'wada_d%d' % slot,
                             deps=[ring1_free[slot]])
        dq['issued'].append((blk, slot, t_w))

    def p0_compute(col_bank=6, bc_bank=7):
        if dq['n_done'] >= len(dq['blocks']):
            return
        blk, slot, t_w = dq['issued'][dq['n_done']]
        dq['n_done'] += 1
        wb_ = wring1[slot]
        if blk in colkind:
            pm = bk(col_bank)
            kind = colkind[blk]
            t_l = None
            for s in range(4):
                chunk = (blk % 2) * 4 + s
                col = (kind * 8 + chunk) * 2
                for k in range(8):
                    t_l = S.op('pe', lambda e, k=k, s=s, col=col, wb_=wb_, pm=pm: e.matmul(
                        pm[:, col:col + 2], lhsT=wb_[:, k, s * 128:(s + 1) * 128], rhs=sc[:, k:16:8],
                        start=(k == 0), stop=(k == 7)), deps=[t_w, t_sc, fb_free[col_bank]] if k == 0 else (),
                        sig=(k == 7 and s == 3))
            ring1_free[slot] = t_l
            lo = (kind * 8 + (blk % 2) * 4) * 2
            t_e = modc_evac(lo, lo + 8, kind in (1, 3), pm, t_l)
            fb_free[col_bank] = t_e
            t_modc23.append(t_e)
        else:
            which, half = bcform[blk]
            bank = bc_bank
            pb = bk(bank)
            gi_ = which * 2 + half
            t_bs = S.dma('sp', lambda e, gi_=gi_: e.dma_start(out=bslice, in_=badabc_d[:, gi_ * 512:(gi_ + 1) * 512]),
                         'bslice', deps=[dq['bs_free']])
            for k in range(8):
                t_mm = S.op('pe', lambda e, k=k, pb=pb, wb_=wb_: e.matmul(
                    pb, lhsT=scb[:, k, :], rhs=wb_[:, k, :], start=(k == 0), stop=(k == 7)),
                    deps=[t_w, t_scb, fb_free[bank]] if k == 0 else (), sig=(k == 7))
            ring1_free[slot] = t_mm
            dst = (g1bc if which == 0 else g2bc)[:, half * 512:(half + 1) * 512]
            t_e = S.op('dve', lambda e, dst=dst, pb=pb: e.tensor_tensor(out=dst, in0=pb, in1=bslice, op=ALU.add),
                       deps=[t_mm, t_bs])
            dq['bs_free'] = t_e
            fb_free[bank] = t_e
            t_g[(which, half)] = t_e
        p0_issue()

    p0_issue()
    p0_issue()
    units = []
    for m in range(4):
        for tg in range(4):
            units.append(('q', m, 128 + tg * 512, 512, tg * 512))
    for tg in range(5):
        n_ = 512 if tg < 4 else 256
        units.append(('k', 4, tg * 512, n_, tg * 512))
    t_q = {}
    t_qk_all = []
    qk = [dict() for _ in units]

    def qk_part1(ui):
        kd, m, hcol, n_, ocol = units[ui]
        b0 = (ui % 3) * 2
        pq = bk(b0)[:, 0:n_]
        for k in range(8):
            t_mm = S.op('pe', lambda e, k=k, pq=pq, m=m, hcol=hcol, n_=n_: e.matmul(
                pq, lhsT=wqk[:, k, m * 128:(m + 1) * 128], rhs=hT[:, k, hcol:hcol + n_],
                start=(k == 0), stop=(k == 7)),
                deps=[t_wqk, fb_free[b0], fb_free[b0 + 1]] if k == 0 else (), sig=(k == 7))
        r = ui % 2
        t_raw = S.op('act', lambda e, dst=qraw[r][:, 0:n_], pq=pq: e.activation(out=dst, in_=pq, func=AF.Copy),
                     deps=[t_mm, qraw_free[r]])
        qk[ui]['mm'] = t_mm
        qk[ui]['raw'] = t_raw

    def qk_part2(ui):
        kd, m, hcol, n_, ocol = units[ui]
        b0 = (ui % 3) * 2
        pq, ps_ = bk(b0)[:, 0:n_], bk(b0 + 1)[:, 0:n_]
        r = ui % 2
        t_mm, t_raw = qk[ui]['mm'], qk[ui]['raw']
        t_pm = S.op('pe', lambda e, ps_=ps_, src=qraw[r][:, 0:n_]: e.matmul(ps_, lhsT=perm, rhs=src, start=True, stop=True),
                    deps=[t_raw, T_CONST])
        qraw_free[r] = t_pm
        t_1 = S.op('dve', lambda e, dst=rt1[r][:, 0:n_], pq=pq, hcol=hcol, n_=n_: e.tensor_tensor(
            out=dst, in0=pq, in1=ropec[:, hcol:hcol + n_], op=ALU.mult), deps=[t_mm, t_raw, t_rope, rt_free[r]])
        t_2 = S.op('dve', lambda e, dst=rt2[r][:, 0:n_], ps_=ps_, hcol=hcol, n_=n_: e.tensor_tensor(
            out=dst, in0=ps_, in1=ropes[:, hcol:hcol + n_], op=ALU.mult), deps=[t_pm])
        fb_free[b0] = t_2
        fb_free[b0 + 1] = t_2
        dst = qT[:, m, ocol:ocol + n_] if kd == 'q' else kT[:, ocol:ocol + n_]
        t_3 = S.op('pool', lambda e, dst=dst, a=rt1[r][:, 0:n_], b=rt2[r][:, 0:n_]: e.tensor_tensor(
            out=dst, in0=a, in1=b, op=ALU.add), deps=[t_1, t_2])
        rt_free[r] = t_3
        t_qk_all.append(t_3)

    for ui in range(len(units)):
        qk_part1(ui)
        if ui >= 1:
            qk_part2(ui - 1)
        if ui in (6, 13, 20):
            p0_compute()
    qk_part2(len(units) - 1)
    if debug_stop == 'A2a':
        dbg['qT'] = (qT.rearrange('p a b -> p (a b)'), [128, 4 * T], BF16)
        dbg['kT'] = (kT, [128, TH], BF16)
        return finish(nc, es, S, dbg, t_qk_all)
    pq = bk(0)[:, 0:CTX]
    for k in range(8):
        t_mm = S.op('pe', lambda e, k=k, pq=pq: e.matmul(pq, lhsT=wqk[:, k, 512:640], rhs=hcT[:, k, :],
                                                         start=(k == 0), stop=(k == 7)),
                    deps=[fb_free[0], t_hcT] if k == 0 else (), sig=(k == 7))
    t_kc = S.op('act', lambda e, pq=pq: e.activation(out=kcT, in_=pq, func=AF.Copy), deps=[t_mm])
    fb_free[0] = t_kc
    t_gu = []
    ui = 0
    for m in range(4):
        for tg in range(4):
            b0 = 1 + (ui % 3); ui += 1
            pq = bk(b0)
            for k in range(8):
                t_mm = S.op('pe', lambda e, k=k, pq=pq, m=m, tg=tg: e.matmul(
                    pq, lhsT=wu[:, k, m * 128:(m + 1) * 128], rhs=hT[:, k, 128 + tg * 512:128 + (tg + 1) * 512],
                    start=(k == 0), stop=(k == 7)), deps=[t_wu, fb_free[b0]] if k == 0 else (), sig=(k == 7))
            t_e = S.op('act', lambda e, pq=pq, m=m, tg=tg: e.activation(
                out=guT[:, m, tg * 512:(tg + 1) * 512], in_=pq, func=AF.Gelu_apprx_tanh), deps=[t_mm])
            fb_free[b0] = t_e
            t_gu.append(t_e)
            if ui in (5, 10, 15):
                p0_compute()
    if debug_stop == 'A2b':
        dbg['kcT'] = (kcT, [128, CTX], BF16)
        dbg['guT'] = (guT.rearrange('p a b -> p (a b)'), [128, 4 * T], BF16)
        return finish(nc, es, S, dbg, [t_kc] + t_gu)
    gvb_free = [None] * 4
    t_v = [None] * NTH
    t_vc = [None] * 2
    t_vn = [None] * NT
    vb_ = [dict() for _ in tiles]

    def is_main(n):
        kindt, ti = tiles[n]
        return kindt == 'x' and 1 <= ti <= NT

    def v_s0(n):
        kindt, ti = tiles[n]
        bv = 4 + (n % 2)
        pv = bk(bv)[:, 0:128]
        for k in range(8):
            lh = hT[:, k, ti * 128:(ti + 1) * 128] if kindt == 'x' else hcT[:, k, ti * 128:(ti + 1) * 128]
            t_mm = S.op('pe', lambda e, k=k, pv=pv, lh=lh: e.matmul(pv, lhsT=lh, rhs=wvvb[:, k, 0:128],
                                                                   start=(k == 0), stop=(k == 7)),
                        deps=[t_wvvb, fb_free[bv]] if k == 0 else (), sig=(k == 7))
        vb_[n]['v'] = t_mm
        if is_main(n):
            mt = ti - 1
            bvb = 6 + (mt % 2)
            pvb = bk(bvb)
            for k in range(8):
                t_mm = S.op('pe', lambda e, k=k, pvb=pvb, ti=ti: e.matmul(
                    pvb, lhsT=hT[:, k, ti * 128:(ti + 1) * 128], rhs=wvvb[:, k, 128:640],
                    start=(k == 0), stop=(k == 7)), deps=[fb_free[bvb]] if k == 0 else (), sig=(k == 7))
            vb_[n]['vb'] = t_mm

    def v_s1(n):
        kindt, ti = tiles[n]
        bv = 4 + (n % 2)
        pv = bk(bv)[:, 0:128]
        dstv = (vaug[:, ti, :, 0:64] if kindt == 'x' else vcaug[:, ti, :, 0:64])
        t_e = S.op('act', lambda e, dstv=dstv, pv=pv: e.activation(
            out=dstv, in_=pv.rearrange('p (a b) -> p a b', a=2), func=AF.Copy), deps=[vb_[n]['v'], t_vones, t_vcones])
        fb_free[bv] = t_e
        if kindt == 'x':
            t_v[ti] = t_e
        else:
            t_vc[ti] = t_e
        if is_main(n):
            mt = ti - 1
            bvb = 6 + (mt % 2)
            g_ = gvb[mt % 4]
            t_ge = S.op('act', lambda e, g_=g_, pvb=bk(bvb): e.activation(out=g_, in_=pvb, func=AF.Gelu_apprx_tanh),
                        deps=[vb_[n]['vb'], gvb_free[mt % 4]])
            fb_free[bvb] = t_ge
            vb_[n]['ge'] = t_ge

    def v_s2(n):
        if not is_main(n):
            return
        mt = tiles[n][1] - 1
        sl = mt % 4
        st = stt[:, sl]; mv = mvr[:, sl]
        g_ = gvb[mt % 4]
        t1 = S.op('dve', lambda e: e.bn_stats(out=st[:, 0:6], in_=g_), deps=[vb_[n]['ge'], st_free[sl]])
        vb_[n]['ag'] = S.op('dve', lambda e: e.bn_aggr(out=mv[:, 0:2], in_=st[:, 0:6]), deps=[t1])

    def v_s3(n):
        if not is_main(n):
            return
        mt = tiles[n][1] - 1
        mv = mvr[:, mt % 4]
        t3 = S.op('pool', lambda e: e.tensor_scalar(out=mv[:, 4:5], in0=mv[:, 1:2], scalar1=EPS, scalar2=None,
                                                    op0=ALU.add), deps=[vb_[n]['ag']])
        vb_[n]['rs'] = S.op('pool', lambda e: e.tensor_tensor(out=mv[:, 2:3], in0=mv[:, 4:5], in1=neghalf[:, 0:1],
                                                              op=ALU.pow), deps=[t3, t_m2])

    def v_s4(n):
        if not is_main(n):
            return
        mt = tiles[n][1] - 1
        mv = mvr[:, mt % 4]
        g_ = gvb[mt % 4]
        tmp_ = rt1[mt % 2]
        t5 = S.op('dve', lambda e: e.scalar_tensor_tensor(
            out=tmp_, in0=g_, scalar=mv[:, 0:1], in1=glg, op0=ALU.subtract, op1=ALU.mult),
            deps=[vb_[n]['rs'], t_gl, rt_free[mt % 2]])
        t6 = S.op('dve', lambda e: e.scalar_tensor_tensor(
            out=vn[:, mt, :], in0=tmp_, scalar=mv[:, 2:3], in1=glb, op0=ALU.mult, op1=ALU.add), deps=[t5])
        rt_free[mt % 2] = t6
        gvb_free[mt % 4] = t6
        st_free[mt % 4] = t6
        t_vn[mt] = t6

    def v_hook(it):
        if it in (5, 12):
            p0_compute(col_bank=0, bc_bank=1)

    run_pipeline([v_s0, v_s1, v_s2, v_s3, v_s4], len(tiles), hook=v_hook)
    while dq['n_done'] < len(dq['blocks']):
        p0_compute(col_bank=0, bc_bank=1)
    t_modc = t_modc01 + t_modc23
    t_g1 = [t_g[(0, 0)], t_g[(0, 1)]]
    t_g2 = [t_g[(1, 0)], t_g[(1, 1)]]

    if debug_stop == 'A2':
        dbg['qT'] = (qT.rearrange('p a b -> p (a b)'), [128, 4 * T], BF16)
        dbg['kT'] = (kT, [128, TH], BF16)
        dbg['kcT'] = (kcT, [128, CTX], BF16)
        dbg['vaug'] = (vaug.rearrange('p a b c -> p (a b c)'), [128, NTH * 2 * 66], BF16)
        dbg['vcaug'] = (vcaug.rearrange('p a b c -> p (a b c)'), [128, 2 * 2 * 66], BF16)
        dbg['guT'] = (guT.rearrange('p a b -> p (a b)'), [128, 4 * T], BF16)
        dbg['vn'] = (vn.rearrange('p a b -> p (a b)'), [128, NT * 512], BF16)
        return finish(nc, es, S, dbg, t_qk_all + [t_kc] + t_gu + t_v + t_vc + t_vn)

    S.barrier()
    o = RD
    wst, _, nb = A.alloc(BF16, [8, 128], at=o); o += nb
    NPT = 20
    PT = []
    for i in range(NPT):
        p_, _, nb = A.alloc(BF16, [4, 128], at=o); o += nb
        PT.append(p_)
    yatm = []
    for i in range(2):
        p_, _, nb = A.alloc(BF16, [512], at=o); o += nb
        yatm.append(p_)
    sbt = []
    for i in range(2):
        p_, _, nb = A.alloc(F32, [512], at=o); o += nb
        sbt.append(p_)
    dens, _, nb = A.alloc(F32, [2, 16], at=o); o += nb
    esk, _, nb = A.alloc(F32, [8], at=o); o += nb
    wab0, _, nb = A.alloc(BF16, [8, 128], at=o); o += nb
    assert o <= RD + RD_SIZE, (o - RD)
    wg0, _, _ = A.alloc(BF16, [16, 128], at=RC + 32 * 1024)
    t_wc1_0 = [pool_cast_load(wab0.rearrange('p a b -> p (a b)'), wc1_d[0][:, 0:1024], 'wc1_0a'),
               pool_cast_load(wg0.rearrange('p a b -> p (a b)'), wc1_d[0][:, 1024:3072], 'wc1_0b')]
    o = RE_FREE
    woutb, _, nb = A.alloc(BF16, [8, 1024], at=o); o += nb
    wstage = []
    for i in range(2):
        p_, _, nb = A.alloc(F32, [1024], at=o); o += nb
        wstage.append(p_)
    assert o <= ARENA_BYTES, (o, ARENA_BYTES)

    t_wst = pool_cast_load(wst.rearrange('p a b -> p (a b)'), wst_d, 'wst')
    t_esk = S.op('act', lambda e: e.activation(out=esk, in_=esink, func=AF.Exp), deps=[T_CONST])
    ws_free = [None, None]
    wo = {'t': None}

    def wout_step(k):
        r = k % 2
        t_l = S.dma('sp', lambda e, dst=wstage[r], k=k: e.dma_start(out=dst, in_=wout_d[:, k * 1024:(k + 1) * 1024]),
                    'wos%d' % r, deps=[ws_free[r]])
        wo['t'] = S.op('pool', lambda e, k=k, src=wstage[r]: e.tensor_tensor(out=woutb[:, k, :], in0=src, in1=g1bc, op=ALU.mult),
                       deps=[t_l] + t_g1)
        ws_free[r] = wo['t']

    fb_free = [None] * 8
    sbt_free = [None, None]
    t_yb = [None] * NT
    sc_st = {'i': 0}

    def gmlp_tile(t):
        b0 = sc_st['i'] % 4; sc_st['i'] += 1
        pf = bk(b0)
        for c in range(4):
            for gi in range(2):
                g = 2 * c + gi
                outp = pf[gi * 64:(gi + 1) * 64, c * 128:(c + 1) * 128]
                t_mm = S.op('pe', lambda e, outp=outp, g=g, t=t: e.matmul(
                    outp, lhsT=vn[:, t, g * 64:(g + 1) * 64], rhs=wst[:, g, :], start=True, stop=True),
                    deps=[t_wst, fb_free[b0], t_vn[t]] if (c == 0 and gi == 0) else (), sig=(c == 3 and gi == 1))
        r = t % 2
        t_sb = S.op('dve', lambda e, pf=pf, dst=sbt[r]: e.tensor_tensor(out=dst, in0=pf, in1=bsbc, op=ALU.add),
                    deps=[t_mm, T_CONST, sbt_free[r]])
        fb_free[b0] = t_sb
        t_yb[t] = S.op('dve', lambda e, src=sbt[r], t=t: e.tensor_tensor(
            out=ybT[:, :, t * 128:(t + 1) * 128], in0=src.rearrange('p (a b) -> p a b', a=4),
            in1=guT[:, :, t * 128:(t + 1) * 128], op=ALU.mult), deps=[t_sb])
        sbt_free[r] = t_yb[t]

    pt_free = [None] * NPT
    yatm_free = [None, None]
    o_free = [None] * 4
    t_ya = [None] * NT
    att = [dict() for _ in range(NT)]

    def att_srcs(j):
        srcs = []
        for s in range(3):
            kt = j + s
            srcs.append((kT[:, kt * 128:(kt + 1) * 128], vaug[:, kt], [t_v[kt]]))
        for s in range(2):
            srcs.append((kcT[:, s * 128:(s + 1) * 128], vcaug[:, s], [t_vc[s]]))
        return srcs

    def att_front(j):
        base = (j % 2) * 10
        srcs = att_srcs(j)
        t_p = {}
        att[j]['p'] = t_p
        for s, (ksrc, vsrc, vdep) in enumerate(srcs):
            for kv in range(2):
                bi = sc_st['i'] % 4; sc_st['i'] += 1
                psb = bk(bi)
                pt = PT[base + s * 2 + kv]
                t_mm = S.op('pe', lambda e, psb=psb, ksrc=ksrc, kv=kv, j=j: e.matmul(
                    psb, lhsT=ksrc[kv * 64:(kv + 1) * 64, :], rhs=qT[kv * 64:(kv + 1) * 64, :, j * 128:(j + 1) * 128],
                    start=True, stop=True), deps=[fb_free[bi]])
                t_e = S.op('act', lambda e, pt=pt, psb=psb: e.activation(
                    out=pt.rearrange('p a b -> p (a b)'), in_=psb, func=AF.Exp, scale=0.125),
                    deps=[t_mm, pt_free[base + s * 2 + kv]])
                fb_free[bi] = t_e
                if s == 0 or s == 2:
                    mi = (0 if j == 0 else 1) if s == 0 else (3 if j == NT - 1 else 2)
                    t_e = S.op('pool', lambda e, pt=pt, mi=mi: e.tensor_tensor(
                        out=pt, in0=pt, in1=masks[:, mi:mi + 1, :].to_broadcast([128, 4, 128]), op=ALU.mult),
                        deps=[t_e, T_CONST])
                t_p[(s, kv)] = t_e

    def att_back(j):
        base = (j % 2) * 10
        srcs = att_srcs(j)
        t_p = att[j]['p']
        t_o = [None, None]
        for kv in range(2):
            ob = bk(4 + 2 * (j % 2) + kv).rearrange('p (a b) -> p a b', a=4)
            for g in range(4):
                for s, (ksrc, vsrc, vdep) in enumerate(srcs):
                    pt = PT[base + s * 2 + kv]
                    t_mm = S.op('pe', lambda e, ob=ob, g=g, pt=pt, vsrc=vsrc, kv=kv, s=s: e.matmul(
                        ob[:, g, 0:65], lhsT=pt[:, g, :], rhs=vsrc[:, kv, 0:65], start=(s == 0), stop=(s == 4)),
                        deps=[t_p[(s, kv)], o_free[2 * (j % 2) + kv]] + vdep, sig=(g == 3 and s == 4))
            t_o[kv] = t_mm
        for s in range(5):
            for kv in range(2):
                pt_free[base + s * 2 + kv] = t_o[kv]
        r = j % 2
        dn = dens[:, r]
        ya = yatm[r].rearrange('p (k g d) -> p k g d', k=2, g=4)
        t_n = None
        for kv in range(2):
            ob = bk(4 + 2 * (j % 2) + kv).rearrange('p (a b) -> p a b', a=4)
            t_a = S.op('dve', lambda e, ob=ob, dn=dn, kv=kv: e.tensor_tensor(
                out=dn[:, kv * 4:(kv + 1) * 4], in0=ob[:, :, 64], in1=esk[:, kv * 4:(kv + 1) * 4], op=ALU.add),
                deps=[t_o[kv], t_esk, yatm_free[r]])
            t_b = S.op('dve', lambda e, dn=dn, kv=kv: e.reciprocal(out=dn[:, 8 + kv * 4:8 + (kv + 1) * 4],
                                                                  in_=dn[:, kv * 4:(kv + 1) * 4]), deps=[t_a])
            t_n = S.op('dve', lambda e, ob=ob, dn=dn, kv=kv, ya=ya: e.tensor_tensor(
                out=ya[:, kv], in0=ob[:, :, 0:64],
                in1=dn[:, 8 + kv * 4:8 + (kv + 1) * 4].unsqueeze(2).to_broadcast([128, 4, 64]), op=ALU.mult),
                deps=[t_b])
            o_free[2 * (j % 2) + kv] = t_n
        att[j]['n'] = t_n

    def att_tail(j):
        r = j % 2
        t_n = att[j]['n']
        ptb = bkb(4 + 2 * (j % 2))
        for c in range(4):
            t_tr = S.op('pe', lambda e, c=c, ptb=ptb, src=yatm[r]: e.transpose(
                ptb[:, c * 128:(c + 1) * 128], src[:, c * 128:(c + 1) * 128], ident),
                deps=[t_n] if c == 0 else (), sig=(c == 3))
        yatm_free[r] = t_tr
        t_ya[j] = S.op('dve', lambda e, ptb=ptb, j=j: e.tensor_copy(
            out=yaT[:, :, j * 128:(j + 1) * 128], in_=ptb[:, 0:512].rearrange('p (a b) -> p a b', a=4)),
            deps=[t_tr])
        o_free[2 * (j % 2)] = t_ya[j]

    att_front(0)
    for j in range(NT + 1):
        if j + 1 < NT:
            att_front(j + 1)
        if j < NT:
            gmlp_tile(j)
            att_back(j)
            if 2 <= j < 10:
                wout_step(j - 2)
        if j >= 1:
            att_tail(j - 1)

    t_wout = wo['t']
    if debug_stop == 'B':
        dbg['yaT'] = (yaT.rearrange('p a b -> p (a b)'), [128, 4 * T], BF16)
        dbg['ybT'] = (ybT.rearrange('p a b -> p (a b)'), [128, 4 * T], BF16)
        return finish(nc, es, S, dbg, t_ya + t_yb + [t_wout])

    S.barrier()
    o = RB
    wc1 = []
    t_wc1 = [None] * 8
    for m in range(8):
        p_, _, nb = A.alloc(BF16, [24, 128], at=o); o += nb
        wc1.append(p_)
        if m == 0:
            t_wc1[m] = t_wc1_0
        else:
            t_wc1[m] = pool_cast_load(p_.rearrange('p a b -> p (a b)'), wc1_d[m], 'wc1_%d' % m)

    def wsel(m, idx):
        if m == 0:
            return wab0[:, idx, :] if idx < 8 else wg0[:, idx - 8, :]
        return wc1[m][:, idx, :]
    outpre = {}
    sga = []
    for i in range(4):
        p_, _, nb = A.alloc(F32, [512], at=o); o += nb
        sga.append(p_)
    mt1 = []
    for i in range(4):
        p_, _, nb = A.alloc(F32, [512], at=o); o += nb
        mt1.append(p_)
    assert o <= RB + RB_SIZE, (o - RB)
    fb_free = [None] * 8
    sga_free = [None] * 4
    mt_free = [None] * 4
    t_merged = {}
    ui = 0
    for m in range(8):
        if m == 4:
            for t_ in range(NT):
                outpre['t'] = S.dma('sp', lambda e, t_=t_: e.dma_start(out=out_d[t_ * 128:(t_ + 1) * 128, :], in_=ln2b_d),
                                    'outpre', deps=[t_wc1[7]])
        wm = wc1[m]
        t_w = t_wc1[m]
        for tg in range(4):
            bs_ = (ui % 2) * 4
            r2 = (ui % 2) * 2
            ui += 1
            pa, pb_, pga, pgb = bk(bs_), bk(bs_ + 1), bk(bs_ + 2), bk(bs_ + 3)
            tok = slice(tg * 512, (tg + 1) * 512)
            htok = slice(128 + tg * 512, 128 + (tg + 1) * 512)
            for k in range(8):
                t_ga = S.op('pe', lambda e, k=k, pga=pga, wm=wm, m=m, htok=htok: e.matmul(
                    pga, lhsT=wsel(m, 8 + k), rhs=hT[:, k, htok], start=(k == 0), stop=(k == 7)),
                    deps=[t_w, fb_free[bs_ + 2]] if k == 0 else (), sig=(k == 7))
            for k in range(8):
                t_gb = S.op('pe', lambda e, k=k, pgb=pgb, wm=wm, m=m, htok=htok: e.matmul(
                    pgb, lhsT=wsel(m, 16 + k), rhs=hT[:, k, htok], start=(k == 0), stop=(k == 7)),
                    deps=[fb_free[bs_ + 3]] if k == 0 else (), sig=(k == 7))
            for k in range(4):
                t_a = S.op('pe', lambda e, k=k, pa=pa, wm=wm, m=m, tok=tok: e.matmul(
                    pa, lhsT=wsel(m, k), rhs=yaT[:, k, tok], start=(k == 0), stop=(k == 3)),
                    deps=[fb_free[bs_]] if k == 0 else (), sig=(k == 3))
            for k in range(4):
                t_b = S.op('pe', lambda e, k=k, pb_=pb_, wm=wm, m=m, tok=tok: e.matmul(
                    pb_, lhsT=wsel(m, 4 + k), rhs=ybT[:, k, tok], start=(k == 0), stop=(k == 3)),
                    deps=[fb_free[bs_ + 1]] if k == 0 else (), sig=(k == 3))
            t_sa = S.op('act', lambda e, dst=sga[r2], pga=pga: e.activation(out=dst, in_=pga, func=AF.Sigmoid),
                        deps=[t_ga, sga_free[r2]])
            t_sb = S.op('act', lambda e, dst=sga[r2 + 1], pgb=pgb: e.activation(out=dst, in_=pgb, func=AF.Sigmoid),
                        deps=[t_gb, sga_free[r2 + 1]])
            fb_free[bs_ + 2] = t_sa
            fb_free[bs_ + 3] = t_sb
            t_1 = S.op('dve', lambda e, dst=mt1[r2], pa=pa, sa=sga[r2]: e.tensor_tensor(out=dst, in0=pa, in1=sa, op=ALU.mult),
                       deps=[t_a, t_sa, mt_free[r2]])
            t_2 = S.op('dve', lambda e, dst=mt1[r2 + 1], pb_=pb_, sb=sga[r2 + 1]: e.tensor_tensor(out=dst, in0=pb_, in1=sb, op=ALU.mult),
                       deps=[t_b, t_sb, mt_free[r2 + 1]])
            fb_free[bs_] = t_1
            fb_free[bs_ + 1] = t_2
            sga_free[r2] = t_1
            sga_free[r2 + 1] = t_2
            t_3 = S.op('pool', lambda e, m=m, tok=tok, a=mt1[r2], b=mt1[r2 + 1]: e.tensor_tensor(
                out=merged[:, m, tok], in0=a, in1=b, op=ALU.add), deps=[t_1, t_2])
            mt_free[r2] = t_3
            mt_free[r2 + 1] = t_3
            t_merged[(m, tg)] = t_3

    if debug_stop == 'C1':
        dbg['merged'] = (merged.rearrange('p a b -> p (a b)'), [128, 8 * T], BF16)
        return finish(nc, es, S, dbg, list(t_merged.values()) + [t_wout])

    S.barrier()
    o = RC
    NXR2, NWK = 2, 4
    xr = []
    for i in range(NXR2):
        x_, _, nb = A.alloc(F32, [D], at=o); o += nb
        xr.append(x_)
    wk = []
    for i in range(NWK):
        x_, _, nb = A.alloc(F32, [D], at=o); o += nb
        wk.append(x_)
    xnb = []
    for i in range(2):
        x_, _, nb = A.alloc(BF16, [D], at=o); o += nb
        xnb.append(x_)
    assert o <= RC + 28 * 1024, (o - RC)
    NWR = 3
    w1 = [None] * NWR
    w2 = [None] * NWR
    w1[0], _, _ = A.alloc(BF16, [8, 512], at=RC + 28 * 1024)
    t_w1_pre = pool_cast_load(w1[0].rearrange('p a b -> p (a b)'), wff1_d[0], 'wf1_0')
    ln1g, ln1b = wstage[0], wstage[1]
    t_ln1 = [sp_load(ln1g, ln1g_d, 'ln1'), sp_load(ln1b, ln1b_d, 'ln1')][-1]

    fb_free = [None] * 8
    xr_free = [None] * NXR2
    wk_free = [None] * NWK
    xnb_free = [None, None]
    t_h2 = [None] * NT
    t_xmid = [None] * NT
    c2 = [dict() for _ in range(NT)]

    def c2_s0(t):
        r = t % NXR2
        c2[t]['x'] = S.dma('sp', lambda e, dst=xr[r], t=t: e.dma_start(out=dst, in_=xh[(t + 1) * 128:(t + 2) * 128, :]),
                           'xr%d' % r, deps=[xr_free[r]])
        b0 = (t % 2) * 2
        c2[t]['mm'] = []
        for half in range(2):
            pm_ = bk(b0 + half)
            for k in range(8):
                t_mm = S.op('pe', lambda e, k=k, pm_=pm_, t=t, half=half: e.matmul(
                    pm_, lhsT=merged[:, k, t * 128:(t + 1) * 128], rhs=woutb[:, k, half * 512:(half + 1) * 512],
                    start=(k == 0), stop=(k == 7)),
                    deps=[t_wout, fb_free[b0 + half]] + [t_merged[(kk, t // 4)] for kk in range(8)] if k == 0 else (),
                    sig=(k == 7))
            c2[t]['mm'].append(t_mm)

    def c2_s1(t):
        r = t % NXR2
        w = wk[t % NWK]
        b0 = (t % 2) * 2
        for half in range(2):
            pm_ = bk(b0 + half)
            t_pre = S.op('dve', lambda e, pm_=pm_, r=r, half=half, w=w: e.scalar_tensor_tensor(
                out=w[:, half * 512:(half + 1) * 512], in0=xr[r][:, half * 512:(half + 1) * 512],
                scalar=ALPHA, in1=pm_, op0=ALU.mult, op1=ALU.add),
                deps=[c2[t]['mm'][half], c2[t]['x'], wk_free[t % NWK]])
            fb_free[b0 + half] = t_pre
        xr_free[r] = t_pre
        sl = t % 4
        st = stt[:, sl]; mv = mvr[:, sl]
        S.op('dve', lambda e: e.bn_stats(out=st[:, 0:6], in_=w[:, 0:512]), deps=[t_pre, st_free[sl]], sig=False)
        t1 = S.op('dve', lambda e: e.bn_stats(out=st[:, 6:12], in_=w[:, 512:1024]))
        c2[t]['ag1'] = S.op('dve', lambda e: e.bn_aggr(out=mv[:, 0:2], in_=st[:, 0:12]), deps=[t1])

    def c2_s2(t):
        mv = mvr[:, t % 4]
        c2[t]['sq1'] = S.op('act', lambda e: e.activation(out=mv[:, 4:5], in_=mv[:, 1:2], func=AF.Sqrt, bias=epst[:, 0:1],
                                                          scale=1.0), deps=[c2[t]['ag1'], t_m3])

    def c2_s3(t):
        _, _, _, c2[t]['r1'] = rstd_part(t % 4, [c2[t]['sq1']])

    def c2_s4(t):
        mv = mvr[:, t % 4]
        w = wk[t % NWK]
        c2[t]['n1'] = S.op('act', lambda e: e.activation(out=w, in_=w, func=AF.Identity, scale=mv[:, 2:3], bias=mv[:, 3:4]),
                           deps=[c2[t]['r1']])
        st_free[t % 4] = c2[t]['n1']

    def c2_s5(t):
        w = wk[t % NWK]
        t_g_ = S.op('pool', lambda e: e.tensor_tensor(out=xmid[:, t, :], in0=w, in1=ln1g, op=ALU.mult),
                    deps=[c2[t]['n1'], t_ln1])
        wk_free[t % NWK] = t_g_
        t_xmid[t] = S.dma('pool', lambda e: e.dma_start(out=xmid[:, t, :], in_=ln1b, accum_op=ALU.add), 'xb%d' % t,
                          deps=[t_g_, t_ln1])

    def c2_s5w(t):
        pass

    def c2_s6(t):
        sl = 4 + t % 4
        st = stt[:, sl]; mv = mvr[:, sl]
        src = xmid[:, t, :]
        S.op('dve', lambda e: e.bn_stats(out=st[:, 0:6], in_=src[:, 0:512]), deps=[t_xmid[t], st_free[sl]], sig=False)
        t1 = S.op('dve', lambda e: e.bn_stats(out=st[:, 6:12], in_=src[:, 512:1024]))
        c2[t]['ag2'] = S.op('dve', lambda e: e.bn_aggr(out=mv[:, 0:2], in_=st[:, 0:12]), deps=[t1])

    def c2_s7(t):
        mv = mvr[:, 4 + t % 4]
        c2[t]['sq2'] = S.op('act', lambda e: e.activation(out=mv[:, 4:5], in_=mv[:, 1:2], func=AF.Sqrt, bias=epst[:, 0:1],
                                                          scale=1.0), deps=[c2[t]['ag2']])

    def c2_s8(t):
        _, _, _, c2[t]['r2'] = rstd_part(4 + t % 4, [c2[t]['sq2']])

    def c2_s9(t):
        mv = mvr[:, 4 + t % 4]
        r = t % 2
        c2[t]['n2'] = S.op('act', lambda e: e.activation(out=xnb[r], in_=xmid[:, t, :], func=AF.Identity,
                                                         scale=mv[:, 2:3], bias=mv[:, 3:4]), deps=[c2[t]['r2'], xnb_free[r]])
        st_free[4 + t % 4] = c2[t]['n2']

    def c2_s10(t):
        r = t % 2
        ptA = bkb(4 + 2 * r)
        ptB = bkb(5 + 2 * r)
        for c in range(8):
            pt = ptA if c < 4 else ptB
            t_tr = S.op('pe', lambda e, c=c, pt=pt, src=xnb[r]: e.transpose(
                pt[:, (c % 4) * 128:(c % 4 + 1) * 128], src[:, c * 128:(c + 1) * 128], ident),
                deps=[c2[t]['n2'], fb_free[4 + 2 * r], fb_free[5 + 2 * r]] if c == 0 else (), sig=(c == 3 or c == 7))
            if c == 3:
                c2[t]['trA'] = t_tr
        c2[t]['trB'] = t_tr
        xnb_free[r] = t_tr

    def c2_s11(t):
        r = t % 2
        ptA = bkb(4 + 2 * r)
        ptB = bkb(5 + 2 * r)
        for c in range(8):
            dst = h2T[:, c, t * 128:(t + 1) * 128]
            pt = ptA if c < 4 else ptB
            t_ev = S.op('act', lambda e, c=c, dst=dst, pt=pt: e.activation(
                out=dst, in_=pt[:, (c % 4) * 128:(c % 4 + 1) * 128], func=AF.Identity,
                scale=mcol(3, c, 0), bias=mcol(2, c, 0)),
                deps=[c2[t]['trA'], c2[t]['trB'], t_modc23] if c == 0 else (), sig=(c == 3 or c == 7))
            if c == 3:
                t_evA = t_ev
        t_evB = t_ev
        fb_free[4 + 2 * r] = t_evA
        fb_free[5 + 2 * r] = t_evB
        t_h2[t] = [t_evA, t_evB]

    run_pipeline([c2_s0, c2_s1, c2_s2, c2_s3, c2_s4, c2_s5, c2_s5w, c2_s6, c2_s7, c2_s8, c2_s9, c2_s10, c2_s11], NT)

    if debug_stop == 'C2':
        dbg['xmid'] = (xmid.rearrange('p a b -> p (a b)'), [128, NT * D], F32)
        dbg['h2T'] = (h2T.rearrange('p a b -> p (a b)'), [128, 8 * T], BF16)
        return finish(nc, es, S, dbg, t_h2 + t_xmid)

    S.barrier()
    o = RC
    for i in range(1, NWR):
        w1[i], _, nb = A.alloc(BF16, [8, 512], at=o); o += nb
    for i in range(NWR):
        w2[i], _, nb = A.alloc(BF16, [FG, 1024], at=o); o += nb
    assert o <= RC + 28 * 1024, (o - RC)
    o = RD
    actb = []
    for i in range(2):
        p_, _, nb = A.alloc(BF16, [FG, T], at=o); o += nb
        actb.append(p_)
    w2st = []
    for i in range(2):
        p_, _, nb = A.alloc(F32, [FG, 1024], at=o); o += nb
        w2st.append(p_)
    assert o <= RD + RD_SIZE, (o - RD)
    o = RE + 64
    sgb = []
    for i in range(2):
        p_, _, nb = A.alloc(F32, [512], at=o); o += nb
        sgb.append(p_)
    ln2g, _, nb = A.alloc(F32, [D], at=o); o += nb
    ln2b, _, nb = A.alloc(F32, [D], at=o); o += nb
    y0 = []
    for i in range(3):
        p_, _, nb = A.alloc(F32, [D], at=o); o += nb
        y0.append(p_)
    assert o <= ARENA_BYTES, (o, ARENA_BYTES)
    t_ln2 = [sp_load(ln2g, ln2g_d, 'ln2'), sp_load(ln2b, ln2b_d, 'ln2')][-1]

    w_free = [None] * NWR
    w2st_free = [None, None]
    t_w1 = [None] * NR
    t_w2 = [None] * NR

    def load_round(r):
        slot = r % NWR
        if r == 0:
            t_w1[r] = t_w1_pre
        else:
            t_w1[r] = pool_cast_load(w1[slot].rearrange('p a b -> p (a b)'), wff1_d[r], 'wf1_%d' % slot, deps=[w_free[slot]])
        s2 = r % 2
        t_l = S.dma('sp', lambda e, dst=w2st[s2], r=r: e.dma_start(out=dst.rearrange('p a b -> p (a b)'), in_=wff2_d[r]),
                    'wf2_%d' % s2, deps=[w2st_free[s2]])
        t_w2[r] = S.op('pool', lambda e, slot=slot, s2=s2: e.tensor_tensor(
            out=w2[slot], in0=w2st[s2], in1=g2bc.unsqueeze(1).to_broadcast([128, FG, 1024]), op=ALU.mult),
            deps=[t_l, w_free[slot]] + t_g2)
        w2st_free[s2] = t_w2[r]

    fb_free = [None] * 8
    sg_free = [None, None]
    act_free = [None, None]
    t_act = {}
    t_acc = [None] * NT
    gu_i = 0

    def emit_gu_unit(r, tg, fi):
        nonlocal gu_i
        slot = r % NWR
        ab = actb[r % 2]
        b0 = (gu_i % 2) * 2
        s_ = gu_i % 2
        gu_i += 1
        pg, pu = bk(b0), bk(b0 + 1)
        tok = slice(tg * 512, (tg + 1) * 512)
        for k in range(8):
            t_g_ = S.op('pe', lambda e, k=k, pg=pg, slot=slot, fi=fi, tok=tok: e.matmul(
                pg, lhsT=w1[slot][:, k, fi * 128:(fi + 1) * 128], rhs=h2T[:, k, tok], start=(k == 0), stop=(k == 7)),
                deps=[t_w1[r], fb_free[b0]] if k == 0 else (), sig=(k == 7))
        for k in range(8):
            t_u_ = S.op('pe', lambda e, k=k, pu=pu, slot=slot, fi=fi, tok=tok: e.matmul(
                pu, lhsT=w1[slot][:, k, 256 + fi * 128:256 + (fi + 1) * 128], rhs=h2T[:, k, tok],
                start=(k == 0), stop=(k == 7)), deps=[fb_free[b0 + 1]] if k == 0 else (), sig=(k == 7))
        t_s = S.op('act', lambda e, dst=sgb[s_], pg=pg: e.activation(out=dst, in_=pg, func=AF.Silu),
                   deps=[t_g_, sg_free[s_]])
        fb_free[b0] = t_s
        t_m = S.op('dve', lambda e, ab=ab, fi=fi, tok=tok, pu=pu, sg=sgb[s_]: e.tensor_tensor(
            out=ab[:, fi, tok], in0=pu, in1=sg, op=ALU.mult), deps=[t_u_, t_s, act_free[r % 2]])
        fb_free[b0 + 1] = t_m
        sg_free[s_] = t_m
        t_act[(r, fi, tg)] = t_m

    def emit_gu_tg(r, tg):
        for fi in range(FG):
            emit_gu_unit(r, tg, fi)

    def emit_gu(r):
        for tg in range(4):
            emit_gu_tg(r, tg)

    def emit_dn_tile(rounds, t):
        b0 = 4 + (t % 2) * 2
        last_mm = None
        for half in range(2):
            pd = bk(b0 + half)
            n_mm = len(rounds) * FG
            i_mm = 0
            for r in rounds:
                slot = r % NWR
                ab = actb[r % 2]
                for fi in range(FG):
                    t_mm = S.op('pe', lambda e, pd=pd, fi=fi, t=t, half=half, ab=ab, slot=slot, i_mm=i_mm, n_mm=n_mm: e.matmul(
                        pd, lhsT=ab[:, fi, t * 128:(t + 1) * 128], rhs=w2[slot][:, fi, half * 512:(half + 1) * 512],
                        start=(i_mm == 0), stop=(i_mm == n_mm - 1)),
                        deps=[t_w2[r], fb_free[b0 + half], t_act[(r, fi, t // 4)]], sig=(i_mm == n_mm - 1))
                    i_mm += 1
            accv = xmid[:, t, half * 512:(half + 1) * 512]
            if rounds[0] == 0:
                t_ad = S.op('dve', lambda e, accv=accv, pd=pd: e.scalar_tensor_tensor(
                    out=accv, in0=accv, scalar=ALPHA, in1=pd, op0=ALU.mult, op1=ALU.add), deps=[t_mm, t_acc[t]])
            else:
                t_ad = S.op('dve', lambda e, accv=accv, pd=pd: e.tensor_tensor(
                    out=accv, in0=pd, in1=accv, op=ALU.add), deps=[t_mm, t_acc[t]])
            fb_free[b0 + half] = t_ad
            t_acc[t] = t_ad
            last_mm = t_mm
        return last_mm

    def emit_dn(r):
        last_mm = None
        for t in range(NT):
            last_mm = emit_dn_tile([r], t)
        act_free[r % 2] = last_mm
        w_free[r % NWR] = last_mm

    y_free = [None] * 3
    t_out = []
    tl = [dict() for _ in range(NT)]

    def tail_s1(t):
        sl = t % 4
        st = stt[:, sl]; mv = mvr[:, sl]
        src = xmid[:, t, :]
        S.op('dve', lambda e: e.bn_stats(out=st[:, 0:6], in_=src[:, 0:512]), deps=[t_acc[t], st_free[sl]], sig=False)
        t1 = S.op('dve', lambda e: e.bn_stats(out=st[:, 6:12], in_=src[:, 512:1024]))
        tl[t]['ag'] = S.op('dve', lambda e: e.bn_aggr(out=mv[:, 0:2], in_=st[:, 0:12]), deps=[t1])

    def tail_s2(t):
        mv = mvr[:, t % 4]
        tl[t]['sq'] = S.op('act', lambda e: e.activation(out=mv[:, 4:5], in_=mv[:, 1:2], func=AF.Sqrt, bias=epst[:, 0:1],
                                                         scale=1.0), deps=[tl[t]['ag']])

    def tail_s3(t):
        _, _, _, tl[t]['r'] = rstd_part(t % 4, [tl[t]['sq']])

    def tail_s4(t):
        mv = mvr[:, t % 4]
        r3 = t % 3
        tl[t]['n'] = S.op('act', lambda e: e.activation(out=y0[r3], in_=xmid[:, t, :], func=AF.Identity,
                                                        scale=mv[:, 2:3], bias=mv[:, 3:4]), deps=[tl[t]['r'], y_free[r3]])
        st_free[t % 4] = tl[t]['n']

    def tail_s5(t):
        r3 = t % 3
        tl[t]['g'] = S.op('pool', lambda e: e.tensor_tensor(out=y0[r3], in0=y0[r3], in1=ln2g, op=ALU.mult),
                          deps=[tl[t]['n'], t_ln2])

    def tail_s6(t):
        r3 = t % 3
        t_st_ = S.dma('pool', lambda e, t=t: e.dma_start(out=out_d[t * 128:(t + 1) * 128, :], in_=y0[r3], accum_op=ALU.add),
                      'out%d' % r3, deps=[tl[t]['g'], outpre['t']])
        y_free[r3] = t_st_
        t_out.append(t_st_)

    tail_stages = [tail_s1, tail_s2, tail_s3, tail_s4, tail_s5, tail_s6]
    tail_state = {'n': 0}

    def tail_step(t_new):
        it = tail_state['n']; tail_state['n'] += 1
        K = len(tail_stages)
        for k in reversed(range(K)):
            i = it - k
            if 0 <= i < NT and (t_new is not None or True):
                if i <= (t_new if t_new is not None else NT - 1):
                    tail_stages[k](i)

    for r in range(min(NWR, NR)):
        load_round(r)
    R1, R2 = NR - 2, NR - 1
    emit_gu(0)
    for r in range(NR - 2):
        if r + 1 < NR - 2:
            gu_units = [(tg, fi) for tg in range(4) for fi in range(FG)]
            per = NT // len(gu_units)
            last_mm = None
            for i_, (tg, fi) in enumerate(gu_units):
                emit_gu_unit(r + 1, tg, fi)
                for t in range(i_ * per, (i_ + 1) * per):
                    last_mm = emit_dn_tile([r], t)
            act_free[r % 2] = last_mm
            w_free[r % NWR] = last_mm
        else:
            emit_gu_tg(R1, 0)
            emit_dn(r)
        if r + NWR < NR:
            load_round(r + NWR)
    emit_gu_tg(R2, 0)
    order = [('GU', 1), ('DN', 0), ('GU', 2), ('DN', 1), ('GU', 3), ('DN', 2), ('DN', 3)]
    for kind_, tg in order:
        if kind_ == 'GU':
            emit_gu_tg(R1, tg)
            emit_gu_tg(R2, tg)
        else:
            for t in range(4 * tg, 4 * tg + 4):
                emit_dn_tile([R1, R2], t)
                tail_step(t)
    for _ in range(len(tail_stages)):
        tail_step(None)
    assert len(t_out) == NT
    return finish(nc, es, S, dbg, t_out)


def finish(nc, es, S, dbg, final_ticks):
    dbg_ticks = []
    for name, spec in dbg.items():
        ap, shape = spec[0], spec[1]
        dt_ = spec[2] if len(spec) > 2 else F32
        d = nc.dram_tensor("dbg_" + name, list(shape), dt_, kind="ExternalOutput").ap()
        dbg_ticks.append(S.dma('sp', lambda e, d=d, ap=ap: e.dma_start(out=d, in_=ap), 'dbg', deps=final_ticks))
    S.barrier()
    S.emit(nc, es)
    es.close()
    return nc


def _rope_tables(start):
    pos = np.arange(start - 128, start - 128 + TH)
    rows = (pos // 64).astype(np.float64)
    cols = (pos % 64).astype(np.float64)
    inv = 10000.0 ** (-np.arange(16, dtype=np.float64) / 16)
    C = np.zeros((64, TH), np.float64)
    Sg = np.zeros((64, TH), np.float64)
    for d in range(64):
        p_ = rows if d < 32 else cols
        dd = d % 32
        i = dd % 16
        ang = p_ * inv[i]
        C[d] = np.cos(ang)
        Sg[d] = -np.sin(ang) if dd < 16 else np.sin(ang)
    C = C.astype(np.float32)
    Sg = Sg.astype(np.float32)
    return np.concatenate([C, C], 0), np.concatenate([Sg, Sg], 0)


def _perm_matrix():
    P = np.zeros((128, 128), np.float32)
    for m in range(128):
        d = m % 64
        dd = d % 32
        partner = d + 16 if dd < 16 else d - 16
        P[(m // 64) * 64 + partner, m] = 1.0
    return P


def prep_inputs(inp):
    f = lambda a: np.ascontiguousarray(np.asarray(a, dtype=np.float32))
    x = f(inp['x']); c = f(inp['c']); ctx = f(inp['ctx']); c_ctx = f(inp['c_ctx'])
    w_ada = f(inp['w_ada'])[0]; b_ada = f(inp['b_ada'])[0]; w_in = f(inp['w_in'])[0]
    sink = f(inp['attn_sink'])[0]
    glg = f(inp['gmlp_ln_g'])[0]; glb = f(inp['gmlp_ln_b'])[0]
    w_s = f(inp['w_spatial'])[0]; b_s = f(inp['b_spatial'])[0]
    w_a = f(inp['w_branch_a'])[0]; w_b = f(inp['w_branch_b'])[0]; w_out = f(inp['w_out'])[0]
    ln1g = f(inp['ln1_g'])[0]; ln1b = f(inp['ln1_b'])[0]; ln2g = f(inp['ln2_g'])[0]; ln2b = f(inp['ln2_b'])[0]
    w_ffn_in = f(inp['w_ffn_in'])[0]; w_ffn_out = f(inp['w_ffn_out'])[0]

    def ktile(w):
        n = w.shape[1]
        return np.ascontiguousarray(w.reshape(8, 128, n).transpose(1, 0, 2)).reshape(128, 8 * n)

    wada_t = np.ascontiguousarray(w_ada.reshape(8, 128, 12, 512).transpose(2, 1, 0, 3)).reshape(12, 128, 4096)
    qcols = []
    for cc in range(4):
        qcols += list(range(cc * 64, cc * 64 + 64)) + list(range((4 + cc) * 64, (4 + cc) * 64 + 64))
    qkcols = qcols + list(range(512, 640))
    wqk = ktile(w_in[:, qkcols])
    wu = ktile(w_in[:, 768:1280])
    wvvb = ktile(w_in[:, list(range(640, 768)) + list(range(1280, 1792))])
    wga = w_in[:, 1792:2816]
    wgb = w_in[:, 2816:3840]
    wc1 = np.zeros((8, 128, 24, 128), np.float32)
    for m in range(8):
        cs = slice(m * 128, (m + 1) * 128)
        wc1[m, :, 0:4] = w_a[:, cs].reshape(4, 128, 128).transpose(1, 0, 2)
        wc1[m, :, 4:8] = w_b[:, cs].reshape(4, 128, 128).transpose(1, 0, 2)
        wc1[m, :, 8:16] = wga[:, cs].reshape(8, 128, 128).transpose(1, 0, 2)
        wc1[m, :, 16:24] = wgb[:, cs].reshape(8, 128, 128).transpose(1, 0, 2)
    wc1 = wc1.reshape(8, 128, 24 * 128)
    wout_t = ktile(w_out)
    wst = np.ascontiguousarray(w_s.transpose(2, 0, 1)).reshape(128, 8 * 128)
    bsbc = np.zeros((128, 4, 128), np.float32)
    for cc in range(4):
        for gi in range(2):
            bsbc[gi * 64:(gi + 1) * 64, cc, :] = b_s[2 * cc + gi][None, :]
    bsbc = bsbc.reshape(128, 512)
    badac = np.zeros((128, 4, 8, 2), np.float32)
    for kind, off in enumerate((0, 1024, 3072, 4096)):
        badac[:, kind, :, :] = b_ada[off:off + 1024].reshape(8, 128).T[:, :, None]
    badac = badac.reshape(128, 64)
    badabc = np.ascontiguousarray(np.broadcast_to(
        np.concatenate([b_ada[2048:3072], b_ada[5120:6144]])[None, :], (128, 2048)))
    wff1 = np.zeros((NR, 128, 8, 512), np.float32)
    wff2 = np.zeros((NR, 128, FG, 1024), np.float32)
    for r in range(NR):
        for fi in range(FG):
            fch = r * FG + fi
            wff1[r, :, :, fi * 128:(fi + 1) * 128] = w_ffn_in[:, fch * 128:(fch + 1) * 128].reshape(8, 128, 128).transpose(1, 0, 2)
            wff1[r, :, :, 256 + fi * 128:256 + (fi + 1) * 128] = \
                w_ffn_in[:, FFH + fch * 128:FFH + (fch + 1) * 128].reshape(8, 128, 128).transpose(1, 0, 2)
            wff2[r, :, fi, :] = w_ffn_out[fch * 128:(fch + 1) * 128, :]
    wff1 = wff1.reshape(NR, 128, 4096)
    wff2 = wff2.reshape(NR, 128, FG * 1024)
    ident = np.eye(128, dtype=np.float32)
    perm = _perm_matrix()
    ki = np.arange(128)[:, None]
    qi = np.arange(128)[None, :]
    maskP = (ki >= qi).astype(np.float32)
    maskN = (ki <= qi).astype(np.float32)
    zero = np.zeros((128, 128), np.float32)
    bc = lambda v: np.ascontiguousarray(np.broadcast_to(v[None, :], (128, v.shape[0])))
    shared = dict(wada=wada_t, badac=badac, badabc=badabc, wqk=wqk, wu=wu, wvvb=wvvb, wc1=wc1, wout=wout_t,
                  wst=wst, bsbc=bsbc, wff1=wff1, wff2=wff2, ident=ident, perm=perm, esink=bc(sink),
                  glg=bc(glg), glb=bc(glb), ln1g=bc(ln1g), ln1b=bc(ln1b), ln2g=bc(ln2g), ln2b=bc(ln2b))
    in_maps = []
    for core in range(NCORES):
        b = core // 4
        seg = core % 4
        start = seg * T
        xhalo = np.zeros((TH, D), np.float32)
        lo = max(start - 128, 0)
        hi = min(start + T + 128, SEQ)
        xhalo[lo - (start - 128): hi - (start - 128)] = x[b, lo:hi]
        cv = np.zeros((128, 16), np.float32)
        cv[:, 0:8] = c[b].reshape(8, 128).T
        cv[:, 8:16] = c_ctx.reshape(8, 128).T
        rc, rs = _rope_tables(start)
        mk = np.stack([zero if seg == 0 else maskP, maskP, maskN, zero if seg == 3 else maskN], 1).reshape(128, 512)
        m = dict(shared)
        m.update(xh=xhalo, ctxb=np.ascontiguousarray(ctx[b]), cvec=cv, ropec=rc, ropes=rs, masks=np.ascontiguousarray(mk))
        in_maps.append(m)
    return in_maps


_NC_CACHE = {}


def kernel(**inputs):
    in_maps = prep_inputs(inputs)
    if 'nc' not in _NC_CACHE:
        _NC_CACHE['nc'] = build_nc(None)
    nc = _NC_CACHE['nc']
    res = run_bass_kernel_spmd(nc, in_maps, core_ids=list(range(NCORES)))
    out = np.zeros((2, SEQ, D), np.float32)
    for core in range(NCORES):
        b = core // 4
        seg = core % 4
        out[b, seg * T:(seg + 1) * T] = res.results[core]["out"]
    return out
```

```python
import numpy as np
from contextlib import ExitStack
import concourse.bass as bass
import concourse.mybir as mybir
from concourse.bass_utils import run_bass_kernel_spmd

F32 = mybir.dt.float32
BF16 = mybir.dt.bfloat16
AF = mybir.ActivationFunctionType
ALU = mybir.AluOpType

NCORES = 8
D = 1024
T = 2048
NT = 16
TH = 2304
NTH = 18
CTX = 256
FFH = 2816
NF = 22
FG = 2
NR = NF // FG
ALPHA = 2.0 ** 0.25
EPS = 1e-5
SEQ = 8192
IN_SPLITS = (512, 640, 768, 1280, 1792, 2816, 3840)

DEBUG_STOP = None


class Sched:
    ENG = ('pe', 'act', 'dve', 'pool', 'sp')

    def __init__(self):
        self.q = {e: [] for e in self.ENG}
        self.cnt = {e: 0 for e in self.ENG}
        self.waited = {e: {} for e in self.ENG}
        self.dmacnt = {}

    def _resolve(self, eng, deps):
        ws = []
        stack = [deps]
        flat = []
        while stack:
            d = stack.pop()
            if d is None:
                continue
            if isinstance(d, tuple) and len(d) == 2 and isinstance(d[0], str):
                flat.append(d)
            else:
                stack.extend(list(d))
        for (p, t) in flat:
            if self.waited[eng].get(p, 0) >= t:
                continue
            self.waited[eng][p] = t
            ws.append((p, t))
        return ws

    def op(self, eng, fn, deps=(), sig=True):
        ws = self._resolve(eng, deps)
        tick = None
        if sig:
            self.cnt[eng] += 1
            tick = (eng, self.cnt[eng])
        self.q[eng].append((ws, fn, eng if sig else None, 1))
        return tick

    def dma(self, eng, fn, key, deps=()):
        ws = self._resolve(eng, deps)
        self.dmacnt[key] = self.dmacnt.get(key, 0) + 16
        self.q[eng].append((ws, fn, 'dma:' + key, 16))
        return ('dma:' + key, self.dmacnt[key])

    def barrier(self):
        ticks = [(e, c) for e, c in self.cnt.items() if c > 0]
        ticks += [('dma:' + k, c) for k, c in self.dmacnt.items()]
        for e in self.ENG:
            ws = self._resolve(e, ticks)
            if ws:
                self.q[e].append((ws, None, None, 0))

    def emit(self, nc, es):
        sems = {}
        for e in self.ENG:
            sems[e] = es.enter_context(nc.semaphore("s_" + e))
        for k in self.dmacnt:
            sems['dma:' + k] = es.enter_context(nc.semaphore("d_" + k))
        block = es.enter_context(nc.Block())
        reg = {'pe': block.tensor, 'act': block.scalar, 'dve': block.vector,
               'pool': block.gpsimd, 'sp': block.sync}
        for e in self.ENG:
            items = self.q[e]

            def body(engine, items=items):
                for (ws, fn, sigkey, inc) in items:
                    for (p, t) in ws:
                        engine.wait_ge(sems[p], t)
                    if fn is None:
                        continue
                    ins = fn(engine)
                    if sigkey is not None:
                        ins.then_inc(sems[sigkey], inc)
            reg[e](body)


class Arena:
    def __init__(self, nc, es, nbytes):
        self.t = es.enter_context(nc.sbuf_tensor("arena", [128, nbytes // 2], BF16))
        self.nbytes = nbytes
        self.top = 0
        self.marks = {}

    def alloc(self, dtype, shape, at=None):
        esz = 4 if dtype == F32 else 2
        n = int(np.prod(shape))
        nb = (n * esz + 63) // 64 * 64
        if at is None:
            off = self.top
            self.top += nb
        else:
            off = at
        assert off + nb <= self.nbytes, ("arena overflow", off, nb, self.nbytes)
        a = self.t[:, off // 2: off // 2 + n * esz // 2]
        if dtype == F32:
            a = a.bitcast(F32)
        if len(shape) == 2:
            a = a.rearrange('p (a b) -> p a b', a=shape[0])
        elif len(shape) == 3:
            a = a.rearrange('p (a b c) -> p a b c', a=shape[0], b=shape[1])
        return a, off, nb


def run_pipeline(stages, n_items, hook=None):
    K = len(stages)
    for it in range(n_items + K - 1):
        for k in reversed(range(K)):
            i = it - k
            if 0 <= i < n_items:
                stages[k](i)
        if hook is not None:
            hook(it)


def build_nc(debug_stop=None):
    nc = bass.Bass("TRN2", target_bir_lowering=False)
    es = ExitStack()
    S = Sched()

    def din(name, shape):
        return nc.dram_tensor(name, list(shape), F32, kind="ExternalInput").ap()

    xh = din("xh", [TH, D])
    ctxb = din("ctxb", [CTX, D])
    cvec = din("cvec", [128, 16])
    wada = din("wada", [12, 128, 8 * 512])
    badac_d = din("badac", [128, 64])
    badabc_d = din("badabc", [128, 2048])
    wqk_d = din("wqk", [128, 8 * 640])
    wu_d = din("wu", [128, 8 * 512])
    wvvb_d = din("wvvb", [128, 8 * 640])
    wc1_d = din("wc1", [8, 128, 24 * 128])
    wout_d = din("wout", [128, 8 * 1024])
    wst_d = din("wst", [128, 8 * 128])
    bsbc_d = din("bsbc", [128, 512])
    wff1_d = din("wff1", [NR, 128, 8 * 512])
    wff2_d = din("wff2", [NR, 128, FG * 1024])
    ropec_d = din("ropec", [128, TH])
    ropes_d = din("ropes", [128, TH])
    masks_d = din("masks", [128, 4 * 128])
    ident_d = din("ident", [128, 128])
    perm_d = din("perm", [128, 128])
    esink_d = din("esink", [128, 8])
    glg_d = din("glg", [128, 512])
    glb_d = din("glb", [128, 512])
    ln1g_d = din("ln1g", [128, D])
    ln1b_d = din("ln1b", [128, D])
    ln2g_d = din("ln2g", [128, D])
    ln2b_d = din("ln2b", [128, D])
    out_d = nc.dram_tensor("out", [T, D], F32, kind="ExternalOutput").ap()
    dbg = {}

    ARENA_BYTES = 207 * 1024
    A = Arena(nc, es, ARENA_BYTES)
    banks = [es.enter_context(nc.psum_tensor("bank%d" % i, [128, 512], F32)) for i in range(8)]

    def bk(i):
        return banks[i][:, :]

    def bkb(i):
        return banks[i][:, :].bitcast(BF16)

    ident, _, _ = A.alloc(BF16, [128])
    perm, _, _ = A.alloc(BF16, [128])
    masks, _, _ = A.alloc(BF16, [4, 128])
    esink, _, _ = A.alloc(F32, [8])
    modc, _, _ = A.alloc(F32, [64])
    sc, _, _ = A.alloc(BF16, [16])
    small, _, _ = A.alloc(F32, [64])
    onesf, _, _ = A.alloc(F32, [128])
    neghalf, _, _ = A.alloc(F32, [32])
    bsbc, _, _ = A.alloc(F32, [512])
    g2bc, _, _ = A.alloc(F32, [1024])
    stt, _, _ = A.alloc(F32, [8, 16])
    mvr, _, _ = A.alloc(F32, [8, 8])
    st_free = [None] * 8
    epst, _, _ = A.alloc(F32, [16])
    CONST_END = A.top

    RA = A.top
    hT, _, nbA = A.alloc(BF16, [8, TH])
    RB = A.top
    RB_SIZE = 64 * 1024
    A.top = RB + RB_SIZE
    RC = A.top
    RC_SIZE = 36 * 1024
    A.top = RC + RC_SIZE
    RD = A.top
    RD_SIZE = 32 * 1024
    A.top = RD + RD_SIZE
    RE = A.top
    RE_SIZE = ARENA_BYTES - RE
    assert RE_SIZE >= 24 * 1024, RE_SIZE

    o = RB
    qT, _, nb = A.alloc(BF16, [4, T], at=o); o += nb
    kT, _, nb = A.alloc(BF16, [TH], at=o); o += nb
    kcT, _, nb = A.alloc(BF16, [CTX], at=o); o += nb
    vaug, _, nb = A.alloc(BF16, [NTH, 2, 66], at=o); o += nb
    vcaug, _, nb = A.alloc(BF16, [2, 2, 66], at=o); o += nb
    guT, _, nb = A.alloc(BF16, [4, T], at=o); o += nb
    vn, _, nb = A.alloc(BF16, [NT, 512], at=o); o += nb
    assert o <= RB + RB_SIZE, (o - RB)
    xmid, _, _ = A.alloc(F32, [NT, D], at=RB)
    h2T, _, _ = A.alloc(BF16, [8, T], at=RA)
    hcT_off = None
    yaT, _, nb1 = A.alloc(BF16, [4, T], at=RC)
    ybT, _, nb2 = A.alloc(BF16, [4, T], at=RC + nb1)
    merged, _, _ = A.alloc(BF16, [8, T], at=RD)

    cdeps = []

    def sp_load(dst, src, key='const'):
        return S.dma('sp', lambda e, dst=dst, src=src: e.dma_start(out=dst, in_=src), key)

    def pool_cast_load(dst, src, key, deps=()):
        return S.dma('pool', lambda e, dst=dst, src=src: e.dma_start(out=dst, in_=src, max_dma_last_dim=4096), key, deps)

    cvec_f, _, _ = A.alloc(F32, [16], at=RE)
    badac, _, _ = A.alloc(F32, [64], at=RE + 64)
    t_cvec = sp_load(cvec_f, cvec, 'cvec')
    t_badac = sp_load(badac, badac_d, 'badac')
    t_c = [sp_load(esink, esink_d), sp_load(bsbc, bsbc_d), ]
    t_cb = [pool_cast_load(ident, ident_d, 'cb'), pool_cast_load(perm, perm_d, 'cb'),
            pool_cast_load(masks.rearrange('p a b -> p (a b)'), masks_d, 'cb')]
    t_m1 = S.op('pool', lambda e: e.memset(onesf, 1.0))
    t_m2 = S.op('pool', lambda e: e.memset(neghalf, -0.5))
    t_m3 = S.op('pool', lambda e: e.memset(epst, EPS))
    T_CONST = [t_c[-1], t_cb[-1], t_m1, t_m2]

    o = RC
    wqk, _, nb = A.alloc(BF16, [8, 640], at=o); o += nb
    wu, _, nb = A.alloc(BF16, [8, 512], at=o); o += nb
    wvvb, _, nb = A.alloc(BF16, [8, 640], at=o); o += nb
    hcT, _, nb = A.alloc(BF16, [8, CTX], at=o); o += nb
    scb, _, nb = A.alloc(BF16, [8, 128], at=o); o += nb
    bslice, _, nb = A.alloc(F32, [512], at=o); o += nb
    assert o <= RC + RC_SIZE, (o - RC)

    o = RB
    wring0 = []
    for i in range(4):
        w_, _, nb = A.alloc(BF16, [8, 512], at=o); o += nb
        wring0.append(w_)
    assert o <= RB + RB_SIZE, o
    g1bc, _, nb = A.alloc(F32, [1024], at=RE + 384)
    RE_FREE = RE + 384 + nb
    wring1 = []
    o = RE_FREE + 8192
    for i in range(2):
        w_, _, nb = A.alloc(BF16, [8, 512], at=o); o += nb
        wring1.append(w_)
    assert o <= ARENA_BYTES, (o, ARENA_BYTES)

    t_sc = S.op('act', lambda e: e.activation(out=sc, in_=cvec_f, func=AF.Silu), deps=[t_cvec])
    t_scb = S.op('dve', lambda e: e.tensor_copy(out=scb, in_=sc[:, 0:8].unsqueeze(2).to_broadcast([128, 8, 128])),
                 deps=[t_sc])
    colkind = {0: 0, 1: 0, 2: 1, 3: 1, 6: 2, 7: 2, 8: 3, 9: 3}
    bcform = {4: (0, 0), 5: (0, 1), 10: (1, 0), 11: (1, 1)}
    t_g = {}
    modc4 = modc.rearrange('p (a b c) -> p a b c', a=4, b=8)

    def p0_col(blk, wb_, t_w, pm):
        kind = colkind[blk]
        t_l = None
        for s in range(4):
            chunk = (blk % 2) * 4 + s
            col = (kind * 8 + chunk) * 2
            for k in range(8):
                t_l = S.op('pe', lambda e, k=k, s=s, col=col, wb_=wb_, pm=pm: e.matmul(
                    pm[:, col:col + 2], lhsT=wb_[:, k, s * 128:(s + 1) * 128], rhs=sc[:, k:16:8],
                    start=(k == 0), stop=(k == 7)), deps=[t_w, t_sc] if k == 0 else (),
                    sig=(k == 7 and s == 3))
        return t_l

    def modc_evac(lo, hi, plus1, pm, dep):
        t0_ = S.op('dve', lambda e: e.tensor_tensor(out=modc[:, lo:hi], in0=pm[:, lo:hi], in1=badac[:, lo:hi], op=ALU.add),
                   deps=[dep, t_badac])
        if plus1:
            t0_ = S.op('dve', lambda e: e.tensor_scalar(out=modc[:, lo:hi], in0=modc[:, lo:hi], scalar1=1.0,
                                                        scalar2=None, op0=ALU.add), deps=[t0_])
        return t0_

    t_l = None
    t_modc01_A, t_modc01_B = [], []
    for blk in (0, 2, 1, 3):
        t_w = pool_cast_load(wring0[blk].rearrange('p a b -> p (a b)'), wada[blk], 'wada_e%d' % blk)
        pm_ = bk(0) if blk in (0, 2) else bk(1)
        t_l = p0_col(blk, wring0[blk], t_w, pm_)
        if blk == 2:
            t_modc01_A = [modc_evac(0, 8, False, bk(0), t_l), modc_evac(16, 24, True, bk(0), t_l)]
        if blk == 3:
            t_modc01_B = [modc_evac(8, 16, False, bk(1), t_l), modc_evac(24, 32, True, bk(1), t_l)]
    t_modc01 = t_modc01_A + t_modc01_B
    t_wqk = pool_cast_load(wqk.rearrange('p a b -> p (a b)'), wqk_d, 'wqk')
    t_wvvb = pool_cast_load(wvvb.rearrange('p a b -> p (a b)'), wvvb_d, 'wvvb')
    t_wu = pool_cast_load(wu.rearrange('p a b -> p (a b)'), wu_d, 'wu')

    def mcol(kind, chunk, which):
        return modc4[:, kind, chunk, which:which + 1]

    def stats_part(src, slot, deps, n=1024):
        st = stt[:, slot]
        mv = mvr[:, slot]
        if n == 1024:
            S.op('dve', lambda e: e.bn_stats(out=st[:, 0:6], in_=src[:, 0:512]), deps=[deps, st_free[slot]], sig=False)
            t1 = S.op('dve', lambda e: e.bn_stats(out=st[:, 6:12], in_=src[:, 512:1024]))
            t2 = S.op('dve', lambda e: e.bn_aggr(out=mv[:, 0:2], in_=st[:, 0:12]), deps=[t1])
        else:
            t1 = S.op('dve', lambda e: e.bn_stats(out=st[:, 0:6], in_=src), deps=[deps, st_free[slot]])
            t2 = S.op('dve', lambda e: e.bn_aggr(out=mv[:, 0:2], in_=st[:, 0:6]), deps=[t1])
        return S.op('act', lambda e: e.activation(out=mv[:, 4:5], in_=mv[:, 1:2], func=AF.Sqrt, bias=epst[:, 0:1], scale=1.0),
                    deps=[t2, t_m3])

    def rstd_part(slot, deps):
        mv = mvr[:, slot]
        t4 = S.op('dve', lambda e: e.reciprocal(out=mv[:, 2:3], in_=mv[:, 4:5]), deps=[deps])
        t5 = S.op('dve', lambda e: e.scalar_tensor_tensor(out=mv[:, 3:4], in0=mv[:, 0:1], scalar=-1.0, in1=mv[:, 2:3],
                                                          op0=ALU.mult, op1=ALU.mult), deps=[t4])
        return mv[:, 0:1], mv[:, 2:3], mv[:, 3:4], t5

    o = RD
    NXR = 6
    xr = []
    for i in range(NXR):
        x_, _, nb = A.alloc(F32, [D], at=o); o += nb
        xr.append(x_)
    xnb = []
    for i in range(2):
        x_, _, nb = A.alloc(BF16, [D], at=o); o += nb
        xnb.append(x_)
    assert o <= RD + RD_SIZE, (o - RD)

    xr_free = [None] * NXR
    xnb_free = [None, None]
    bankA_free = [None, None]
    bankB_free = [None, None]
    t_hT = [None] * NTH
    t_hcT = [None] * 2
    tiles = [('x', i) for i in range(NTH)] + [('c', i) for i in range(2)]
    NA1 = len(tiles)
    a1 = {}
    for it in range(NA1 + 2):
        if it < NA1:
            n = it
            kindt, ti = tiles[n]
            xs = n % NXR
            src = xh[ti * 128:(ti + 1) * 128, :] if kindt == 'x' else ctxb[ti * 128:(ti + 1) * 128, :]
            t_ld = S.dma('sp', lambda e, dst=xr[xs], src=src: e.dma_start(out=dst, in_=src), 'xr%d' % xs,
                         deps=[xr_free[xs]])
            a1[n] = {'sq': stats_part(xr[xs], n % 4, [t_ld])}
        if 0 <= it - 1 < NA1:
            n = it - 1
            kindt, ti = tiles[n]
            xs = n % NXR
            bs_ = n % 2
            mean, rstd, nmr, t_r = rstd_part(n % 4, [a1[n]['sq']])
            t_xn = S.op('act', lambda e, dst=xnb[bs_], src=xr[xs], rstd=rstd, nmr=nmr: e.activation(
                out=dst, in_=src, func=AF.Identity, scale=rstd, bias=nmr), deps=[t_r, xnb_free[bs_]])
            xr_free[xs] = t_xn
            st_free[n % 4] = t_xn
            ptA = bkb(4 + 2 * bs_)
            ptB = bkb(5 + 2 * bs_)
            for c in range(8):
                pt = ptA if c < 4 else ptB
                t_tr = S.op('pe', lambda e, c=c, pt=pt, src=xnb[bs_]: e.transpose(
                    pt[:, (c % 4) * 128:(c % 4 + 1) * 128], src[:, c * 128:(c + 1) * 128], ident),
                    deps=[t_xn, bankA_free[bs_], bankB_free[bs_], T_CONST] if c == 0 else (), sig=(c == 3 or c == 7))
                if c == 3:
                    a1[n]['trA'] = t_tr
            a1[n]['trB'] = t_tr
            xnb_free[bs_] = t_tr
        if 0 <= it - 2 < NA1:
            n = it - 2
            kindt, ti = tiles[n]
            bs_ = n % 2
            ptA = bkb(4 + 2 * bs_)
            ptB = bkb(5 + 2 * bs_)
            which = 0 if kindt == 'x' else 1
            for c in range(8):
                dst = hT[:, c, ti * 128:(ti + 1) * 128] if kindt == 'x' else hcT[:, c, ti * 128:(ti + 1) * 128]
                if c < 4:
                    t_evA = S.op('act', lambda e, c=c, dst=dst, ptA=ptA, which=which: e.activation(
                        out=dst, in_=ptA[:, c * 128:(c + 1) * 128], func=AF.Identity,
                        scale=mcol(1, c, which), bias=mcol(0, c, which)),
                        deps=[a1[n]['trA'], t_modc01_A] if c == 0 else (), sig=(c == 3))
                else:
                    t_evB = S.op('dve', lambda e, c=c, dst=dst, ptB=ptB, which=which: e.tensor_scalar(
                        out=dst, in0=ptB[:, (c - 4) * 128:(c - 3) * 128], scalar1=mcol(1, c, which),
                        scalar2=mcol(0, c, which), op0=ALU.mult, op1=ALU.add),
                        deps=[a1[n]['trB'], t_modc01_B] if c == 4 else (), sig=(c == 7))
            bankA_free[bs_] = t_evA
            bankB_free[bs_] = t_evB
            if kindt == 'x':
                t_hT[ti] = [t_evA, t_evB]
            else:
                t_hcT[ti] = [t_evA, t_evB]
    if debug_stop == 'A1':
        dbg['hT'] = (hT.rearrange('p a b -> p (a b)'), [128, 8 * TH], BF16)
        dbg['hcT'] = (hcT.rearrange('p a b -> p (a b)'), [128, 8 * CTX], BF16)
        return finish(nc, es, S, dbg, t_hT + t_hcT)

    S.barrier()
    o = RD
    ropec, _, nb = A.alloc(F32, [TH], at=o); o += nb
    ropes, _, nb = A.alloc(F32, [TH], at=o); o += nb
    glg, _, nb = A.alloc(F32, [512], at=o); o += nb
    glb, _, nb = A.alloc(F32, [512], at=o); o += nb
    qraw = []
    for i in range(2):
        q_, _, nb = A.alloc(BF16, [512], at=o); o += nb
        qraw.append(q_)
    rt1 = []
    rt2 = []
    for i in range(2):
        q_, _, nb = A.alloc(F32, [512], at=o); o += nb
        rt1.append(q_)
        q_, _, nb = A.alloc(F32, [512], at=o); o += nb
        rt2.append(q_)
    assert o <= RD + RD_SIZE, (o - RD)
    o = RE_FREE
    gvb = []
    for i in range(4):
        q_, _, nb = A.alloc(F32, [512], at=o); o += nb
        gvb.append(q_)
    assert o <= RE_FREE + 8192, (o, RE_FREE)
    t_rope = [sp_load(ropec, ropec_d, 'rope'), sp_load(ropes, ropes_d, 'rope')][-1]
    t_gl = [sp_load(glg, glg_d, 'gl'), sp_load(glb, glb_d, 'gl')][-1]
    t_vones = S.op('pool', lambda e: e.memset(vaug[:, :, :, 64:66], 1.0))
    t_vcones = S.op('pool', lambda e: e.memset(vcaug[:, :, :, 64:66], 1.0))

    fb_free = [None] * 8
    qraw_free = [None, None]
    rt_free = [None, None]

    dq = {'blocks': [4, 5, 6, 7, 8, 9, 10, 11], 'issued': [], 'n_issued': 0, 'n_done': 0, 'bs_free': None, 'pb': 0}
    ring1_free = [None, None]
    t_modc23 = []

    def p0_issue():
        if dq['n_issued'] >= len(dq['blocks']):
            return
        blk = dq['blocks'][dq['n_issued']]
        slot = dq['n_issued'] % 2
        dq['n_issued'] += 1
        t_w = pool_cast_load(wring1[slot].rearrange('p a b -> p (a b)'), wada[blk], 'wada_d%d' % slot,
                             deps=[ring1_free[slot]])
        dq['issued'].append((blk, slot, t_w))

    def p0_compute(col_bank=6, bc_bank=7):
        if dq['n_done'] >= len(dq['blocks']):
            return
        blk, slot, t_w = dq['issued'][dq['n_done']]
        dq['n_done'] += 1
        wb_ = wring1[slot]
        if blk in colkind:
            pm = bk(col_bank)
            kind = colkind[blk]
            t_l = None
            for s in range(4):
                chunk = (blk % 2) * 4 + s
                col = (kind * 8 + chunk) * 2
                for k in range(8):
                    t_l = S.op('pe', lambda e, k=k, s=s, col=col, wb_=wb_, pm=pm: e.matmul(
                        pm[:, col:col + 2], lhsT=wb_[:, k, s * 128:(s + 1) * 128], rhs=sc[:, k:16:8],
                        start=(k == 0), stop=(k == 7)), deps=[t_w, t_sc, fb_free[col_bank]] if k == 0 else (),
                        sig=(k == 7 and s == 3))
            ring1_free[slot] = t_l
            lo = (kind * 8 + (blk % 2) * 4) * 2
            t_e = modc_evac(lo, lo + 8, kind in (1, 3), pm, t_l)
            fb_free[col_bank] = t_e
            t_modc23.append(t_e)
        else:
            which, half = bcform[blk]
            bank = bc_bank
            pb = bk(bank)
            gi_ = which * 2 + half
            t_bs = S.dma('sp', lambda e, gi_=gi_: e.dma_start(out=bslice, in_=badabc_d[:, gi_ * 512:(gi_ + 1) * 512]),
                         'bslice', deps=[dq['bs_free']])
            for k in range(8):
                t_mm = S.op('pe', lambda e, k=k, pb=pb, wb_=wb_: e.matmul(
                    pb, lhsT=scb[:, k, :], rhs=wb_[:, k, :], start=(k == 0), stop=(k == 7)),
                    deps=[t_w, t_scb, fb_free[bank]] if k == 0 else (), sig=(k == 7))
            ring1_free[slot] = t_mm
            dst = (g1bc if which == 0 else g2bc)[:, half * 512:(half + 1) * 512]
            t_e = S.op('dve', lambda e, dst=dst, pb=pb: e.tensor_tensor(out=dst, in0=pb, in1=bslice, op=ALU.add),
                       deps=[t_mm, t_bs])
            dq['bs_free'] = t_e
            fb_free[bank] = t_e
            t_g[(which, half)] = t_e
        p0_issue()

    p0_issue()
    p0_issue()
    units = []
    for m in range(4):
        for tg in range(4):
            units.append(('q', m, 128 + tg * 512, 512, tg * 512))
    for tg in range(5):
        n_ = 512 if tg < 4 else 256
        units.append(('k', 4, tg * 512, n_, tg * 512))
    t_q = {}
    t_qk_all = []
    qk = [dict() for _ in units]

    def qk_part1(ui):
        kd, m, hcol, n_, ocol = units[ui]
        b0 = (ui % 3) * 2
        pq = bk(b0)[:, 0:n_]
        for k in range(8):
            t_mm = S.op('pe', lambda e, k=k, pq=pq, m=m, hcol=hcol, n_=n_: e.matmul(
                pq, lhsT=wqk[:, k, m * 128:(m + 1) * 128], rhs=hT[:, k, hcol:hcol + n_],
                start=(k == 0), stop=(k == 7)),
                deps=[t_wqk, fb_free[b0], fb_free[b0 + 1]] if k == 0 else (), sig=(k == 7))
        r = ui % 2
        t_raw = S.op('act', lambda e, dst=qraw[r][:, 0:n_], pq=pq: e.activation(out=dst, in_=pq, func=AF.Copy),
                     deps=[t_mm, qraw_free[r]])
        qk[ui]['mm'] = t_mm
        qk[ui]['raw'] = t_raw

    def qk_part2(ui):
        kd, m, hcol, n_, ocol = units[ui]
        b0 = (ui % 3) * 2
        pq, ps_ = bk(b0)[:, 0:n_], bk(b0 + 1)[:, 0:n_]
        r = ui % 2
        t_mm, t_raw = qk[ui]['mm'], qk[ui]['raw']
        t_pm = S.op('pe', lambda e, ps_=ps_, src=qraw[r][:, 0:n_]: e.matmul(ps_, lhsT=perm, rhs=src, start=True, stop=True),
                    deps=[t_raw, T_CONST])
        qraw_free[r] = t_pm
        t_1 = S.op('dve', lambda e, dst=rt1[r][:, 0:n_], pq=pq, hcol=hcol, n_=n_: e.tensor_tensor(
            out=dst, in0=pq, in1=ropec[:, hcol:hcol + n_], op=ALU.mult), deps=[t_mm, t_raw, t_rope, rt_free[r]])
        t_2 = S.op('dve', lambda e, dst=rt2[r][:, 0:n_], ps_=ps_, hcol=hcol, n_=n_: e.tensor_tensor(
            out=dst, in0=ps_, in1=ropes[:, hcol:hcol + n_], op=ALU.mult), deps=[t_pm])
        fb_free[b0] = t_2
        fb_free[b0 + 1] = t_2
        dst = qT[:, m, ocol:ocol + n_] if kd == 'q' else kT[:, ocol:ocol + n_]
        t_3 = S.op('pool', lambda e, dst=dst, a=rt1[r][:, 0:n_], b=rt2[r][:, 0:n_]: e.tensor_tensor(
            out=dst, in0=a, in1=b, op=ALU.add), deps=[t_1, t_2])
        rt_free[r] = t_3
        t_qk_all.append(t_3)

    for ui in range(len(units)):
        qk_part1(ui)
        if ui >= 1:
            qk_part2(ui - 1)
        if ui in (6, 13, 20):
            p0_compute()
    qk_part2(len(units) - 1)
    if debug_stop == 'A2a':
        dbg['qT'] = (qT.rearrange('p a b -> p (a b)'), [128, 4 * T], BF16)
        dbg['kT'] = (kT, [128, TH], BF16)
        return finish(nc, es, S, dbg, t_qk_all)
    pq = bk(0)[:, 0:CTX]
    for k in range(8):
        t_mm = S.op('pe', lambda e, k=k, pq=pq: e.matmul(pq, lhsT=wqk[:, k, 512:640], rhs=hcT[:, k, :],
                                                         start=(k == 0), stop=(k == 7)),
                    deps=[fb_free[0], t_hcT] if k == 0 else (), sig=(k == 7))
    t_kc = S.op('act', lambda e, pq=pq: e.activation(out=kcT, in_=pq, func=AF.Copy), deps=[t_mm])
    fb_free[0] = t_kc
    t_gu = []
    ui = 0
    for m in range(4):
        for tg in range(4):
            b0 = 1 + (ui % 3); ui += 1
            pq = bk(b0)
            for k in range(8):
                t_mm = S.op('pe', lambda e, k=k, pq=pq, m=m, tg=tg: e.matmul(
                    pq, lhsT=wu[:, k, m * 128:(m + 1) * 128], rhs=hT[:, k, 128 + tg * 512:128 + (tg + 1) * 512],
                    start=(k == 0), stop=(k == 7)), deps=[t_wu, fb_free[b0]] if k == 0 else (), sig=(k == 7))
            t_e = S.op('act', lambda e, pq=pq, m=m, tg=tg: e.activation(
                out=guT[:, m, tg * 512:(tg + 1) * 512], in_=pq, func=AF.Gelu_apprx_tanh), deps=[t_mm])
            fb_free[b0] = t_e
            t_gu.append(t_e)
            if ui in (5, 10, 15):
                p0_compute()
    if debug_stop == 'A2b':
        dbg['kcT'] = (kcT, [128, CTX], BF16)
        dbg['guT'] = (guT.rearrange('p a b -> p (a b)'), [128, 4 * T], BF16)
        return finish(nc, es, S, dbg, [t_kc] + t_gu)
    gvb_free = [None] * 4
    t_v = [None] * NTH
    t_vc = [None] * 2
    t_vn = [None] * NT
    vb_ = [dict() for _ in tiles]

    def is_main(n):
        kindt, ti = tiles[n]
        return kindt == 'x' and 1 <= ti <= NT

    def v_s0(n):
        kindt, ti = tiles[n]
        bv = 4 + (n % 2)
        pv = bk(bv)[:, 0:128]
        for k in range(8):
            lh = hT[:, k, ti * 128:(ti + 1) * 128] if kindt == 'x' else hcT[:, k, ti * 128:(ti + 1) * 128]
            t_mm = S.op('pe', lambda e, k=k, pv=pv, lh=lh: e.matmul(pv, lhsT=lh, rhs=wvvb[:, k, 0:128],
                                                                   start=(k == 0), stop=(k == 7)),
                        deps=[t_wvvb, fb_free[bv]] if k == 0 else (), sig=(k == 7))
        vb_[n]['v'] = t_mm
        if is_main(n):
            mt = ti - 1
            bvb = 6 + (mt % 2)
            pvb = bk(bvb)
            for k in range(8):
                t_mm = S.op('pe', lambda e, k=k, pvb=pvb, ti=ti: e.matmul(
                    pvb, lhsT=hT[:, k, ti * 128:(ti + 1) * 128], rhs=wvvb[:, k, 128:640],
                    start=(k == 0), stop=(k == 7)), deps=[fb_free[bvb]] if k == 0 else (), sig=(k == 7))
            vb_[n]['vb'] = t_mm

    def v_s1(n):
        kindt, ti = tiles[n]
        bv = 4 + (n % 2)
        pv = bk(bv)[:, 0:128]
        dstv = (vaug[:, ti, :, 0:64] if kindt == 'x' else vcaug[:, ti, :, 0:64])
        t_e = S.op('act', lambda e, dstv=dstv, pv=pv: e.activation(
            out=dstv, in_=pv.rearrange('p (a b) -> p a b', a=2), func=AF.Copy), deps=[vb_[n]['v'], t_vones, t_vcones])
        fb_free[bv] = t_e
        if kindt == 'x':
            t_v[ti] = t_e
        else:
            t_vc[ti] = t_e
        if is_main(n):
            mt = ti - 1
            bvb = 6 + (mt % 2)
            g_ = gvb[mt % 4]
            t_ge = S.op('act', lambda e, g_=g_, pvb=bk(bvb): e.activation(out=g_, in_=pvb, func=AF.Gelu_apprx_tanh),
                        deps=[vb_[n]['vb'], gvb_free[mt % 4]])
            fb_free[bvb] = t_ge
            vb_[n]['ge'] = t_ge

    def v_s2(n):
        if not is_main(n):
            return
        mt = tiles[n][1] - 1
        sl = mt % 4
        st = stt[:, sl]; mv = mvr[:, sl]
        g_ = gvb[mt % 4]
        t1 = S.op('dve', lambda e: e.bn_stats(out=st[:, 0:6], in_=g_), deps=[vb_[n]['ge'], st_free[sl]])
        vb_[n]['ag'] = S.op('dve', lambda e: e.bn_aggr(out=mv[:, 0:2], in_=st[:, 0:6]), deps=[t1])

    def v_s3(n):
        if not is_main(n):
            return
        mt = tiles[n][1] - 1
        mv = mvr[:, mt % 4]
        t3 = S.op('pool', lambda e: e.tensor_scalar(out=mv[:, 4:5], in0=mv[:, 1:2], scalar1=EPS, scalar2=None,
                                                    op0=ALU.add), deps=[vb_[n]['ag']])
        vb_[n]['rs'] = S.op('pool', lambda e: e.tensor_tensor(out=mv[:, 2:3], in0=mv[:, 4:5], in1=neghalf[:, 0:1],
                                                              op=ALU.pow), deps=[t3, t_m2])

    def v_s4(n):
        if not is_main(n):
            return
        mt = tiles[n][1] - 1
        mv = mvr[:, mt % 4]
        g_ = gvb[mt % 4]
        tmp_ = rt1[mt % 2]
        t5 = S.op('dve', lambda e: e.scalar_tensor_tensor(
            out=tmp_, in0=g_, scalar=mv[:, 0:1], in1=glg, op0=ALU.subtract, op1=ALU.mult),
            deps=[vb_[n]['rs'], t_gl, rt_free[mt % 2]])
        t6 = S.op('dve', lambda e: e.scalar_tensor_tensor(
            out=vn[:, mt, :], in0=tmp_, scalar=mv[:, 2:3], in1=glb, op0=ALU.mult, op1=ALU.add), deps=[t5])
        rt_free[mt % 2] = t6
        gvb_free[mt % 4] = t6
        st_free[mt % 4] = t6
        t_vn[mt] = t6

    def v_hook(it):
        if it in (5, 12):
            p0_compute(col_bank=0, bc_bank=1)

    run_pipeline([v_s0, v_s1, v_s2, v_s3, v_s4], len(tiles), hook=v_hook)
    while dq['n_done'] < len(dq['blocks']):
        p0_compute(col_bank=0, bc_bank=1)
    t_modc = t_modc01 + t_modc23
    t_g1 = [t_g[(0, 0)], t_g[(0, 1)]]
    t_g2 = [t_g[(1, 0)], t_g[(1, 1)]]

    if debug_stop == 'A2':
        dbg['qT'] = (qT.rearrange('p a b -> p (a b)'), [128, 4 * T], BF16)
        dbg['kT'] = (kT, [128, TH], BF16)
        dbg['kcT'] = (kcT, [128, CTX], BF16)
        dbg['vaug'] = (vaug.rearrange('p a b c -> p (a b c)'), [128, NTH * 2 * 66], BF16)
        dbg['vcaug'] = (vcaug.rearrange('p a b c -> p (a b c)'), [128, 2 * 2 * 66], BF16)
        dbg['guT'] = (guT.rearrange('p a b -> p (a b)'), [128, 4 * T], BF16)
        dbg['vn'] = (vn.rearrange('p a b -> p (a b)'), [128, NT * 512], BF16)
        return finish(nc, es, S, dbg, t_qk_all + [t_kc] + t_gu + t_v + t_vc + t_vn)

    S.barrier()
    o = RD
    wst, _, nb = A.alloc(BF16, [8, 128], at=o); o += nb
    NPT = 20
    PT = []
    for i in range(NPT):
        p_, _, nb = A.alloc(BF16, [4, 128], at=o); o += nb
        PT.append(p_)
    yatm = []
    for i in range(2):
        p_, _, nb = A.alloc(BF16, [512], at=o); o += nb
        yatm.append(p_)
    sbt = []
    for i in range(2):
        p_, _, nb = A.alloc(F32, [512], at=o); o += nb
        sbt.append(p_)
    dens, _, nb = A.alloc(F32, [2, 16], at=o); o += nb
    esk, _, nb = A.alloc(F32, [8], at=o); o += nb
    wab0, _, nb = A.alloc(BF16, [8, 128], at=o); o += nb
    assert o <= RD + RD_SIZE, (o - RD)
    wg0, _, _ = A.alloc(BF16, [16, 128], at=RC + 32 * 1024)
    wc1_pre = {}

    def issue_wc1_slot0():
        wc1_pre['t'] = [pool_cast_load(wab0.rearrange('p a b -> p (a b)'), wc1_d[0][:, 0:1024], 'wc1_0a'),
                        pool_cast_load(wg0.rearrange('p a b -> p (a b)'), wc1_d[0][:, 1024:3072], 'wc1_0b')]
    o = RE_FREE
    woutb, _, nb = A.alloc(BF16, [8, 1024], at=o); o += nb
    wstage = []
    for i in range(2):
        p_, _, nb = A.alloc(F32, [1024], at=o); o += nb
        wstage.append(p_)
    assert o <= ARENA_BYTES, (o, ARENA_BYTES)

    t_wst = pool_cast_load(wst.rearrange('p a b -> p (a b)'), wst_d, 'wst')
    t_esk = S.op('act', lambda e: e.activation(out=esk, in_=esink, func=AF.Exp), deps=[T_CONST])
    ws_free = [None, None]
    wo = {'t': None}

    def wout_step(k):
        r = k % 2
        t_l = S.dma('sp', lambda e, dst=wstage[r], k=k: e.dma_start(out=dst, in_=wout_d[:, k * 1024:(k + 1) * 1024]),
                    'wos%d' % r, deps=[ws_free[r]])
        wo['t'] = S.op('pool', lambda e, k=k, src=wstage[r]: e.tensor_tensor(out=woutb[:, k, :], in0=src, in1=g1bc, op=ALU.mult),
                       deps=[t_l] + t_g1)
        ws_free[r] = wo['t']

    fb_free = [None] * 8
    sbt_free = [None, None]
    t_yb = [None] * NT
    sc_st = {'i': 0}

    def gmlp_tile(t):
        b0 = sc_st['i'] % 4; sc_st['i'] += 1
        pf = bk(b0)
        for c in range(4):
            for gi in range(2):
                g = 2 * c + gi
                outp = pf[gi * 64:(gi + 1) * 64, c * 128:(c + 1) * 128]
                t_mm = S.op('pe', lambda e, outp=outp, g=g, t=t: e.matmul(
                    outp, lhsT=vn[:, t, g * 64:(g + 1) * 64], rhs=wst[:, g, :], start=True, stop=True),
                    deps=[t_wst, fb_free[b0], t_vn[t]] if (c == 0 and gi == 0) else (), sig=(c == 3 and gi == 1))
        r = t % 2
        t_sb = S.op('dve', lambda e, pf=pf, dst=sbt[r]: e.tensor_tensor(out=dst, in0=pf, in1=bsbc, op=ALU.add),
                    deps=[t_mm, T_CONST, sbt_free[r]])
        fb_free[b0] = t_sb
        t_yb[t] = S.op('dve', lambda e, src=sbt[r], t=t: e.tensor_tensor(
            out=ybT[:, :, t * 128:(t + 1) * 128], in0=src.rearrange('p (a b) -> p a b', a=4),
            in1=guT[:, :, t * 128:(t + 1) * 128], op=ALU.mult), deps=[t_sb])
        sbt_free[r] = t_yb[t]

    pt_free = [None] * NPT
    yatm_free = [None, None]
    o_free = [None] * 4
    t_ya = [None] * NT
    att = [dict() for _ in range(NT)]

    def att_srcs(j):
        srcs = []
        for s in range(3):
            kt = j + s
            srcs.append((kT[:, kt * 128:(kt + 1) * 128], vaug[:, kt], [t_v[kt]]))
        for s in range(2):
            srcs.append((kcT[:, s * 128:(s + 1) * 128], vcaug[:, s], [t_vc[s]]))
        return srcs

    def att_front(j):
        base = (j % 2) * 10
        srcs = att_srcs(j)
        t_p = {}
        att[j]['p'] = t_p
        for s, (ksrc, vsrc, vdep) in enumerate(srcs):
            for kv in range(2):
                bi = sc_st['i'] % 4; sc_st['i'] += 1
                psb = bk(bi)
                pt = PT[base + s * 2 + kv]
                t_mm = S.op('pe', lambda e, psb=psb, ksrc=ksrc, kv=kv, j=j: e.matmul(
                    psb, lhsT=ksrc[kv * 64:(kv + 1) * 64, :], rhs=qT[kv * 64:(kv + 1) * 64, :, j * 128:(j + 1) * 128],
                    start=True, stop=True), deps=[fb_free[bi]])
                t_e = S.op('act', lambda e, pt=pt, psb=psb: e.activation(
                    out=pt.rearrange('p a b -> p (a b)'), in_=psb, func=AF.Exp, scale=0.125),
                    deps=[t_mm, pt_free[base + s * 2 + kv]])
                fb_free[bi] = t_e
                if s == 0 or s == 2:
                    mi = (0 if j == 0 else 1) if s == 0 else (3 if j == NT - 1 else 2)
                    t_e = S.op('pool', lambda e, pt=pt, mi=mi: e.tensor_tensor(
                        out=pt, in0=pt, in1=masks[:, mi:mi + 1, :].to_broadcast([128, 4, 128]), op=ALU.mult),
                        deps=[t_e, T_CONST])
                t_p[(s, kv)] = t_e

    def att_back(j):
        base = (j % 2) * 10
        srcs = att_srcs(j)
        t_p = att[j]['p']
        t_o = [None, None]
        for kv in range(2):
            ob = bk(4 + 2 * (j % 2) + kv).rearrange('p (a b) -> p a b', a=4)
            for g in range(4):
                for s, (ksrc, vsrc, vdep) in enumerate(srcs):
                    pt = PT[base + s * 2 + kv]
                    t_mm = S.op('pe', lambda e, ob=ob, g=g, pt=pt, vsrc=vsrc, kv=kv, s=s: e.matmul(
                        ob[:, g, 0:65], lhsT=pt[:, g, :], rhs=vsrc[:, kv, 0:65], start=(s == 0), stop=(s == 4)),
                        deps=[t_p[(s, kv)], o_free[2 * (j % 2) + kv]] + vdep, sig=(g == 3 and s == 4))
            t_o[kv] = t_mm
        for s in range(5):
            for kv in range(2):
                pt_free[base + s * 2 + kv] = t_o[kv]
        r = j % 2
        dn = dens[:, r]
        ya = yatm[r].rearrange('p (k g d) -> p k g d', k=2, g=4)
        t_n = None
        for kv in range(2):
            ob = bk(4 + 2 * (j % 2) + kv).rearrange('p (a b) -> p a b', a=4)
            t_a = S.op('dve', lambda e, ob=ob, dn=dn, kv=kv: e.tensor_tensor(
                out=dn[:, kv * 4:(kv + 1) * 4], in0=ob[:, :, 64], in1=esk[:, kv * 4:(kv + 1) * 4], op=ALU.add),
                deps=[t_o[kv], t_esk, yatm_free[r]])
            t_b = S.op('dve', lambda e, dn=dn, kv=kv: e.reciprocal(out=dn[:, 8 + kv * 4:8 + (kv + 1) * 4],
                                                                  in_=dn[:, kv * 4:(kv + 1) * 4]), deps=[t_a])
            t_n = S.op('dve', lambda e, ob=ob, dn=dn, kv=kv, ya=ya: e.tensor_tensor(
                out=ya[:, kv], in0=ob[:, :, 0:64],
                in1=dn[:, 8 + kv * 4:8 + (kv + 1) * 4].unsqueeze(2).to_broadcast([128, 4, 64]), op=ALU.mult),
                deps=[t_b])
            o_free[2 * (j % 2) + kv] = t_n
        att[j]['n'] = t_n

    def att_tail(j):
        r = j % 2
        t_n = att[j]['n']
        ptb = bkb(4 + 2 * (j % 2))
        for c in range(4):
            t_tr = S.op('pe', lambda e, c=c, ptb=ptb, src=yatm[r]: e.transpose(
                ptb[:, c * 128:(c + 1) * 128], src[:, c * 128:(c + 1) * 128], ident),
                deps=[t_n] if c == 0 else (), sig=(c == 3))
        yatm_free[r] = t_tr
        t_ya[j] = S.op('dve', lambda e, ptb=ptb, j=j: e.tensor_copy(
            out=yaT[:, :, j * 128:(j + 1) * 128], in_=ptb[:, 0:512].rearrange('p (a b) -> p a b', a=4)),
            deps=[t_tr])
        o_free[2 * (j % 2)] = t_ya[j]

    att_front(0)
    for j in range(NT + 1):
        if j + 1 < NT:
            att_front(j + 1)
        if j < NT:
            gmlp_tile(j)
            att_back(j)
            if 2 <= j < 10:
                wout_step(j - 2)
            if j == 10:
                issue_wc1_slot0()
        if j >= 1:
            att_tail(j - 1)

    t_wout = wo['t']
    if debug_stop == 'B':
        dbg['yaT'] = (yaT.rearrange('p a b -> p (a b)'), [128, 4 * T], BF16)
        dbg['ybT'] = (ybT.rearrange('p a b -> p (a b)'), [128, 4 * T], BF16)
        return finish(nc, es, S, dbg, t_ya + t_yb + [t_wout])

    S.barrier()
    o = RB
    wc1 = []
    t_wc1 = [None] * 8
    for m in range(8):
        p_, _, nb = A.alloc(BF16, [24, 128], at=o); o += nb
        wc1.append(p_)
        if m == 0:
            t_wc1[m] = wc1_pre['t']
        else:
            t_wc1[m] = pool_cast_load(p_.rearrange('p a b -> p (a b)'), wc1_d[m], 'wc1_%d' % m)

    def wsel(m, idx):
        if m == 0:
            return wab0[:, idx, :] if idx < 8 else wg0[:, idx - 8, :]
        return wc1[m][:, idx, :]
    outpre = {}
    sga = []
    for i in range(4):
        p_, _, nb = A.alloc(F32, [512], at=o); o += nb
        sga.append(p_)
    mt1 = []
    for i in range(4):
        p_, _, nb = A.alloc(F32, [512], at=o); o += nb
        mt1.append(p_)
    assert o <= RB + RB_SIZE, (o - RB)
    fb_free = [None] * 8
    sga_free = [None] * 4
    mt_free = [None] * 4
    t_merged = {}
    ui = 0
    for m in range(8):
        if m == 4:
            for t_ in range(NT):
                outpre['t'] = S.dma('sp', lambda e, t_=t_: e.dma_start(out=out_d[t_ * 128:(t_ + 1) * 128, :], in_=ln2b_d),
                                    'outpre', deps=[t_wc1[7]])
        wm = wc1[m]
        t_w = t_wc1[m]
        for tg in range(4):
            bs_ = (ui % 2) * 4
            r2 = (ui % 2) * 2
            ui += 1
            pa, pb_, pga, pgb = bk(bs_), bk(bs_ + 1), bk(bs_ + 2), bk(bs_ + 3)
            tok = slice(tg * 512, (tg + 1) * 512)
            htok = slice(128 + tg * 512, 128 + (tg + 1) * 512)
            for k in range(8):
                t_ga = S.op('pe', lambda e, k=k, pga=pga, wm=wm, m=m, htok=htok: e.matmul(
                    pga, lhsT=wsel(m, 8 + k), rhs=hT[:, k, htok], start=(k == 0), stop=(k == 7)),
                    deps=[t_w, fb_free[bs_ + 2]] if k == 0 else (), sig=(k == 7))
            for k in range(8):
                t_gb = S.op('pe', lambda e, k=k, pgb=pgb, wm=wm, m=m, htok=htok: e.matmul(
                    pgb, lhsT=wsel(m, 16 + k), rhs=hT[:, k, htok], start=(k == 0), stop=(k == 7)),
                    deps=[fb_free[bs_ + 3]] if k == 0 else (), sig=(k == 7))
            for k in range(4):
                t_a = S.op('pe', lambda e, k=k, pa=pa, wm=wm, m=m, tok=tok: e.matmul(
                    pa, lhsT=wsel(m, k), rhs=yaT[:, k, tok], start=(k == 0), stop=(k == 3)),
                    deps=[fb_free[bs_]] if k == 0 else (), sig=(k == 3))
            for k in range(4):
                t_b = S.op('pe', lambda e, k=k, pb_=pb_, wm=wm, m=m, tok=tok: e.matmul(
                    pb_, lhsT=wsel(m, 4 + k), rhs=ybT[:, k, tok], start=(k == 0), stop=(k == 3)),
                    deps=[fb_free[bs_ + 1]] if k == 0 else (), sig=(k == 3))
            t_sa = S.op('act', lambda e, dst=sga[r2], pga=pga: e.activation(out=dst, in_=pga, func=AF.Sigmoid),
                        deps=[t_ga, sga_free[r2]])
            t_sb = S.op('act', lambda e, dst=sga[r2 + 1], pgb=pgb: e.activation(out=dst, in_=pgb, func=AF.Sigmoid),
                        deps=[t_gb, sga_free[r2 + 1]])
            fb_free[bs_ + 2] = t_sa
            fb_free[bs_ + 3] = t_sb
            t_1 = S.op('dve', lambda e, dst=mt1[r2], pa=pa, sa=sga[r2]: e.tensor_tensor(out=dst, in0=pa, in1=sa, op=ALU.mult),
                       deps=[t_a, t_sa, mt_free[r2]])
            t_2 = S.op('dve', lambda e, dst=mt1[r2 + 1], pb_=pb_, sb=sga[r2 + 1]: e.tensor_tensor(out=dst, in0=pb_, in1=sb, op=ALU.mult),
                       deps=[t_b, t_sb, mt_free[r2 + 1]])
            fb_free[bs_] = t_1
            fb_free[bs_ + 1] = t_2
            sga_free[r2] = t_1
            sga_free[r2 + 1] = t_2
            t_3 = S.op('pool', lambda e, m=m, tok=tok, a=mt1[r2], b=mt1[r2 + 1]: e.tensor_tensor(
                out=merged[:, m, tok], in0=a, in1=b, op=ALU.add), deps=[t_1, t_2])
            mt_free[r2] = t_3
            mt_free[r2 + 1] = t_3
            t_merged[(m, tg)] = t_3

    if debug_stop == 'C1':
        dbg['merged'] = (merged.rearrange('p a b -> p (a b)'), [128, 8 * T], BF16)
        return finish(nc, es, S, dbg, list(t_merged.values()) + [t_wout])

    S.barrier()
    o = RC
    NXR2, NWK = 2, 4
    xr = []
    for i in range(NXR2):
        x_, _, nb = A.alloc(F32, [D], at=o); o += nb
        xr.append(x_)
    wk = []
    for i in range(NWK):
        x_, _, nb = A.alloc(F32, [D], at=o); o += nb
        wk.append(x_)
    xnb = []
    for i in range(2):
        x_, _, nb = A.alloc(BF16, [D], at=o); o += nb
        xnb.append(x_)
    assert o <= RC + 28 * 1024, (o - RC)
    NWR = 3
    w1 = [None] * NWR
    w2 = [None] * NWR
    w1[0], _, _ = A.alloc(BF16, [8, 512], at=RC + 28 * 1024)
    t_w1_pre = pool_cast_load(w1[0].rearrange('p a b -> p (a b)'), wff1_d[0], 'wf1_0')
    ln1g, ln1b = wstage[0], wstage[1]
    t_ln1 = [sp_load(ln1g, ln1g_d, 'ln1'), sp_load(ln1b, ln1b_d, 'ln1')][-1]

    fb_free = [None] * 8
    xr_free = [None] * NXR2
    wk_free = [None] * NWK
    xnb_free = [None, None]
    t_h2 = [None] * NT
    t_xmid = [None] * NT
    c2 = [dict() for _ in range(NT)]

    def c2_s0(t):
        r = t % NXR2
        c2[t]['x'] = S.dma('sp', lambda e, dst=xr[r], t=t: e.dma_start(out=dst, in_=xh[(t + 1) * 128:(t + 2) * 128, :]),
                           'xr%d' % r, deps=[xr_free[r]])
        b0 = (t % 2) * 2
        c2[t]['mm'] = []
        for half in range(2):
            pm_ = bk(b0 + half)
            for k in range(8):
                t_mm = S.op('pe', lambda e, k=k, pm_=pm_, t=t, half=half: e.matmul(
                    pm_, lhsT=merged[:, k, t * 128:(t + 1) * 128], rhs=woutb[:, k, half * 512:(half + 1) * 512],
                    start=(k == 0), stop=(k == 7)),
                    deps=[t_wout, fb_free[b0 + half]] + [t_merged[(kk, t // 4)] for kk in range(8)] if k == 0 else (),
                    sig=(k == 7))
            c2[t]['mm'].append(t_mm)

    def c2_s1(t):
        r = t % NXR2
        w = wk[t % NWK]
        b0 = (t % 2) * 2
        for half in range(2):
            pm_ = bk(b0 + half)
            t_pre = S.op('dve', lambda e, pm_=pm_, r=r, half=half, w=w: e.scalar_tensor_tensor(
                out=w[:, half * 512:(half + 1) * 512], in0=xr[r][:, half * 512:(half + 1) * 512],
                scalar=ALPHA, in1=pm_, op0=ALU.mult, op1=ALU.add),
                deps=[c2[t]['mm'][half], c2[t]['x'], wk_free[t % NWK]])
            fb_free[b0 + half] = t_pre
        xr_free[r] = t_pre
        sl = t % 4
        st = stt[:, sl]; mv = mvr[:, sl]
        S.op('dve', lambda e: e.bn_stats(out=st[:, 0:6], in_=w[:, 0:512]), deps=[t_pre, st_free[sl]], sig=False)
        t1 = S.op('dve', lambda e: e.bn_stats(out=st[:, 6:12], in_=w[:, 512:1024]))
        c2[t]['ag1'] = S.op('dve', lambda e: e.bn_aggr(out=mv[:, 0:2], in_=st[:, 0:12]), deps=[t1])

    def c2_s2(t):
        mv = mvr[:, t % 4]
        c2[t]['sq1'] = S.op('act', lambda e: e.activation(out=mv[:, 4:5], in_=mv[:, 1:2], func=AF.Sqrt, bias=epst[:, 0:1],
                                                          scale=1.0), deps=[c2[t]['ag1'], t_m3])

    def c2_s3(t):
        _, _, _, c2[t]['r1'] = rstd_part(t % 4, [c2[t]['sq1']])

    def c2_s4(t):
        mv = mvr[:, t % 4]
        w = wk[t % NWK]
        c2[t]['n1'] = S.op('act', lambda e: e.activation(out=w, in_=w, func=AF.Identity, scale=mv[:, 2:3], bias=mv[:, 3:4]),
                           deps=[c2[t]['r1']])
        st_free[t % 4] = c2[t]['n1']

    def c2_s5(t):
        w = wk[t % NWK]
        t_g_ = S.op('pool', lambda e: e.tensor_tensor(out=xmid[:, t, :], in0=w, in1=ln1g, op=ALU.mult),
                    deps=[c2[t]['n1'], t_ln1])
        wk_free[t % NWK] = t_g_
        t_xmid[t] = S.dma('pool', lambda e: e.dma_start(out=xmid[:, t, :], in_=ln1b, accum_op=ALU.add), 'xb%d' % t,
                          deps=[t_g_, t_ln1])

    def c2_s5w(t):
        pass

    def c2_s6(t):
        sl = 4 + t % 4
        st = stt[:, sl]; mv = mvr[:, sl]
        src = xmid[:, t, :]
        S.op('dve', lambda e: e.bn_stats(out=st[:, 0:6], in_=src[:, 0:512]), deps=[t_xmid[t], st_free[sl]], sig=False)
        t1 = S.op('dve', lambda e: e.bn_stats(out=st[:, 6:12], in_=src[:, 512:1024]))
        c2[t]['ag2'] = S.op('dve', lambda e: e.bn_aggr(out=mv[:, 0:2], in_=st[:, 0:12]), deps=[t1])

    def c2_s7(t):
        mv = mvr[:, 4 + t % 4]
        c2[t]['sq2'] = S.op('act', lambda e: e.activation(out=mv[:, 4:5], in_=mv[:, 1:2], func=AF.Sqrt, bias=epst[:, 0:1],
                                                          scale=1.0), deps=[c2[t]['ag2']])

    def c2_s8(t):
        _, _, _, c2[t]['r2'] = rstd_part(4 + t % 4, [c2[t]['sq2']])

    def c2_s9(t):
        mv = mvr[:, 4 + t % 4]
        r = t % 2
        c2[t]['n2'] = S.op('act', lambda e: e.activation(out=xnb[r], in_=xmid[:, t, :], func=AF.Identity,
                                                         scale=mv[:, 2:3], bias=mv[:, 3:4]), deps=[c2[t]['r2'], xnb_free[r]])
        st_free[4 + t % 4] = c2[t]['n2']

    def c2_s10(t):
        r = t % 2
        ptA = bkb(4 + 2 * r)
        ptB = bkb(5 + 2 * r)
        for c in range(8):
            pt = ptA if c < 4 else ptB
            t_tr = S.op('pe', lambda e, c=c, pt=pt, src=xnb[r]: e.transpose(
                pt[:, (c % 4) * 128:(c % 4 + 1) * 128], src[:, c * 128:(c + 1) * 128], ident),
                deps=[c2[t]['n2'], fb_free[4 + 2 * r], fb_free[5 + 2 * r]] if c == 0 else (), sig=(c == 3 or c == 7))
            if c == 3:
                c2[t]['trA'] = t_tr
        c2[t]['trB'] = t_tr
        xnb_free[r] = t_tr

    def c2_s11(t):
        r = t % 2
        ptA = bkb(4 + 2 * r)
        ptB = bkb(5 + 2 * r)
        for c in range(8):
            dst = h2T[:, c, t * 128:(t + 1) * 128]
            pt = ptA if c < 4 else ptB
            t_ev = S.op('act', lambda e, c=c, dst=dst, pt=pt: e.activation(
                out=dst, in_=pt[:, (c % 4) * 128:(c % 4 + 1) * 128], func=AF.Identity,
                scale=mcol(3, c, 0), bias=mcol(2, c, 0)),
                deps=[c2[t]['trA'], c2[t]['trB'], t_modc23] if c == 0 else (), sig=(c == 3 or c == 7))
            if c == 3:
                t_evA = t_ev
        t_evB = t_ev
        fb_free[4 + 2 * r] = t_evA
        fb_free[5 + 2 * r] = t_evB
        t_h2[t] = [t_evA, t_evB]

    run_pipeline([c2_s0, c2_s1, c2_s2, c2_s3, c2_s4, c2_s5, c2_s5w, c2_s6, c2_s7, c2_s8, c2_s9, c2_s10, c2_s11], NT)

    if debug_stop == 'C2':
        dbg['xmid'] = (xmid.rearrange('p a b -> p (a b)'), [128, NT * D], F32)
        dbg['h2T'] = (h2T.rearrange('p a b -> p (a b)'), [128, 8 * T], BF16)
        return finish(nc, es, S, dbg, t_h2 + t_xmid)

    S.barrier()
    o = RC
    for i in range(1, NWR):
        w1[i], _, nb = A.alloc(BF16, [8, 512], at=o); o += nb
    for i in range(NWR):
        w2[i], _, nb = A.alloc(BF16, [FG, 1024], at=o); o += nb
    assert o <= RC + 28 * 1024, (o - RC)
    o = RD
    actb = []
    for i in range(2):
        p_, _, nb = A.alloc(BF16, [FG, T], at=o); o += nb
        actb.append(p_)
    w2st = []
    for i in range(2):
        p_, _, nb = A.alloc(F32, [FG, 1024], at=o); o += nb
        w2st.append(p_)
    assert o <= RD + RD_SIZE, (o - RD)
    o = RE + 64
    sgb = []
    for i in range(2):
        p_, _, nb = A.alloc(F32, [512], at=o); o += nb
        sgb.append(p_)
    ln2g, _, nb = A.alloc(F32, [D], at=o); o += nb
    ln2b, _, nb = A.alloc(F32, [D], at=o); o += nb
    y0 = []
    for i in range(3):
        p_, _, nb = A.alloc(F32, [D], at=o); o += nb
        y0.append(p_)
    assert o <= ARENA_BYTES, (o, ARENA_BYTES)
    t_ln2 = [sp_load(ln2g, ln2g_d, 'ln2'), sp_load(ln2b, ln2b_d, 'ln2')][-1]

    w_free = [None] * NWR
    w2st_free = [None, None]
    t_w1 = [None] * NR
    t_w2 = [None] * NR

    def load_round(r):
        slot = r % NWR
        if r == 0:
            t_w1[r] = t_w1_pre
        else:
            t_w1[r] = pool_cast_load(w1[slot].rearrange('p a b -> p (a b)'), wff1_d[r], 'wf1_%d' % slot, deps=[w_free[slot]])
        s2 = r % 2
        t_l = S.dma('sp', lambda e, dst=w2st[s2], r=r: e.dma_start(out=dst.rearrange('p a b -> p (a b)'), in_=wff2_d[r]),
                    'wf2_%d' % s2, deps=[w2st_free[s2]])
        t_w2[r] = S.op('pool', lambda e, slot=slot, s2=s2: e.tensor_tensor(
            out=w2[slot], in0=w2st[s2], in1=g2bc.unsqueeze(1).to_broadcast([128, FG, 1024]), op=ALU.mult),
            deps=[t_l, w_free[slot]] + t_g2)
        w2st_free[s2] = t_w2[r]

    fb_free = [None] * 8
    sg_free = [None, None]
    act_free = [None, None]
    t_act = {}
    t_acc = [None] * NT
    gu_i = 0

    def emit_gu_unit(r, tg, fi):
        nonlocal gu_i
        slot = r % NWR
        ab = actb[r % 2]
        b0 = (gu_i % 2) * 2
        s_ = gu_i % 2
        gu_i += 1
        pg, pu = bk(b0), bk(b0 + 1)
        tok = slice(tg * 512, (tg + 1) * 512)
        for k in range(8):
            t_g_ = S.op('pe', lambda e, k=k, pg=pg, slot=slot, fi=fi, tok=tok: e.matmul(
                pg, lhsT=w1[slot][:, k, fi * 128:(fi + 1) * 128], rhs=h2T[:, k, tok], start=(k == 0), stop=(k == 7)),
                deps=[t_w1[r], fb_free[b0]] if k == 0 else (), sig=(k == 7))
        for k in range(8):
            t_u_ = S.op('pe', lambda e, k=k, pu=pu, slot=slot, fi=fi, tok=tok: e.matmul(
                pu, lhsT=w1[slot][:, k, 256 + fi * 128:256 + (fi + 1) * 128], rhs=h2T[:, k, tok],
                start=(k == 0), stop=(k == 7)), deps=[fb_free[b0 + 1]] if k == 0 else (), sig=(k == 7))
        t_s = S.op('act', lambda e, dst=sgb[s_], pg=pg: e.activation(out=dst, in_=pg, func=AF.Silu),
                   deps=[t_g_, sg_free[s_]])
        fb_free[b0] = t_s
        t_m = S.op('dve', lambda e, ab=ab, fi=fi, tok=tok, pu=pu, sg=sgb[s_]: e.tensor_tensor(
            out=ab[:, fi, tok], in0=pu, in1=sg, op=ALU.mult), deps=[t_u_, t_s, act_free[r % 2]])
        fb_free[b0 + 1] = t_m
        sg_free[s_] = t_m
        t_act[(r, fi, tg)] = t_m

    def emit_gu_tg(r, tg):
        for fi in range(FG):
            emit_gu_unit(r, tg, fi)

    def emit_gu(r):
        for tg in range(4):
            emit_gu_tg(r, tg)

    def emit_dn_tile(rounds, t):
        b0 = 4 + (t % 2) * 2
        last_mm = None
        for half in range(2):
            pd = bk(b0 + half)
            n_mm = len(rounds) * FG
            i_mm = 0
            for r in rounds:
                slot = r % NWR
                ab = actb[r % 2]
                for fi in range(FG):
                    t_mm = S.op('pe', lambda e, pd=pd, fi=fi, t=t, half=half, ab=ab, slot=slot, i_mm=i_mm, n_mm=n_mm: e.matmul(
                        pd, lhsT=ab[:, fi, t * 128:(t + 1) * 128], rhs=w2[slot][:, fi, half * 512:(half + 1) * 512],
                        start=(i_mm == 0), stop=(i_mm == n_mm - 1)),
                        deps=[t_w2[r], fb_free[b0 + half], t_act[(r, fi, t // 4)]], sig=(i_mm == n_mm - 1))
                    i_mm += 1
            accv = xmid[:, t, half * 512:(half + 1) * 512]
            if rounds[0] == 0:
                t_ad = S.op('dve', lambda e, accv=accv, pd=pd: e.scalar_tensor_tensor(
                    out=accv, in0=accv, scalar=ALPHA, in1=pd, op0=ALU.mult, op1=ALU.add), deps=[t_mm, t_acc[t]])
            else:
                t_ad = S.op('dve', lambda e, accv=accv, pd=pd: e.tensor_tensor(
                    out=accv, in0=pd, in1=accv, op=ALU.add), deps=[t_mm, t_acc[t]])
            fb_free[b0 + half] = t_ad
            t_acc[t] = t_ad
            last_mm = t_mm
        return last_mm

    def emit_dn(r):
        last_mm = None
        for t in range(NT):
            last_mm = emit_dn_tile([r], t)
        act_free[r % 2] = last_mm
        w_free[r % NWR] = last_mm

    y_free = [None] * 3
    t_out = []
    tl = [dict() for _ in range(NT)]

    def tail_s1(t):
        sl = t % 4
        st = stt[:, sl]; mv = mvr[:, sl]
        src = xmid[:, t, :]
        S.op('dve', lambda e: e.bn_stats(out=st[:, 0:6], in_=src[:, 0:512]), deps=[t_acc[t], st_free[sl]], sig=False)
        t1 = S.op('dve', lambda e: e.bn_stats(out=st[:, 6:12], in_=src[:, 512:1024]))
        tl[t]['ag'] = S.op('dve', lambda e: e.bn_aggr(out=mv[:, 0:2], in_=st[:, 0:12]), deps=[t1])

    def tail_s2(t):
        mv = mvr[:, t % 4]
        tl[t]['sq'] = S.op('act', lambda e: e.activation(out=mv[:, 4:5], in_=mv[:, 1:2], func=AF.Sqrt, bias=epst[:, 0:1],
                                                         scale=1.0), deps=[tl[t]['ag']])

    def tail_s3(t):
        _, _, _, tl[t]['r'] = rstd_part(t % 4, [tl[t]['sq']])

    def tail_s4(t):
        mv = mvr[:, t % 4]
        r3 = t % 3
        tl[t]['n'] = S.op('act', lambda e: e.activation(out=y0[r3], in_=xmid[:, t, :], func=AF.Identity,
                                                        scale=mv[:, 2:3], bias=mv[:, 3:4]), deps=[tl[t]['r'], y_free[r3]])
        st_free[t % 4] = tl[t]['n']

    def tail_s5(t):
        r3 = t % 3
        tl[t]['g'] = S.op('pool', lambda e: e.tensor_tensor(out=y0[r3], in0=y0[r3], in1=ln2g, op=ALU.mult),
                          deps=[tl[t]['n'], t_ln2])

    def tail_s6(t):
        r3 = t % 3
        t_st_ = S.dma('pool', lambda e, t=t: e.dma_start(out=out_d[t * 128:(t + 1) * 128, :], in_=y0[r3], accum_op=ALU.add),
                      'out%d' % r3, deps=[tl[t]['g'], outpre['t']])
        y_free[r3] = t_st_
        t_out.append(t_st_)

    tail_stages = [tail_s1, tail_s2, tail_s3, tail_s4, tail_s5, tail_s6]
    tail_state = {'n': 0}

    def tail_step(t_new):
        it = tail_state['n']; tail_state['n'] += 1
        K = len(tail_stages)
        for k in reversed(range(K)):
            i = it - k
            if 0 <= i < NT and (t_new is not None or True):
                if i <= (t_new if t_new is not None else NT - 1):
                    tail_stages[k](i)

    for r in range(min(NWR, NR)):
        load_round(r)
    R1, R2 = NR - 2, NR - 1
    emit_gu(0)
    for r in range(NR - 2):
        if r + 1 < NR - 2:
            gu_units = [(tg, fi) for tg in range(4) for fi in range(FG)]
            per = NT // len(gu_units)
            last_mm = None
            for i_, (tg, fi) in enumerate(gu_units):
                emit_gu_unit(r + 1, tg, fi)
                for t in range(i_ * per, (i_ + 1) * per):
                    last_mm = emit_dn_tile([r], t)
            act_free[r % 2] = last_mm
            w_free[r % NWR] = last_mm
        else:
            emit_gu_tg(R1, 0)
            emit_dn(r)
        if r + NWR < NR:
            load_round(r + NWR)
    emit_gu_tg(R2, 0)
    order = [('GU', 1), ('DN', 0), ('GU', 2), ('DN', 1), ('GU', 3), ('DN', 2), ('DN', 3)]
    for kind_, tg in order:
        if kind_ == 'GU':
            emit_gu_tg(R1, tg)
            emit_gu_tg(R2, tg)
        else:
            for t in range(4 * tg, 4 * tg + 4):
                emit_dn_tile([R1, R2], t)
                tail_step(t)
    for _ in range(len(tail_stages)):
        tail_step(None)
    assert len(t_out) == NT
    return finish(nc, es, S, dbg, t_out)


def finish(nc, es, S, dbg, final_ticks):
    dbg_ticks = []
    for name, spec in dbg.items():
        ap, shape = spec[0], spec[1]
        dt_ = spec[2] if len(spec) > 2 else F32
        d = nc.dram_tensor("dbg_" + name, list(shape), dt_, kind="ExternalOutput").ap()
        dbg_ticks.append(S.dma('sp', lambda e, d=d, ap=ap: e.dma_start(out=d, in_=ap), 'dbg', deps=final_ticks))
    S.barrier()
    S.emit(nc, es)
    es.close()
    return nc


def _rope_tables(start):
    pos = np.arange(start - 128, start - 128 + TH)
    rows = (pos // 64).astype(np.float64)
    cols = (pos % 64).astype(np.float64)
    inv = 10000.0 ** (-np.arange(16, dtype=np.float64) / 16)
    C = np.zeros((64, TH), np.float64)
    Sg = np.zeros((64, TH), np.float64)
    for d in range(64):
        p_ = rows if d < 32 else cols
        dd = d % 32
        i = dd % 16
        ang = p_ * inv[i]
        C[d] = np.cos(ang)
        Sg[d] = -np.sin(ang) if dd < 16 else np.sin(ang)
    C = C.astype(np.float32)
    Sg = Sg.astype(np.float32)
    return np.concatenate([C, C], 0), np.concatenate([Sg, Sg], 0)


def _perm_matrix():
    P = np.zeros((128, 128), np.float32)
    for m in range(128):
        d = m % 64
        dd = d % 32
        partner = d + 16 if dd < 16 else d - 16
        P[(m // 64) * 64 + partner, m] = 1.0
    return P


def prep_inputs(inp):
    f = lambda a: np.ascontiguousarray(np.asarray(a, dtype=np.float32))
    x = f(inp['x']); c = f(inp['c']); ctx = f(inp['ctx']); c_ctx = f(inp['c_ctx'])
    w_ada = f(inp['w_ada'])[0]; b_ada = f(inp['b_ada'])[0]; w_in = f(inp['w_in'])[0]
    sink = f(inp['attn_sink'])[0]
    glg = f(inp['gmlp_ln_g'])[0]; glb = f(inp['gmlp_ln_b'])[0]
    w_s = f(inp['w_spatial'])[0]; b_s = f(inp['b_spatial'])[0]
    w_a = f(inp['w_branch_a'])[0]; w_b = f(inp['w_branch_b'])[0]; w_out = f(inp['w_out'])[0]
    ln1g = f(inp['ln1_g'])[0]; ln1b = f(inp['ln1_b'])[0]; ln2g = f(inp['ln2_g'])[0]; ln2b = f(inp['ln2_b'])[0]
    w_ffn_in = f(inp['w_ffn_in'])[0]; w_ffn_out = f(inp['w_ffn_out'])[0]

    def ktile(w):
        n = w.shape[1]
        return np.ascontiguousarray(w.reshape(8, 128, n).transpose(1, 0, 2)).reshape(128, 8 * n)

    wada_t = np.ascontiguousarray(w_ada.reshape(8, 128, 12, 512).transpose(2, 1, 0, 3)).reshape(12, 128, 4096)
    qcols = []
    for cc in range(4):
        qcols += list(range(cc * 64, cc * 64 + 64)) + list(range((4 + cc) * 64, (4 + cc) * 64 + 64))
    qkcols = qcols + list(range(512, 640))
    wqk = ktile(w_in[:, qkcols])
    wu = ktile(w_in[:, 768:1280])
    wvvb = ktile(w_in[:, list(range(640, 768)) + list(range(1280, 1792))])
    wga = w_in[:, 1792:2816]
    wgb = w_in[:, 2816:3840]
    wc1 = np.zeros((8, 128, 24, 128), np.float32)
    for m in range(8):
        cs = slice(m * 128, (m + 1) * 128)
        wc1[m, :, 0:4] = w_a[:, cs].reshape(4, 128, 128).transpose(1, 0, 2)
        wc1[m, :, 4:8] = w_b[:, cs].reshape(4, 128, 128).transpose(1, 0, 2)
        wc1[m, :, 8:16] = wga[:, cs].reshape(8, 128, 128).transpose(1, 0, 2)
        wc1[m, :, 16:24] = wgb[:, cs].reshape(8, 128, 128).transpose(1, 0, 2)
    wc1 = wc1.reshape(8, 128, 24 * 128)
    wout_t = ktile(w_out)
    wst = np.ascontiguousarray(w_s.transpose(2, 0, 1)).reshape(128, 8 * 128)
    bsbc = np.zeros((128, 4, 128), np.float32)
    for cc in range(4):
        for gi in range(2):
            bsbc[gi * 64:(gi + 1) * 64, cc, :] = b_s[2 * cc + gi][None, :]
    bsbc = bsbc.reshape(128, 512)
    badac = np.zeros((128, 4, 8, 2), np.float32)
    for kind, off in enumerate((0, 1024, 3072, 4096)):
        badac[:, kind, :, :] = b_ada[off:off + 1024].reshape(8, 128).T[:, :, None]
    badac = badac.reshape(128, 64)
    badabc = np.ascontiguousarray(np.broadcast_to(
        np.concatenate([b_ada[2048:3072], b_ada[5120:6144]])[None, :], (128, 2048)))
    wff1 = np.zeros((NR, 128, 8, 512), np.float32)
    wff2 = np.zeros((NR, 128, FG, 1024), np.float32)
    for r in range(NR):
        for fi in range(FG):
            fch = r * FG + fi
            wff1[r, :, :, fi * 128:(fi + 1) * 128] = w_ffn_in[:, fch * 128:(fch + 1) * 128].reshape(8, 128, 128).transpose(1, 0, 2)
            wff1[r, :, :, 256 + fi * 128:256 + (fi + 1) * 128] = \
                w_ffn_in[:, FFH + fch * 128:FFH + (fch + 1) * 128].reshape(8, 128, 128).transpose(1, 0, 2)
            wff2[r, :, fi, :] = w_ffn_out[fch * 128:(fch + 1) * 128, :]
    wff1 = wff1.reshape(NR, 128, 4096)
    wff2 = wff2.reshape(NR, 128, FG * 1024)
    ident = np.eye(128, dtype=np.float32)
    perm = _perm_matrix()
    ki = np.arange(128)[:, None]
    qi = np.arange(128)[None, :]
    maskP = (ki >= qi).astype(np.float32)
    maskN = (ki <= qi).astype(np.float32)
    zero = np.zeros((128, 128), np.float32)
    bc = lambda v: np.ascontiguousarray(np.broadcast_to(v[None, :], (128, v.shape[0])))
    shared = dict(wada=wada_t, badac=badac, badabc=badabc, wqk=wqk, wu=wu, wvvb=wvvb, wc1=wc1, wout=wout_t,
                  wst=wst, bsbc=bsbc, wff1=wff1, wff2=wff2, ident=ident, perm=perm, esink=bc(sink),
                  glg=bc(glg), glb=bc(glb), ln1g=bc(ln1g), ln1b=bc(ln1b), ln2g=bc(ln2g), ln2b=bc(ln2b))
    in_maps = []
    for core in range(NCORES):
        b = core // 4
        seg = core % 4
        start = seg * T
        xhalo = np.zeros((TH, D), np.float32)
        lo = max(start - 128, 0)
        hi = min(start + T + 128, SEQ)
        xhalo[lo - (start - 128): hi - (start - 128)] = x[b, lo:hi]
        cv = np.zeros((128, 16), np.float32)
        cv[:, 0:8] = c[b].reshape(8, 128).T
        cv[:, 8:16] = c_ctx.reshape(8, 128).T
        rc, rs = _rope_tables(start)
        mk = np.stack([zero if seg == 0 else maskP, maskP, maskN, zero if seg == 3 else maskN], 1).reshape(128, 512)
        m = dict(shared)
        m.update(xh=xhalo, ctxb=np.ascontiguousarray(ctx[b]), cvec=cv, ropec=rc, ropes=rs, masks=np.ascontiguousarray(mk))
        in_maps.append(m)
    return in_maps


_NC_CACHE = {}


def kernel(**inputs):
    in_maps = prep_inputs(inputs)
    if 'nc' not in _NC_CACHE:
        _NC_CACHE['nc'] = build_nc(None)
    nc = _NC_CACHE['nc']
    res = run_bass_kernel_spmd(nc, in_maps, core_ids=list(range(NCORES)))
    out = np.zeros((2, SEQ, D), np.float32)
    for core in range(NCORES):
        b = core // 4
        seg = core % 4
        out[b, seg * T:(seg + 1) * T] = res.results[core]["out"]
    return out
```

```python
import numpy as np
from contextlib import ExitStack
import concourse.bass as bass
import concourse.mybir as mybir
from concourse.bass_utils import run_bass_kernel_spmd

F32 = mybir.dt.float32
BF16 = mybir.dt.bfloat16
AF = mybir.ActivationFunctionType
ALU = mybir.AluOpType

NCORES = 8
D = 1024
T = 2048
NT = 16
TH = 2304
NTH = 18
CTX = 256
FFH = 2816
NF = 22
FG = 2
NR = NF // FG
ALPHA = 2.0 ** 0.25
EPS = 1e-5
SEQ = 8192
IN_SPLITS = (512, 640, 768, 1280, 1792, 2816, 3840)

DEBUG_STOP = None


class Sched:
    ENG = ('pe', 'act', 'dve', 'pool', 'sp')

    def __init__(self):
        self.q = {e: [] for e in self.ENG}
        self.cnt = {e: 0 for e in self.ENG}
        self.waited = {e: {} for e in self.ENG}
        self.dmacnt = {}

    def _resolve(self, eng, deps):
        ws = []
        stack = [deps]
        flat = []
        while stack:
            d = stack.pop()
            if d is None:
                continue
            if isinstance(d, tuple) and len(d) == 2 and isinstance(d[0], str):
                flat.append(d)
            else:
                stack.extend(list(d))
        for (p, t) in flat:
            if self.waited[eng].get(p, 0) >= t:
                continue
            self.waited[eng][p] = t
            ws.append((p, t))
        return ws

    def op(self, eng, fn, deps=(), sig=True):
        ws = self._resolve(eng, deps)
        tick = None
        if sig:
            self.cnt[eng] += 1
            tick = (eng, self.cnt[eng])
        self.q[eng].append((ws, fn, eng if sig else None, 1))
        return tick

    def dma(self, eng, fn, key, deps=()):
        ws = self._resolve(eng, deps)
        self.dmacnt[key] = self.dmacnt.get(key, 0) + 16
        self.q[eng].append((ws, fn, 'dma:' + key, 16))
        return ('dma:' + key, self.dmacnt[key])

    def barrier(self):
        ticks = [(e, c) for e, c in self.cnt.items() if c > 0]
        ticks += [('dma:' + k, c) for k, c in self.dmacnt.items()]
        for e in self.ENG:
            ws = self._resolve(e, ticks)
            if ws:
                self.q[e].append((ws, None, None, 0))

    def emit(self, nc, es):
        sems = {}
        for e in self.ENG:
            sems[e] = es.enter_context(nc.semaphore("s_" + e))
        for k in self.dmacnt:
            sems['dma:' + k] = es.enter_context(nc.semaphore("d_" + k))
        block = es.enter_context(nc.Block())
        reg = {'pe': block.tensor, 'act': block.scalar, 'dve': block.vector,
               'pool': block.gpsimd, 'sp': block.sync}
        for e in self.ENG:
            items = self.q[e]

            def body(engine, items=items):
                for (ws, fn, sigkey, inc) in items:
                    for (p, t) in ws:
                        engine.wait_ge(sems[p], t)
                    if fn is None:
                        continue
                    ins = fn(engine)
                    if sigkey is not None:
                        ins.then_inc(sems[sigkey], inc)
            reg[e](body)


class Arena:
    def __init__(self, nc, es, nbytes):
        self.t = es.enter_context(nc.sbuf_tensor("arena", [128, nbytes // 2], BF16))
        self.nbytes = nbytes
        self.top = 0
        self.marks = {}

    def alloc(self, dtype, shape, at=None):
        esz = 4 if dtype == F32 else 2
        n = int(np.prod(shape))
        nb = (n * esz + 63) // 64 * 64
        if at is None:
            off = self.top
            self.top += nb
        else:
            off = at
        assert off + nb <= self.nbytes, ("arena overflow", off, nb, self.nbytes)
        a = self.t[:, off // 2: off // 2 + n * esz // 2]
        if dtype == F32:
            a = a.bitcast(F32)
        if len(shape) == 2:
            a = a.rearrange('p (a b) -> p a b', a=shape[0])
        elif len(shape) == 3:
            a = a.rearrange('p (a b c) -> p a b c', a=shape[0], b=shape[1])
        return a, off, nb


def run_pipeline(stages, n_items, hook=None):
    K = len(stages)
    for it in range(n_items + K - 1):
        for k in reversed(range(K)):
            i = it - k
            if 0 <= i < n_items:
                stages[k](i)
        if hook is not None:
            hook(it)


def build_nc(debug_stop=None):
    nc = bass.Bass("TRN2", target_bir_lowering=False)
    es = ExitStack()
    S = Sched()

    def din(name, shape):
        return nc.dram_tensor(name, list(shape), F32, kind="ExternalInput").ap()

    xh = din("xh", [TH, D])
    ctxb = din("ctxb", [CTX, D])
    cvec = din("cvec", [128, 16])
    wada = din("wada", [12, 128, 8 * 512])
    badac_d = din("badac", [128, 64])
    badabc_d = din("badabc", [128, 2048])
    wqk_d = din("wqk", [128, 8 * 640])
    wu_d = din("wu", [128, 8 * 512])
    wvvb_d = din("wvvb", [128, 8 * 640])
    wc1_d = din("wc1", [8, 128, 24 * 128])
    wout_d = din("wout", [128, 8 * 1024])
    wst_d = din("wst", [128, 8 * 128])
    bsbc_d = din("bsbc", [128, 512])
    wff1_d = din("wff1", [NR, 128, 8 * 512])
    wff2_d = din("wff2", [NR, 128, FG * 1024])
    ropec_d = din("ropec", [128, TH])
    ropes_d = din("ropes", [128, TH])
    masks_d = din("masks", [128, 4 * 128])
    ident_d = din("ident", [128, 128])
    perm_d = din("perm", [128, 128])
    esink_d = din("esink", [128, 8])
    glg_d = din("glg", [128, 512])
    glb_d = din("glb", [128, 512])
    ln1g_d = din("ln1g", [128, D])
    ln1b_d = din("ln1b", [128, D])
    ln2g_d = din("ln2g", [128, D])
    ln2b_d = din("ln2b", [128, D])
    out_d = nc.dram_tensor("out", [T, D], F32, kind="ExternalOutput").ap()
    dbg = {}

    ARENA_BYTES = 207 * 1024
    A = Arena(nc, es, ARENA_BYTES)
    banks = [es.enter_context(nc.psum_tensor("bank%d" % i, [128, 512], F32)) for i in range(8)]

    def bk(i):
        return banks[i][:, :]

    def bkb(i):
        return banks[i][:, :].bitcast(BF16)

    ident, _, _ = A.alloc(BF16, [128])
    perm, _, _ = A.alloc(BF16, [128])
    masks, _, _ = A.alloc(BF16, [4, 128])
    esink, _, _ = A.alloc(F32, [8])
    modc, _, _ = A.alloc(F32, [64])
    sc, _, _ = A.alloc(BF16, [16])
    small, _, _ = A.alloc(F32, [64])
    onesf, _, _ = A.alloc(F32, [128])
    neghalf, _, _ = A.alloc(F32, [32])
    bsbc, _, _ = A.alloc(F32, [512])
    g2bc, _, _ = A.alloc(F32, [1024])
    stt, _, _ = A.alloc(F32, [8, 16])
    mvr, _, _ = A.alloc(F32, [8, 8])
    st_free = [None] * 8
    epst, _, _ = A.alloc(F32, [16])
    CONST_END = A.top

    RA = A.top
    hT, _, nbA = A.alloc(BF16, [8, TH])
    RB = A.top
    RB_SIZE = 64 * 1024
    A.top = RB + RB_SIZE
    RC = A.top
    RC_SIZE = 36 * 1024
    A.top = RC + RC_SIZE
    RD = A.top
    RD_SIZE = 32 * 1024
    A.top = RD + RD_SIZE
    RE = A.top
    RE_SIZE = ARENA_BYTES - RE
    assert RE_SIZE >= 24 * 1024, RE_SIZE

    o = RB
    qT, _, nb = A.alloc(BF16, [4, T], at=o); o += nb
    kT, _, nb = A.alloc(BF16, [TH], at=o); o += nb
    kcT, _, nb = A.alloc(BF16, [CTX], at=o); o += nb
    vaug, _, nb = A.alloc(BF16, [NTH, 2, 66], at=o); o += nb
    vcaug, _, nb = A.alloc(BF16, [2, 2, 66], at=o); o += nb
    guT, _, nb = A.alloc(BF16, [4, T], at=o); o += nb
    vn, _, nb = A.alloc(BF16, [NT, 512], at=o); o += nb
    assert o <= RB + RB_SIZE, (o - RB)
    xmid, _, _ = A.alloc(F32, [NT, D], at=RB)
    h2T, _, _ = A.alloc(BF16, [8, T], at=RA)
    hcT_off = None
    yaT, _, nb1 = A.alloc(BF16, [4, T], at=RC)
    ybT, _, nb2 = A.alloc(BF16, [4, T], at=RC + nb1)
    merged, _, _ = A.alloc(BF16, [8, T], at=RD)

    cdeps = []

    def sp_load(dst, src, key='const'):
        return S.dma('sp', lambda e, dst=dst, src=src: e.dma_start(out=dst, in_=src), key)

    def pool_cast_load(dst, src, key, deps=()):
        return S.dma('pool', lambda e, dst=dst, src=src: e.dma_start(out=dst, in_=src, max_dma_last_dim=4096), key, deps)

    cvec_f, _, _ = A.alloc(F32, [16], at=RE)
    badac, _, _ = A.alloc(F32, [64], at=RE + 64)
    t_cvec = sp_load(cvec_f, cvec, 'cvec')
    t_badac = sp_load(badac, badac_d, 'badac')
    t_c = [sp_load(esink, esink_d), sp_load(bsbc, bsbc_d), ]
    t_cb = [pool_cast_load(ident, ident_d, 'cb'), pool_cast_load(perm, perm_d, 'cb'),
            pool_cast_load(masks.rearrange('p a b -> p (a b)'), masks_d, 'cb')]
    t_m1 = S.op('pool', lambda e: e.memset(onesf, 1.0))
    t_m2 = S.op('pool', lambda e: e.memset(neghalf, -0.5))
    t_m3 = S.op('pool', lambda e: e.memset(epst, EPS))
    T_CONST = [t_c[-1], t_cb[-1], t_m1, t_m2]

    o = RC
    wqk, _, nb = A.alloc(BF16, [8, 640], at=o); o += nb
    wu, _, nb = A.alloc(BF16, [8, 512], at=o); o += nb
    wvvb, _, nb = A.alloc(BF16, [8, 640], at=o); o += nb
    hcT, _, nb = A.alloc(BF16, [8, CTX], at=o); o += nb
    scb, _, nb = A.alloc(BF16, [8, 128], at=o); o += nb
    bslice, _, nb = A.alloc(F32, [512], at=o); o += nb
    assert o <= RC + RC_SIZE, (o - RC)

    o = RB
    wring0 = []
    for i in range(4):
        w_, _, nb = A.alloc(BF16, [8, 512], at=o); o += nb
        wring0.append(w_)
    assert o <= RB + RB_SIZE, o
    g1bc, _, nb = A.alloc(F32, [1024], at=RE + 384)
    RE_FREE = RE + 384 + nb
    wring1 = []
    o = RE_FREE + 8192
    for i in range(2):
        w_, _, nb = A.alloc(BF16, [8, 512], at=o); o += nb
        wring1.append(w_)
    assert o <= ARENA_BYTES, (o, ARENA_BYTES)

    t_sc = S.op('act', lambda e: e.activation(out=sc, in_=cvec_f, func=AF.Silu), deps=[t_cvec])
    t_scb = S.op('dve', lambda e: e.tensor_copy(out=scb, in_=sc[:, 0:8].unsqueeze(2).to_broadcast([128, 8, 128])),
                 deps=[t_sc])
    colkind = {0: 0, 1: 0, 2: 1, 3: 1, 6: 2, 7: 2, 8: 3, 9: 3}
    bcform = {4: (0, 0), 5: (0, 1), 10: (1, 0), 11: (1, 1)}
    t_g = {}
    modc4 = modc.rearrange('p (a b c) -> p a b c', a=4, b=8)

    def p0_col(blk, wb_, t_w, pm):
        kind = colkind[blk]
        t_l = None
        for s in range(4):
            chunk = (blk % 2) * 4 + s
            col = (kind * 8 + chunk) * 2
            for k in range(8):
                t_l = S.op('pe', lambda e, k=k, s=s, col=col, wb_=wb_, pm=pm: e.matmul(
                    pm[:, col:col + 2], lhsT=wb_[:, k, s * 128:(s + 1) * 128], rhs=sc[:, k:16:8],
                    start=(k == 0), stop=(k == 7)), deps=[t_w, t_sc] if k == 0 else (),
                    sig=(k == 7 and s == 3))
        return t_l

    def modc_evac(lo, hi, plus1, pm, dep):
        t0_ = S.op('dve', lambda e: e.tensor_tensor(out=modc[:, lo:hi], in0=pm[:, lo:hi], in1=badac[:, lo:hi], op=ALU.add),
                   deps=[dep, t_badac])
        if plus1:
            t0_ = S.op('dve', lambda e: e.tensor_scalar(out=modc[:, lo:hi], in0=modc[:, lo:hi], scalar1=1.0,
                                                        scalar2=None, op0=ALU.add), deps=[t0_])
        return t0_

    t_l = None
    t_modc01_A, t_modc01_B = [], []
    for blk in (0, 2, 1, 3):
        t_w = pool_cast_load(wring0[blk].rearrange('p a b -> p (a b)'), wada[blk], 'wada_e%d' % blk)
        pm_ = bk(0) if blk in (0, 2) else bk(1)
        t_l = p0_col(blk, wring0[blk], t_w, pm_)
        if blk == 2:
            t_modc01_A = [modc_evac(0, 8, False, bk(0), t_l), modc_evac(16, 24, True, bk(0), t_l)]
        if blk == 3:
            t_modc01_B = [modc_evac(8, 16, False, bk(1), t_l), modc_evac(24, 32, True, bk(1), t_l)]
    t_modc01 = t_modc01_A + t_modc01_B
    t_wqk = pool_cast_load(wqk.rearrange('p a b -> p (a b)'), wqk_d, 'wqk')
    t_wvvb = pool_cast_load(wvvb.rearrange('p a b -> p (a b)'), wvvb_d, 'wvvb')
    t_wu = pool_cast_load(wu.rearrange('p a b -> p (a b)'), wu_d, 'wu')

    def mcol(kind, chunk, which):
        return modc4[:, kind, chunk, which:which + 1]

    def stats_part(src, slot, deps, n=1024):
        st = stt[:, slot]
        mv = mvr[:, slot]
        if n == 1024:
            S.op('dve', lambda e: e.bn_stats(out=st[:, 0:6], in_=src[:, 0:512]), deps=[deps, st_free[slot]], sig=False)
            t1 = S.op('dve', lambda e: e.bn_stats(out=st[:, 6:12], in_=src[:, 512:1024]))
            t2 = S.op('dve', lambda e: e.bn_aggr(out=mv[:, 0:2], in_=st[:, 0:12]), deps=[t1])
        else:
            t1 = S.op('dve', lambda e: e.bn_stats(out=st[:, 0:6], in_=src), deps=[deps, st_free[slot]])
            t2 = S.op('dve', lambda e: e.bn_aggr(out=mv[:, 0:2], in_=st[:, 0:6]), deps=[t1])
        return S.op('act', lambda e: e.activation(out=mv[:, 4:5], in_=mv[:, 1:2], func=AF.Sqrt, bias=epst[:, 0:1], scale=1.0),
                    deps=[t2, t_m3])

    def rstd_part(slot, deps):
        mv = mvr[:, slot]
        t4 = S.op('dve', lambda e: e.reciprocal(out=mv[:, 2:3], in_=mv[:, 4:5]), deps=[deps])
        t5 = S.op('dve', lambda e: e.scalar_tensor_tensor(out=mv[:, 3:4], in0=mv[:, 0:1], scalar=-1.0, in1=mv[:, 2:3],
                                                          op0=ALU.mult, op1=ALU.mult), deps=[t4])
        return mv[:, 0:1], mv[:, 2:3], mv[:, 3:4], t5

    o = RD
    NXR = 6
    xr = []
    for i in range(NXR):
        x_, _, nb = A.alloc(F32, [D], at=o); o += nb
        xr.append(x_)
    xnb = []
    for i in range(2):
        x_, _, nb = A.alloc(BF16, [D], at=o); o += nb
        xnb.append(x_)
    assert o <= RD + RD_SIZE, (o - RD)

    xr_free = [None] * NXR
    xnb_free = [None, None]
    bankA_free = [None, None]
    bankB_free = [None, None]
    t_hT = [None] * NTH
    t_hcT = [None] * 2
    tiles = [('x', i) for i in range(NTH)] + [('c', i) for i in range(2)]
    NA1 = len(tiles)
    a1 = {}
    for it in range(NA1 + 2):
        if it < NA1:
            n = it
            kindt, ti = tiles[n]
            xs = n % NXR
            src = xh[ti * 128:(ti + 1) * 128, :] if kindt == 'x' else ctxb[ti * 128:(ti + 1) * 128, :]
            t_ld = S.dma('sp', lambda e, dst=xr[xs], src=src: e.dma_start(out=dst, in_=src), 'xr%d' % xs,
                         deps=[xr_free[xs]])
            a1[n] = {'sq': stats_part(xr[xs], n % 4, [t_ld])}
        if 0 <= it - 1 < NA1:
            n = it - 1
            kindt, ti = tiles[n]
            xs = n % NXR
            bs_ = n % 2
            mean, rstd, nmr, t_r = rstd_part(n % 4, [a1[n]['sq']])
            t_xn = S.op('act', lambda e, dst=xnb[bs_], src=xr[xs], rstd=rstd, nmr=nmr: e.activation(
                out=dst, in_=src, func=AF.Identity, scale=rstd, bias=nmr), deps=[t_r, xnb_free[bs_]])
            xr_free[xs] = t_xn
            st_free[n % 4] = t_xn
            ptA = bkb(4 + 2 * bs_)
            ptB = bkb(5 + 2 * bs_)
            for c in range(8):
                pt = ptA if c < 4 else ptB
                t_tr = S.op('pe', lambda e, c=c, pt=pt, src=xnb[bs_]: e.transpose(
                    pt[:, (c % 4) * 128:(c % 4 + 1) * 128], src[:, c * 128:(c + 1) * 128], ident),
                    deps=[t_xn, bankA_free[bs_], bankB_free[bs_], T_CONST] if c == 0 else (), sig=(c == 3 or c == 7))
                if c == 3:
                    a1[n]['trA'] = t_tr
            a1[n]['trB'] = t_tr
            xnb_free[bs_] = t_tr
        if 0 <= it - 2 < NA1:
            n = it - 2
            kindt, ti = tiles[n]
            bs_ = n % 2
            ptA = bkb(4 + 2 * bs_)
            ptB = bkb(5 + 2 * bs_)
            which = 0 if kindt == 'x' else 1
            for c in range(8):
                dst = hT[:, c, ti * 128:(ti + 1) * 128] if kindt == 'x' else hcT[:, c, ti * 128:(ti + 1) * 128]
                if c < 4:
                    t_evA = S.op('act', lambda e, c=c, dst=dst, ptA=ptA, which=which: e.activation(
                        out=dst, in_=ptA[:, c * 128:(c + 1) * 128], func=AF.Identity,
                        scale=mcol(1, c, which), bias=mcol(0, c, which)),
                        deps=[a1[n]['trA'], t_modc01_A] if c == 0 else (), sig=(c == 3))
                else:
                    t_evB = S.op('dve', lambda e, c=c, dst=dst, ptB=ptB, which=which: e.tensor_scalar(
                        out=dst, in0=ptB[:, (c - 4) * 128:(c - 3) * 128], scalar1=mcol(1, c, which),
                        scalar2=mcol(0, c, which), op0=ALU.mult, op1=ALU.add),
                        deps=[a1[n]['trB'], t_modc01_B] if c == 4 else (), sig=(c == 7))
            bankA_free[bs_] = t_evA
            bankB_free[bs_] = t_evB
            if kindt == 'x':
                t_hT[ti] = [t_evA, t_evB]
            else:
                t_hcT[ti] = [t_evA, t_evB]
    if debug_stop == 'A1':
        dbg['hT'] = (hT.rearrange('p a b -> p (a b)'), [128, 8 * TH], BF16)
        dbg['hcT'] = (hcT.rearrange('p a b -> p (a b)'), [128, 8 * CTX], BF16)
        return finish(nc, es, S, dbg, t_hT + t_hcT)

    S.barrier()
    o = RD
    ropec, _, nb = A.alloc(F32, [TH], at=o); o += nb
    ropes, _, nb = A.alloc(F32, [TH], at=o); o += nb
    glg, _, nb = A.alloc(F32, [512], at=o); o += nb
    glb, _, nb = A.alloc(F32, [512], at=o); o += nb
    qraw = []
    for i in range(2):
        q_, _, nb = A.alloc(BF16, [512], at=o); o += nb
        qraw.append(q_)
    rt1 = []
    rt2 = []
    for i in range(2):
        q_, _, nb = A.alloc(F32, [512], at=o); o += nb
        rt1.append(q_)
        q_, _, nb = A.alloc(F32, [512], at=o); o += nb
        rt2.append(q_)
    assert o <= RD + RD_SIZE, (o - RD)
    o = RE_FREE
    gvb = []
    for i in range(4):
        q_, _, nb = A.alloc(F32, [512], at=o); o += nb
        gvb.append(q_)
    assert o <= RE_FREE + 8192, (o, RE_FREE)
    t_rope = [sp_load(ropec, ropec_d, 'rope'), sp_load(ropes, ropes_d, 'rope')][-1]
    t_gl = [sp_load(glg, glg_d, 'gl'), sp_load(glb, glb_d, 'gl')][-1]
    t_vones = S.op('pool', lambda e: e.memset(vaug[:, :, :, 64:66], 1.0))
    t_vcones = S.op('pool', lambda e: e.memset(vcaug[:, :, :, 64:66], 1.0))

    fb_free = [None] * 8
    qraw_free = [None, None]
    rt_free = [None, None]

    dq = {'blocks': [4, 5, 6, 7, 8, 9, 10, 11], 'issued': [], 'n_issued': 0, 'n_done': 0, 'bs_free': None, 'pb': 0}
    ring1_free = [None, None]
    t_modc23 = []

    def p0_issue():
        if dq['n_issued'] >= len(dq['blocks']):
            return
        blk = dq['blocks'][dq['n_issued']]
        slot = dq['n_issued'] % 2
        dq['n_issued'] += 1
        t_w = pool_cast_load(wring1[slot].rearrange('p a b -> p (a b)'), wada[blk], 'wada_d%d' % slot,
                             deps=[ring1_free[slot]])
        dq['issued'].append((blk, slot, t_w))

    def p0_compute(col_bank=6, bc_bank=7):
        if dq['n_done'] >= len(dq['blocks']):
            return
        blk, slot, t_w = dq['issued'][dq['n_done']]
        dq['n_done'] += 1
        wb_ = wring1[slot]
        if blk in colkind:
            pm = bk(col_bank)
            kind = colkind[blk]
            t_l = None
            for s in range(4):
                chunk = (blk % 2) * 4 + s
                col = (kind * 8 + chunk) * 2
                for k in range(8):
                    t_l = S.op('pe', lambda e, k=k, s=s, col=col, wb_=wb_, pm=pm: e.matmul(
                        pm[:, col:col + 2], lhsT=wb_[:, k, s * 128:(s + 1) * 128], rhs=sc[:, k:16:8],
                        start=(k == 0), stop=(k == 7)), deps=[t_w, t_sc, fb_free[col_bank]] if k == 0 else (),
                        sig=(k == 7 and s == 3))
            ring1_free[slot] = t_l
            lo = (kind * 8 + (blk % 2) * 4) * 2
            t_e = modc_evac(lo, lo + 8, kind in (1, 3), pm, t_l)
            fb_free[col_bank] = t_e
            t_modc23.append(t_e)
        else:
            which, half = bcform[blk]
            bank = bc_bank
            pb = bk(bank)
            gi_ = which * 2 + half
            t_bs = S.dma('sp', lambda e, gi_=gi_: e.dma_start(out=bslice, in_=badabc_d[:, gi_ * 512:(gi_ + 1) * 512]),
                         'bslice', deps=[dq['bs_free']])
            for k in range(8):
                t_mm = S.op('pe', lambda e, k=k, pb=pb, wb_=wb_: e.matmul(
                    pb, lhsT=scb[:, k, :], rhs=wb_[:, k, :], start=(k == 0), stop=(k == 7)),
                    deps=[t_w, t_scb, fb_free[bank]] if k == 0 else (), sig=(k == 7))
            ring1_free[slot] = t_mm
            dst = (g1bc if which == 0 else g2bc)[:, half * 512:(half + 1) * 512]
            t_e = S.op('dve', lambda e, dst=dst, pb=pb: e.tensor_tensor(out=dst, in0=pb, in1=bslice, op=ALU.add),
                       deps=[t_mm, t_bs])
            dq['bs_free'] = t_e
            fb_free[bank] = t_e
            t_g[(which, half)] = t_e
        p0_issue()

    p0_issue()
    p0_issue()
    units = []
    for m in range(4):
        for tg in range(4):
            units.append(('q', m, 128 + tg * 512, 512, tg * 512))
    for tg in range(5):
        n_ = 512 if tg < 4 else 256
        units.append(('k', 4, tg * 512, n_, tg * 512))
    t_q = {}
    t_qk_all = []
    qk = [dict() for _ in units]

    def qk_part1(ui):
        kd, m, hcol, n_, ocol = units[ui]
        b0 = (ui % 3) * 2
        pq = bk(b0)[:, 0:n_]
        for k in range(8):
            t_mm = S.op('pe', lambda e, k=k, pq=pq, m=m, hcol=hcol, n_=n_: e.matmul(
                pq, lhsT=wqk[:, k, m * 128:(m + 1) * 128], rhs=hT[:, k, hcol:hcol + n_],
                start=(k == 0), stop=(k == 7)),
                deps=[t_wqk, fb_free[b0], fb_free[b0 + 1]] if k == 0 else (), sig=(k == 7))
        r = ui % 2
        t_raw = S.op('act', lambda e, dst=qraw[r][:, 0:n_], pq=pq: e.activation(out=dst, in_=pq, func=AF.Copy),
                     deps=[t_mm, qraw_free[r]])
        qk[ui]['mm'] = t_mm
        qk[ui]['raw'] = t_raw

    def qk_part2(ui):
        kd, m, hcol, n_, ocol = units[ui]
        b0 = (ui % 3) * 2
        pq, ps_ = bk(b0)[:, 0:n_], bk(b0 + 1)[:, 0:n_]
        r = ui % 2
        t_mm, t_raw = qk[ui]['mm'], qk[ui]['raw']
        t_pm = S.op('pe', lambda e, ps_=ps_, src=qraw[r][:, 0:n_]: e.matmul(ps_, lhsT=perm, rhs=src, start=True, stop=True),
                    deps=[t_raw, T_CONST])
        qraw_free[r] = t_pm
        t_1 = S.op('dve', lambda e, dst=rt1[r][:, 0:n_], pq=pq, hcol=hcol, n_=n_: e.tensor_tensor(
            out=dst, in0=pq, in1=ropec[:, hcol:hcol + n_], op=ALU.mult), deps=[t_mm, t_raw, t_rope, rt_free[r]])
        t_2 = S.op('dve', lambda e, dst=rt2[r][:, 0:n_], ps_=ps_, hcol=hcol, n_=n_: e.tensor_tensor(
            out=dst, in0=ps_, in1=ropes[:, hcol:hcol + n_], op=ALU.mult), deps=[t_pm])
        fb_free[b0] = t_2
        fb_free[b0 + 1] = t_2
        dst = qT[:, m, ocol:ocol + n_] if kd == 'q' else kT[:, ocol:ocol + n_]
        t_3 = S.op('pool', lambda e, dst=dst, a=rt1[r][:, 0:n_], b=rt2[r][:, 0:n_]: e.tensor_tensor(
            out=dst, in0=a, in1=b, op=ALU.add), deps=[t_1, t_2])
        rt_free[r] = t_3
        t_qk_all.append(t_3)

    for ui in range(len(units)):
        qk_part1(ui)
        if ui >= 1:
            qk_part2(ui - 1)
        if ui in (6, 13, 20):
            p0_compute()
    qk_part2(len(units) - 1)
    if debug_stop == 'A2a':
        dbg['qT'] = (qT.rearrange('p a b -> p (a b)'), [128, 4 * T], BF16)
        dbg['kT'] = (kT, [128, TH], BF16)
        return finish(nc, es, S, dbg, t_qk_all)
    pq = bk(0)[:, 0:CTX]
    for k in range(8):
        t_mm = S.op('pe', lambda e, k=k, pq=pq: e.matmul(pq, lhsT=wqk[:, k, 512:640], rhs=hcT[:, k, :],
                                                         start=(k == 0), stop=(k == 7)),
                    deps=[fb_free[0], t_hcT] if k == 0 else (), sig=(k == 7))
    t_kc = S.op('act', lambda e, pq=pq: e.activation(out=kcT, in_=pq, func=AF.Copy), deps=[t_mm])
    fb_free[0] = t_kc
    t_gu = []
    ui = 0
    for m in range(4):
        for tg in range(4):
            b0 = 1 + (ui % 3); ui += 1
            pq = bk(b0)
            for k in range(8):
                t_mm = S.op('pe', lambda e, k=k, pq=pq, m=m, tg=tg: e.matmul(
                    pq, lhsT=wu[:, k, m * 128:(m + 1) * 128], rhs=hT[:, k, 128 + tg * 512:128 + (tg + 1) * 512],
                    start=(k == 0), stop=(k == 7)), deps=[t_wu, fb_free[b0]] if k == 0 else (), sig=(k == 7))
            t_e = S.op('act', lambda e, pq=pq, m=m, tg=tg: e.activation(
                out=guT[:, m, tg * 512:(tg + 1) * 512], in_=pq, func=AF.Gelu_apprx_tanh), deps=[t_mm])
            fb_free[b0] = t_e
            t_gu.append(t_e)
            if ui in (5, 10, 15):
                p0_compute()
    if debug_stop == 'A2b':
        dbg['kcT'] = (kcT, [128, CTX], BF16)
        dbg['guT'] = (guT.rearrange('p a b -> p (a b)'), [128, 4 * T], BF16)
        return finish(nc, es, S, dbg, [t_kc] + t_gu)
    gvb_free = [None] * 4
    t_v = [None] * NTH
    t_vc = [None] * 2
    t_vn = [None] * NT
    vb_ = [dict() for _ in tiles]

    def is_main(n):
        kindt, ti = tiles[n]
        return kindt == 'x' and 1 <= ti <= NT

    def v_s0(n):
        kindt, ti = tiles[n]
        bv = 4 + (n % 2)
        pv = bk(bv)[:, 0:128]
        for k in range(8):
            lh = hT[:, k, ti * 128:(ti + 1) * 128] if kindt == 'x' else hcT[:, k, ti * 128:(ti + 1) * 128]
            t_mm = S.op('pe', lambda e, k=k, pv=pv, lh=lh: e.matmul(pv, lhsT=lh, rhs=wvvb[:, k, 0:128],
                                                                   start=(k == 0), stop=(k == 7)),
                        deps=[t_wvvb, fb_free[bv]] if k == 0 else (), sig=(k == 7))
        vb_[n]['v'] = t_mm
        if is_main(n):
            mt = ti - 1
            bvb = 6 + (mt % 2)
            pvb = bk(bvb)
            for k in range(8):
                t_mm = S.op('pe', lambda e, k=k, pvb=pvb, ti=ti: e.matmul(
                    pvb, lhsT=hT[:, k, ti * 128:(ti + 1) * 128], rhs=wvvb[:, k, 128:640],
                    start=(k == 0), stop=(k == 7)), deps=[fb_free[bvb]] if k == 0 else (), sig=(k == 7))
            vb_[n]['vb'] = t_mm

    def v_s1(n):
        kindt, ti = tiles[n]
        bv = 4 + (n % 2)
        pv = bk(bv)[:, 0:128]
        dstv = (vaug[:, ti, :, 0:64] if kindt == 'x' else vcaug[:, ti, :, 0:64])
        t_e = S.op('act', lambda e, dstv=dstv, pv=pv: e.activation(
            out=dstv, in_=pv.rearrange('p (a b) -> p a b', a=2), func=AF.Copy), deps=[vb_[n]['v'], t_vones, t_vcones])
        fb_free[bv] = t_e
        if kindt == 'x':
            t_v[ti] = t_e
        else:
            t_vc[ti] = t_e
        if is_main(n):
            mt = ti - 1
            bvb = 6 + (mt % 2)
            g_ = gvb[mt % 4]
            t_ge = S.op('act', lambda e, g_=g_, pvb=bk(bvb): e.activation(out=g_, in_=pvb, func=AF.Gelu_apprx_tanh),
                        deps=[vb_[n]['vb'], gvb_free[mt % 4]])
            fb_free[bvb] = t_ge
            vb_[n]['ge'] = t_ge

    def v_s2(n):
        if not is_main(n):
            return
        mt = tiles[n][1] - 1
        sl = mt % 4
        st = stt[:, sl]; mv = mvr[:, sl]
        g_ = gvb[mt % 4]
        t1 = S.op('dve', lambda e: e.bn_stats(out=st[:, 0:6], in_=g_), deps=[vb_[n]['ge'], st_free[sl]])
        vb_[n]['ag'] = S.op('dve', lambda e: e.bn_aggr(out=mv[:, 0:2], in_=st[:, 0:6]), deps=[t1])

    def v_s3(n):
        if not is_main(n):
            return
        mt = tiles[n][1] - 1
        mv = mvr[:, mt % 4]
        t3 = S.op('pool', lambda e: e.tensor_scalar(out=mv[:, 4:5], in0=mv[:, 1:2], scalar1=EPS, scalar2=None,
                                                    op0=ALU.add), deps=[vb_[n]['ag']])
        vb_[n]['rs'] = S.op('pool', lambda e: e.tensor_tensor(out=mv[:, 2:3], in0=mv[:, 4:5], in1=neghalf[:, 0:1],
                                                              op=ALU.pow), deps=[t3, t_m2])

    def v_s4(n):
        if not is_main(n):
            return
        mt = tiles[n][1] - 1
        mv = mvr[:, mt % 4]
        g_ = gvb[mt % 4]
        tmp_ = rt1[mt % 2]
        t5 = S.op('dve', lambda e: e.scalar_tensor_tensor(
            out=tmp_, in0=g_, scalar=mv[:, 0:1], in1=glg, op0=ALU.subtract, op1=ALU.mult),
            deps=[vb_[n]['rs'], t_gl, rt_free[mt % 2]])
        t6 = S.op('dve', lambda e: e.scalar_tensor_tensor(
            out=vn[:, mt, :], in0=tmp_, scalar=mv[:, 2:3], in1=glb, op0=ALU.mult, op1=ALU.add), deps=[t5])
        rt_free[mt % 2] = t6
        gvb_free[mt % 4] = t6
        st_free[mt % 4] = t6
        t_vn[mt] = t6

    def v_hook(it):
        if it in (5, 12):
            p0_compute(col_bank=0, bc_bank=1)

    run_pipeline([v_s0, v_s1, v_s2, v_s3, v_s4], len(tiles), hook=v_hook)
    while dq['n_done'] < len(dq['blocks']):
        p0_compute(col_bank=0, bc_bank=1)
    t_modc = t_modc01 + t_modc23
    t_g1 = [t_g[(0, 0)], t_g[(0, 1)]]
    t_g2 = [t_g[(1, 0)], t_g[(1, 1)]]

    if debug_stop == 'A2':
        dbg['qT'] = (qT.rearrange('p a b -> p (a b)'), [128, 4 * T], BF16)
        dbg['kT'] = (kT, [128, TH], BF16)
        dbg['kcT'] = (kcT, [128, CTX], BF16)
        dbg['vaug'] = (vaug.rearrange('p a b c -> p (a b c)'), [128, NTH * 2 * 66], BF16)
        dbg['vcaug'] = (vcaug.rearrange('p a b c -> p (a b c)'), [128, 2 * 2 * 66], BF16)
        dbg['guT'] = (guT.rearrange('p a b -> p (a b)'), [128, 4 * T], BF16)
        dbg['vn'] = (vn.rearrange('p a b -> p (a b)'), [128, NT * 512], BF16)
        return finish(nc, es, S, dbg, t_qk_all + [t_kc] + t_gu + t_v + t_vc + t_vn)

    S.barrier()
    o = RD
    wst, _, nb = A.alloc(BF16, [8, 128], at=o); o += nb
    NPT = 20
    PT = []
    for i in range(NPT):
        p_, _, nb = A.alloc(BF16, [4, 128], at=o); o += nb
        PT.append(p_)
    yatm = []
    for i in range(2):
        p_, _, nb = A.alloc(BF16, [512], at=o); o += nb
        yatm.append(p_)
    sbt = []
    for i in range(2):
        p_, _, nb = A.alloc(F32, [512], at=o); o += nb
        sbt.append(p_)
    dens, _, nb = A.alloc(F32, [2, 16], at=o); o += nb
    esk, _, nb = A.alloc(F32, [8], at=o); o += nb
    wab0, _, nb = A.alloc(BF16, [8, 128], at=o); o += nb
    assert o <= RD + RD_SIZE, (o - RD)
    wg0, _, _ = A.alloc(BF16, [16, 128], at=RC + 32 * 1024)
    wc1_pre = {}

    def issue_wc1_slot0():
        wc1_pre['t'] = [pool_cast_load(wab0.rearrange('p a b -> p (a b)'), wc1_d[0][:, 0:1024], 'wc1_0a'),
                        pool_cast_load(wg0.rearrange('p a b -> p (a b)'), wc1_d[0][:, 1024:3072], 'wc1_0b')]
    o = RE_FREE
    woutb, _, nb = A.alloc(BF16, [8, 1024], at=o); o += nb
    wstage = []
    for i in range(2):
        p_, _, nb = A.alloc(F32, [1024], at=o); o += nb
        wstage.append(p_)
    assert o <= ARENA_BYTES, (o, ARENA_BYTES)

    t_wst = pool_cast_load(wst.rearrange('p a b -> p (a b)'), wst_d, 'wst')
    t_esk = S.op('act', lambda e: e.activation(out=esk, in_=esink, func=AF.Exp), deps=[T_CONST])
    ws_free = [None, None]
    wo = {'t': None}

    def wout_step(k):
        r = k % 2
        t_l = S.dma('sp', lambda e, dst=wstage[r], k=k: e.dma_start(out=dst, in_=wout_d[:, k * 1024:(k + 1) * 1024]),
                    'wos%d' % r, deps=[ws_free[r]])
        wo['t'] = S.op('pool', lambda e, k=k, src=wstage[r]: e.tensor_tensor(out=woutb[:, k, :], in0=src, in1=g1bc, op=ALU.mult),
                       deps=[t_l] + t_g1)
        ws_free[r] = wo['t']

    fb_free = [None] * 8
    sbt_free = [None, None]
    t_yb = [None] * NT
    sc_st = {'i': 0}

    def gmlp_tile(t):
        b0 = sc_st['i'] % 4; sc_st['i'] += 1
        pf = bk(b0)
        for c in range(4):
            for gi in range(2):
                g = 2 * c + gi
                outp = pf[gi * 64:(gi + 1) * 64, c * 128:(c + 1) * 128]
                t_mm = S.op('pe', lambda e, outp=outp, g=g, t=t: e.matmul(
                    outp, lhsT=vn[:, t, g * 64:(g + 1) * 64], rhs=wst[:, g, :], start=True, stop=True),
                    deps=[t_wst, fb_free[b0], t_vn[t]] if (c == 0 and gi == 0) else (), sig=(c == 3 and gi == 1))
        r = t % 2
        t_sb = S.op('dve', lambda e, pf=pf, dst=sbt[r]: e.tensor_tensor(out=dst, in0=pf, in1=bsbc, op=ALU.add),
                    deps=[t_mm, T_CONST, sbt_free[r]])
        fb_free[b0] = t_sb
        t_yb[t] = S.op('dve', lambda e, src=sbt[r], t=t: e.tensor_tensor(
            out=ybT[:, :, t * 128:(t + 1) * 128], in0=src.rearrange('p (a b) -> p a b', a=4),
            in1=guT[:, :, t * 128:(t + 1) * 128], op=ALU.mult), deps=[t_sb])
        sbt_free[r] = t_yb[t]

    pt_free = [None] * NPT
    yatm_free = [None, None]
    o_free = [None] * 4
    t_ya = [None] * NT
    att = [dict() for _ in range(NT)]

    def att_srcs(j):
        srcs = []
        for s in range(3):
            kt = j + s
            srcs.append((kT[:, kt * 128:(kt + 1) * 128], vaug[:, kt], [t_v[kt]]))
        for s in range(2):
            srcs.append((kcT[:, s * 128:(s + 1) * 128], vcaug[:, s], [t_vc[s]]))
        return srcs

    def att_front(j):
        base = (j % 2) * 10
        srcs = att_srcs(j)
        t_p = {}
        att[j]['p'] = t_p
        for s, (ksrc, vsrc, vdep) in enumerate(srcs):
            for kv in range(2):
                bi = sc_st['i'] % 4; sc_st['i'] += 1
                psb = bk(bi)
                pt = PT[base + s * 2 + kv]
                t_mm = S.op('pe', lambda e, psb=psb, ksrc=ksrc, kv=kv, j=j: e.matmul(
                    psb, lhsT=ksrc[kv * 64:(kv + 1) * 64, :], rhs=qT[kv * 64:(kv + 1) * 64, :, j * 128:(j + 1) * 128],
                    start=True, stop=True), deps=[fb_free[bi]])
                t_e = S.op('act', lambda e, pt=pt, psb=psb: e.activation(
                    out=pt.rearrange('p a b -> p (a b)'), in_=psb, func=AF.Exp, scale=0.125),
                    deps=[t_mm, pt_free[base + s * 2 + kv]])
                fb_free[bi] = t_e
                if s == 0 or s == 2:
                    mi = (0 if j == 0 else 1) if s == 0 else (3 if j == NT - 1 else 2)
                    t_e = S.op('pool', lambda e, pt=pt, mi=mi: e.tensor_tensor(
                        out=pt, in0=pt, in1=masks[:, mi:mi + 1, :].to_broadcast([128, 4, 128]), op=ALU.mult),
                        deps=[t_e, T_CONST])
                t_p[(s, kv)] = t_e

    def att_back(j):
        base = (j % 2) * 10
        srcs = att_srcs(j)
        t_p = att[j]['p']
        t_o = [None, None]
        for kv in range(2):
            ob = bk(4 + 2 * (j % 2) + kv).rearrange('p (a b) -> p a b', a=4)
            for g in range(4):
                for s, (ksrc, vsrc, vdep) in enumerate(srcs):
                    pt = PT[base + s * 2 + kv]
                    t_mm = S.op('pe', lambda e, ob=ob, g=g, pt=pt, vsrc=vsrc, kv=kv, s=s: e.matmul(
                        ob[:, g, 0:65], lhsT=pt[:, g, :], rhs=vsrc[:, kv, 0:65], start=(s == 0), stop=(s == 4)),
                        deps=[t_p[(s, kv)], o_free[2 * (j % 2) + kv]] + vdep, sig=(g == 3 and s == 4))
            t_o[kv] = t_mm
        for s in range(5):
            for kv in range(2):
                pt_free[base + s * 2 + kv] = t_o[kv]
        r = j % 2
        dn = dens[:, r]
        ya = yatm[r].rearrange('p (k g d) -> p k g d', k=2, g=4)
        t_n = None
        for kv in range(2):
            ob = bk(4 + 2 * (j % 2) + kv).rearrange('p (a b) -> p a b', a=4)
            t_a = S.op('dve', lambda e, ob=ob, dn=dn, kv=kv: e.tensor_tensor(
                out=dn[:, kv * 4:(kv + 1) * 4], in0=ob[:, :, 64], in1=esk[:, kv * 4:(kv + 1) * 4], op=ALU.add),
                deps=[t_o[kv], t_esk, yatm_free[r]])
            t_b = S.op('dve', lambda e, dn=dn, kv=kv: e.reciprocal(out=dn[:, 8 + kv * 4:8 + (kv + 1) * 4],
                                                                  in_=dn[:, kv * 4:(kv + 1) * 4]), deps=[t_a])
            t_n = S.op('dve', lambda e, ob=ob, dn=dn, kv=kv, ya=ya: e.tensor_tensor(
                out=ya[:, kv], in0=ob[:, :, 0:64],
                in1=dn[:, 8 + kv * 4:8 + (kv + 1) * 4].unsqueeze(2).to_broadcast([128, 4, 64]), op=ALU.mult),
                deps=[t_b])
            o_free[2 * (j % 2) + kv] = t_n
        att[j]['n'] = t_n

    def att_tail(j):
        r = j % 2
        t_n = att[j]['n']
        ptb = bkb(4 + 2 * (j % 2))
        for c in range(4):
            t_tr = S.op('pe', lambda e, c=c, ptb=ptb, src=yatm[r]: e.transpose(
                ptb[:, c * 128:(c + 1) * 128], src[:, c * 128:(c + 1) * 128], ident),
                deps=[t_n] if c == 0 else (), sig=(c == 3))
        yatm_free[r] = t_tr
        t_ya[j] = S.op('dve', lambda e, ptb=ptb, j=j: e.tensor_copy(
            out=yaT[:, :, j * 128:(j + 1) * 128], in_=ptb[:, 0:512].rearrange('p (a b) -> p a b', a=4)),
            deps=[t_tr])
        o_free[2 * (j % 2)] = t_ya[j]

    att_front(0)
    for j in range(NT + 1):
        if j + 1 < NT:
            att_front(j + 1)
        if j < NT:
            gmlp_tile(j)
            att_back(j)
            if 2 <= j < 10:
                wout_step(j - 2)
            if j == 10:
                issue_wc1_slot0()
        if j >= 1:
            att_tail(j - 1)

    t_wout = wo['t']
    if debug_stop == 'B':
        dbg['yaT'] = (yaT.rearrange('p a b -> p (a b)'), [128, 4 * T], BF16)
        dbg['ybT'] = (ybT.rearrange('p a b -> p (a b)'), [128, 4 * T], BF16)
        return finish(nc, es, S, dbg, t_ya + t_yb + [t_wout])

    S.barrier()
    o = RB
    wc1 = []
    t_wc1 = [None] * 8
    for m in range(8):
        p_, _, nb = A.alloc(BF16, [24, 128], at=o); o += nb
        wc1.append(p_)
        if m == 0:
            t_wc1[m] = wc1_pre['t']
        else:
            t_wc1[m] = pool_cast_load(p_.rearrange('p a b -> p (a b)'), wc1_d[m], 'wc1_%d' % m)

    def wsel(m, idx):
        if m == 0:
            return wab0[:, idx, :] if idx < 8 else wg0[:, idx - 8, :]
        return wc1[m][:, idx, :]
    outpre = {}
    sga = []
    for i in range(4):
        p_, _, nb = A.alloc(F32, [512], at=o); o += nb
        sga.append(p_)
    mt1 = []
    for i in range(4):
        p_, _, nb = A.alloc(F32, [512], at=o); o += nb
        mt1.append(p_)
    assert o <= RB + RB_SIZE, (o - RB)
    fb_free = [None] * 8
    sga_free = [None] * 4
    mt_free = [None] * 4
    t_merged = {}
    ui = 0
    for m in range(8):
        if m == 4:
            for t_ in range(NT):
                outpre['t'] = S.dma('sp', lambda e, t_=t_: e.dma_start(out=out_d[t_ * 128:(t_ + 1) * 128, :], in_=ln2b_d),
                                    'outpre', deps=[t_wc1[7]])
        wm = wc1[m]
        t_w = t_wc1[m]
        for tg in range(4):
            bs_ = (ui % 2) * 4
            r2 = (ui % 2) * 2
            ui += 1
            pa, pb_, pga, pgb = bk(bs_), bk(bs_ + 1), bk(bs_ + 2), bk(bs_ + 3)
            tok = slice(tg * 512, (tg + 1) * 512)
            htok = slice(128 + tg * 512, 128 + (tg + 1) * 512)
            for k in range(8):
                t_ga = S.op('pe', lambda e, k=k, pga=pga, wm=wm, m=m, htok=htok: e.matmul(
                    pga, lhsT=wsel(m, 8 + k), rhs=hT[:, k, htok], start=(k == 0), stop=(k == 7)),
                    deps=[t_w, fb_free[bs_ + 2]] if k == 0 else (), sig=(k == 7))
            for k in range(8):
                t_gb = S.op('pe', lambda e, k=k, pgb=pgb, wm=wm, m=m, htok=htok: e.matmul(
                    pgb, lhsT=wsel(m, 16 + k), rhs=hT[:, k, htok], start=(k == 0), stop=(k == 7)),
                    deps=[fb_free[bs_ + 3]] if k == 0 else (), sig=(k == 7))
            for k in range(4):
                t_a = S.op('pe', lambda e, k=k, pa=pa, wm=wm, m=m, tok=tok: e.matmul(
                    pa, lhsT=wsel(m, k), rhs=yaT[:, k, tok], start=(k == 0), stop=(k == 3)),
                    deps=[fb_free[bs_]] if k == 0 else (), sig=(k == 3))
            for k in range(4):
                t_b = S.op('pe', lambda e, k=k, pb_=pb_, wm=wm, m=m, tok=tok: e.matmul(
                    pb_, lhsT=wsel(m, 4 + k), rhs=ybT[:, k, tok], start=(k == 0), stop=(k == 3)),
                    deps=[fb_free[bs_ + 1]] if k == 0 else (), sig=(k == 3))
            t_sa = S.op('act', lambda e, dst=sga[r2], pga=pga: e.activation(out=dst, in_=pga, func=AF.Sigmoid),
                        deps=[t_ga, sga_free[r2]])
            t_sb = S.op('act', lambda e, dst=sga[r2 + 1], pgb=pgb: e.activation(out=dst, in_=pgb, func=AF.Sigmoid),
                        deps=[t_gb, sga_free[r2 + 1]])
            fb_free[bs_ + 2] = t_sa
            fb_free[bs_ + 3] = t_sb
            t_1 = S.op('dve', lambda e, dst=mt1[r2], pa=pa, sa=sga[r2]: e.tensor_tensor(out=dst, in0=pa, in1=sa, op=ALU.mult),
                       deps=[t_a, t_sa, mt_free[r2]])
            t_2 = S.op('dve', lambda e, dst=mt1[r2 + 1], pb_=pb_, sb=sga[r2 + 1]: e.tensor_tensor(out=dst, in0=pb_, in1=sb, op=ALU.mult),
                       deps=[t_b, t_sb, mt_free[r2 + 1]])
            fb_free[bs_] = t_1
            fb_free[bs_ + 1] = t_2
            sga_free[r2] = t_1
            sga_free[r2 + 1] = t_2
            t_3 = S.op('pool', lambda e, m=m, tok=tok, a=mt1[r2], b=mt1[r2 + 1]: e.tensor_tensor(
                out=merged[:, m, tok], in0=a, in1=b, op=ALU.add), deps=[t_1, t_2])
            mt_free[r2] = t_3
            mt_free[r2 + 1] = t_3
            t_merged[(m, tg)] = t_3

    if debug_stop == 'C1':
        dbg['merged'] = (merged.rearrange('p a b -> p (a b)'), [128, 8 * T], BF16)
        return finish(nc, es, S, dbg, list(t_merged.values()) + [t_wout])

    S.barrier()
    o = RC
    NXR2, NWK = 2, 4
    xr = []
    for i in range(NXR2):
        x_, _, nb = A.alloc(F32, [D], at=o); o += nb
        xr.append(x_)
    wk = []
    for i in range(NWK):
        x_, _, nb = A.alloc(F32, [D], at=o); o += nb
        wk.append(x_)
    xnb = []
    for i in range(2):
        x_, _, nb = A.alloc(BF16, [D], at=o); o += nb
        xnb.append(x_)
    assert o <= RC + 28 * 1024, (o - RC)
    NWR = 3
    w1 = [None] * NWR
    w2 = [None] * NWR
    w1[0], _, _ = A.alloc(BF16, [8, 512], at=RC + 28 * 1024)
    c2pre = {}
    ln1g, ln1b = wstage[0], wstage[1]

    fb_free = [None] * 8
    xr_free = [None] * NXR2
    wk_free = [None] * NWK
    xnb_free = [None, None]
    t_h2 = [None] * NT
    t_xmid = [None] * NT
    c2 = [dict() for _ in range(NT)]

    def c2_s0(t):
        r = t % NXR2
        c2[t]['x'] = S.dma('sp', lambda e, dst=xr[r], t=t: e.dma_start(out=dst, in_=xh[(t + 1) * 128:(t + 2) * 128, :]),
                           'xr%d' % r, deps=[xr_free[r]])
        b0 = (t % 2) * 2
        c2[t]['mm'] = []
        for half in range(2):
            pm_ = bk(b0 + half)
            for k in range(8):
                t_mm = S.op('pe', lambda e, k=k, pm_=pm_, t=t, half=half: e.matmul(
                    pm_, lhsT=merged[:, k, t * 128:(t + 1) * 128], rhs=woutb[:, k, half * 512:(half + 1) * 512],
                    start=(k == 0), stop=(k == 7)),
                    deps=[t_wout, fb_free[b0 + half]] + [t_merged[(kk, t // 4)] for kk in range(8)] if k == 0 else (),
                    sig=(k == 7))
            c2[t]['mm'].append(t_mm)

    def c2_s1(t):
        r = t % NXR2
        w = wk[t % NWK]
        b0 = (t % 2) * 2
        for half in range(2):
            pm_ = bk(b0 + half)
            t_pre = S.op('dve', lambda e, pm_=pm_, r=r, half=half, w=w: e.scalar_tensor_tensor(
                out=w[:, half * 512:(half + 1) * 512], in0=xr[r][:, half * 512:(half + 1) * 512],
                scalar=ALPHA, in1=pm_, op0=ALU.mult, op1=ALU.add),
                deps=[c2[t]['mm'][half], c2[t]['x'], wk_free[t % NWK]])
            fb_free[b0 + half] = t_pre
        xr_free[r] = t_pre
        sl = t % 4
        st = stt[:, sl]; mv = mvr[:, sl]
        S.op('dve', lambda e: e.bn_stats(out=st[:, 0:6], in_=w[:, 0:512]), deps=[t_pre, st_free[sl]], sig=False)
        t1 = S.op('dve', lambda e: e.bn_stats(out=st[:, 6:12], in_=w[:, 512:1024]))
        c2[t]['ag1'] = S.op('dve', lambda e: e.bn_aggr(out=mv[:, 0:2], in_=st[:, 0:12]), deps=[t1])

    def c2_s2(t):
        mv = mvr[:, t % 4]
        c2[t]['sq1'] = S.op('act', lambda e: e.activation(out=mv[:, 4:5], in_=mv[:, 1:2], func=AF.Sqrt, bias=epst[:, 0:1],
                                                          scale=1.0), deps=[c2[t]['ag1'], t_m3])

    def c2_s3(t):
        _, _, _, c2[t]['r1'] = rstd_part(t % 4, [c2[t]['sq1']])

    def c2_s4(t):
        mv = mvr[:, t % 4]
        w = wk[t % NWK]
        c2[t]['n1'] = S.op('act', lambda e: e.activation(out=w, in_=w, func=AF.Identity, scale=mv[:, 2:3], bias=mv[:, 3:4]),
                           deps=[c2[t]['r1']])
        st_free[t % 4] = c2[t]['n1']

    def c2_s5(t):
        w = wk[t % NWK]
        t_g_ = S.op('pool', lambda e: e.tensor_tensor(out=xmid[:, t, :], in0=w, in1=ln1g, op=ALU.mult),
                    deps=[c2[t]['n1'], c2pre['ln1']])
        wk_free[t % NWK] = t_g_
        t_xmid[t] = S.dma('pool', lambda e: e.dma_start(out=xmid[:, t, :], in_=ln1b, accum_op=ALU.add), 'xb%d' % t,
                          deps=[t_g_, c2pre['ln1']])

    def c2_s5w(t):
        pass

    def c2_s6(t):
        sl = 4 + t % 4
        st = stt[:, sl]; mv = mvr[:, sl]
        src = xmid[:, t, :]
        S.op('dve', lambda e: e.bn_stats(out=st[:, 0:6], in_=src[:, 0:512]), deps=[t_xmid[t], st_free[sl]], sig=False)
        t1 = S.op('dve', lambda e: e.bn_stats(out=st[:, 6:12], in_=src[:, 512:1024]))
        c2[t]['ag2'] = S.op('dve', lambda e: e.bn_aggr(out=mv[:, 0:2], in_=st[:, 0:12]), deps=[t1])

    def c2_s7(t):
        mv = mvr[:, 4 + t % 4]
        c2[t]['sq2'] = S.op('act', lambda e: e.activation(out=mv[:, 4:5], in_=mv[:, 1:2], func=AF.Sqrt, bias=epst[:, 0:1],
                                                          scale=1.0), deps=[c2[t]['ag2']])

    def c2_s8(t):
        _, _, _, c2[t]['r2'] = rstd_part(4 + t % 4, [c2[t]['sq2']])

    def c2_s9(t):
        mv = mvr[:, 4 + t % 4]
        r = t % 2
        c2[t]['n2'] = S.op('act', lambda e: e.activation(out=xnb[r], in_=xmid[:, t, :], func=AF.Identity,
                                                         scale=mv[:, 2:3], bias=mv[:, 3:4]), deps=[c2[t]['r2'], xnb_free[r]])
        st_free[4 + t % 4] = c2[t]['n2']

    def c2_s10(t):
        r = t % 2
        ptA = bkb(4 + 2 * r)
        ptB = bkb(5 + 2 * r)
        for c in range(8):
            pt = ptA if c < 4 else ptB
            t_tr = S.op('pe', lambda e, c=c, pt=pt, src=xnb[r]: e.transpose(
                pt[:, (c % 4) * 128:(c % 4 + 1) * 128], src[:, c * 128:(c + 1) * 128], ident),
                deps=[c2[t]['n2'], fb_free[4 + 2 * r], fb_free[5 + 2 * r]] if c == 0 else (), sig=(c == 3 or c == 7))
            if c == 3:
                c2[t]['trA'] = t_tr
        c2[t]['trB'] = t_tr
        xnb_free[r] = t_tr

    def c2_s11(t):
        r = t % 2
        ptA = bkb(4 + 2 * r)
        ptB = bkb(5 + 2 * r)
        for c in range(8):
            dst = h2T[:, c, t * 128:(t + 1) * 128]
            pt = ptA if c < 4 else ptB
            t_ev = S.op('act', lambda e, c=c, dst=dst, pt=pt: e.activation(
                out=dst, in_=pt[:, (c % 4) * 128:(c % 4 + 1) * 128], func=AF.Identity,
                scale=mcol(3, c, 0), bias=mcol(2, c, 0)),
                deps=[c2[t]['trA'], c2[t]['trB'], t_modc23] if c == 0 else (), sig=(c == 3 or c == 7))
            if c == 3:
                t_evA = t_ev
        t_evB = t_ev
        fb_free[4 + 2 * r] = t_evA
        fb_free[5 + 2 * r] = t_evB
        t_h2[t] = [t_evA, t_evB]

    def c2_hook(it):
        if it == 1:
            c2pre['ln1'] = [sp_load(ln1g, ln1g_d, 'ln1'), sp_load(ln1b, ln1b_d, 'ln1')][-1]
        if it == 3:
            c2pre['w1'] = pool_cast_load(w1[0].rearrange('p a b -> p (a b)'), wff1_d[0], 'wf1_0')

    run_pipeline([c2_s0, c2_s1, c2_s2, c2_s3, c2_s4, c2_s5, c2_s5w, c2_s6, c2_s7, c2_s8, c2_s9, c2_s10, c2_s11], NT,
                 hook=c2_hook)

    if debug_stop == 'C2':
        dbg['xmid'] = (xmid.rearrange('p a b -> p (a b)'), [128, NT * D], F32)
        dbg['h2T'] = (h2T.rearrange('p a b -> p (a b)'), [128, 8 * T], BF16)
        return finish(nc, es, S, dbg, t_h2 + t_xmid)

    S.barrier()
    o = RC
    for i in range(1, NWR):
        w1[i], _, nb = A.alloc(BF16, [8, 512], at=o); o += nb
    for i in range(NWR):
        w2[i], _, nb = A.alloc(BF16, [FG, 1024], at=o); o += nb
    assert o <= RC + 28 * 1024, (o - RC)
    o = RD
    actb = []
    for i in range(2):
        p_, _, nb = A.alloc(BF16, [FG, T], at=o); o += nb
        actb.append(p_)
    w2st = []
    for i in range(2):
        p_, _, nb = A.alloc(F32, [FG, 1024], at=o); o += nb
        w2st.append(p_)
    assert o <= RD + RD_SIZE, (o - RD)
    o = RE + 64
    sgb = []
    for i in range(2):
        p_, _, nb = A.alloc(F32, [512], at=o); o += nb
        sgb.append(p_)
    ln2g, _, nb = A.alloc(F32, [D], at=o); o += nb
    ln2b, _, nb = A.alloc(F32, [D], at=o); o += nb
    y0 = []
    for i in range(3):
        p_, _, nb = A.alloc(F32, [D], at=o); o += nb
        y0.append(p_)
    assert o <= ARENA_BYTES, (o, ARENA_BYTES)
    t_ln2 = [sp_load(ln2g, ln2g_d, 'ln2'), sp_load(ln2b, ln2b_d, 'ln2')][-1]

    w_free = [None] * NWR
    w2st_free = [None, None]
    t_w1 = [None] * NR
    t_w2 = [None] * NR

    def load_round(r):
        slot = r % NWR
        if r == 0:
            t_w1[r] = c2pre['w1']
        else:
            t_w1[r] = pool_cast_load(w1[slot].rearrange('p a b -> p (a b)'), wff1_d[r], 'wf1_%d' % slot, deps=[w_free[slot]])
        s2 = r % 2
        t_l = S.dma('sp', lambda e, dst=w2st[s2], r=r: e.dma_start(out=dst.rearrange('p a b -> p (a b)'), in_=wff2_d[r]),
                    'wf2_%d' % s2, deps=[w2st_free[s2]])
        t_w2[r] = S.op('pool', lambda e, slot=slot, s2=s2: e.tensor_tensor(
            out=w2[slot], in0=w2st[s2], in1=g2bc.unsqueeze(1).to_broadcast([128, FG, 1024]), op=ALU.mult),
            deps=[t_l, w_free[slot]] + t_g2)
        w2st_free[s2] = t_w2[r]

    fb_free = [None] * 8
    sg_free = [None, None]
    act_free = [None, None]
    t_act = {}
    t_acc = [None] * NT
    gu_i = 0

    def emit_gu_unit(r, tg, fi):
        nonlocal gu_i
        slot = r % NWR
        ab = actb[r % 2]
        b0 = (gu_i % 2) * 2
        s_ = gu_i % 2
        gu_i += 1
        pg, pu = bk(b0), bk(b0 + 1)
        tok = slice(tg * 512, (tg + 1) * 512)
        for k in range(8):
            t_g_ = S.op('pe', lambda e, k=k, pg=pg, slot=slot, fi=fi, tok=tok: e.matmul(
                pg, lhsT=w1[slot][:, k, fi * 128:(fi + 1) * 128], rhs=h2T[:, k, tok], start=(k == 0), stop=(k == 7)),
                deps=[t_w1[r], fb_free[b0]] if k == 0 else (), sig=(k == 7))
        for k in range(8):
            t_u_ = S.op('pe', lambda e, k=k, pu=pu, slot=slot, fi=fi, tok=tok: e.matmul(
                pu, lhsT=w1[slot][:, k, 256 + fi * 128:256 + (fi + 1) * 128], rhs=h2T[:, k, tok],
                start=(k == 0), stop=(k == 7)), deps=[fb_free[b0 + 1]] if k == 0 else (), sig=(k == 7))
        t_s = S.op('act', lambda e, dst=sgb[s_], pg=pg: e.activation(out=dst, in_=pg, func=AF.Silu),
                   deps=[t_g_, sg_free[s_]])
        fb_free[b0] = t_s
        t_m = S.op('dve', lambda e, ab=ab, fi=fi, tok=tok, pu=pu, sg=sgb[s_]: e.tensor_tensor(
            out=ab[:, fi, tok], in0=pu, in1=sg, op=ALU.mult), deps=[t_u_, t_s, act_free[r % 2]])
        fb_free[b0 + 1] = t_m
        sg_free[s_] = t_m
        t_act[(r, fi, tg)] = t_m

    def emit_gu_tg(r, tg):
        for fi in range(FG):
            emit_gu_unit(r, tg, fi)

    def emit_gu(r):
        for tg in range(4):
            emit_gu_tg(r, tg)

    def emit_dn_tile(rounds, t):
        b0 = 4 + (t % 2) * 2
        last_mm = None
        for half in range(2):
            pd = bk(b0 + half)
            n_mm = len(rounds) * FG
            i_mm = 0
            for r in rounds:
                slot = r % NWR
                ab = actb[r % 2]
                for fi in range(FG):
                    t_mm = S.op('pe', lambda e, pd=pd, fi=fi, t=t, half=half, ab=ab, slot=slot, i_mm=i_mm, n_mm=n_mm: e.matmul(
                        pd, lhsT=ab[:, fi, t * 128:(t + 1) * 128], rhs=w2[slot][:, fi, half * 512:(half + 1) * 512],
                        start=(i_mm == 0), stop=(i_mm == n_mm - 1)),
                        deps=[t_w2[r], fb_free[b0 + half], t_act[(r, fi, t // 4)]], sig=(i_mm == n_mm - 1))
                    i_mm += 1
            accv = xmid[:, t, half * 512:(half + 1) * 512]
            if rounds[0] == 0:
                t_ad = S.op('dve', lambda e, accv=accv, pd=pd: e.scalar_tensor_tensor(
                    out=accv, in0=accv, scalar=ALPHA, in1=pd, op0=ALU.mult, op1=ALU.add), deps=[t_mm, t_acc[t]])
            else:
                t_ad = S.op('dve', lambda e, accv=accv, pd=pd: e.tensor_tensor(
                    out=accv, in0=pd, in1=accv, op=ALU.add), deps=[t_mm, t_acc[t]])
            fb_free[b0 + half] = t_ad
            t_acc[t] = t_ad
            last_mm = t_mm
        return last_mm

    def emit_dn(r):
        last_mm = None
        for t in range(NT):
            last_mm = emit_dn_tile([r], t)
        act_free[r % 2] = last_mm
        w_free[r % NWR] = last_mm

    y_free = [None] * 3
    t_out = []
    tl = [dict() for _ in range(NT)]

    def tail_s1(t):
        sl = t % 4
        st = stt[:, sl]; mv = mvr[:, sl]
        src = xmid[:, t, :]
        S.op('dve', lambda e: e.bn_stats(out=st[:, 0:6], in_=src[:, 0:512]), deps=[t_acc[t], st_free[sl]], sig=False)
        t1 = S.op('dve', lambda e: e.bn_stats(out=st[:, 6:12], in_=src[:, 512:1024]))
        tl[t]['ag'] = S.op('dve', lambda e: e.bn_aggr(out=mv[:, 0:2], in_=st[:, 0:12]), deps=[t1])

    def tail_s2(t):
        mv = mvr[:, t % 4]
        tl[t]['sq'] = S.op('act', lambda e: e.activation(out=mv[:, 4:5], in_=mv[:, 1:2], func=AF.Sqrt, bias=epst[:, 0:1],
                                                         scale=1.0), deps=[tl[t]['ag']])

    def tail_s3(t):
        _, _, _, tl[t]['r'] = rstd_part(t % 4, [tl[t]['sq']])

    def tail_s4(t):
        mv = mvr[:, t % 4]
        r3 = t % 3
        tl[t]['n'] = S.op('act', lambda e: e.activation(out=y0[r3], in_=xmid[:, t, :], func=AF.Identity,
                                                        scale=mv[:, 2:3], bias=mv[:, 3:4]), deps=[tl[t]['r'], y_free[r3]])
        st_free[t % 4] = tl[t]['n']

    def tail_s5(t):
        r3 = t % 3
        tl[t]['g'] = S.op('pool', lambda e: e.tensor_tensor(out=y0[r3], in0=y0[r3], in1=ln2g, op=ALU.mult),
                          deps=[tl[t]['n'], t_ln2])

    def tail_s6(t):
        r3 = t % 3
        t_st_ = S.dma('pool', lambda e, t=t: e.dma_start(out=out_d[t * 128:(t + 1) * 128, :], in_=y0[r3], accum_op=ALU.add),
                      'out%d' % r3, deps=[tl[t]['g'], outpre['t']])
        y_free[r3] = t_st_
        t_out.append(t_st_)

    tail_stages = [tail_s1, tail_s2, tail_s3, tail_s4, tail_s5, tail_s6]
    tail_state = {'n': 0}

    def tail_step(t_new):
        it = tail_state['n']; tail_state['n'] += 1
        K = len(tail_stages)
        for k in reversed(range(K)):
            i = it - k
            if 0 <= i < NT and (t_new is not None or True):
                if i <= (t_new if t_new is not None else NT - 1):
                    tail_stages[k](i)

    for r in range(min(NWR, NR)):
        load_round(r)
    R1, R2 = NR - 2, NR - 1
    emit_gu(0)
    for r in range(NR - 2):
        if r + 1 < NR - 2:
            gu_units = [(tg, fi) for tg in range(4) for fi in range(FG)]
            per = NT // len(gu_units)
            last_mm = None
            for i_, (tg, fi) in enumerate(gu_units):
                emit_gu_unit(r + 1, tg, fi)
                for t in range(i_ * per, (i_ + 1) * per):
                    last_mm = emit_dn_tile([r], t)
            act_free[r % 2] = last_mm
            w_free[r % NWR] = last_mm
        else:
            emit_gu_tg(R1, 0)
            emit_dn(r)
        if r + NWR < NR:
            load_round(r + NWR)
    emit_gu_tg(R2, 0)
    order = [('GU', 1), ('DN', 0), ('GU', 2), ('DN', 1), ('GU', 3), ('DN', 2), ('DN', 3)]
    for kind_, tg in order:
        if kind_ == 'GU':
            emit_gu_tg(R1, tg)
            emit_gu_tg(R2, tg)
        else:
            for t in range(4 * tg, 4 * tg + 4):
                emit_dn_tile([R1, R2], t)
                tail_step(t)
    for _ in range(len(tail_stages)):
        tail_step(None)
    assert len(t_out) == NT
    return finish(nc, es, S, dbg, t_out)


def finish(nc, es, S, dbg, final_ticks):
    dbg_ticks = []
    for name, spec in dbg.items():
        ap, shape = spec[0], spec[1]
        dt_ = spec[2] if len(spec) > 2 else F32
        d = nc.dram_tensor("dbg_" + name, list(shape), dt_, kind="ExternalOutput").ap()
        dbg_ticks.append(S.dma('sp', lambda e, d=d, ap=ap: e.dma_start(out=d, in_=ap), 'dbg', deps=final_ticks))
    S.barrier()
    S.emit(nc, es)
    es.close()
    return nc


def _rope_tables(start):
    pos = np.arange(start - 128, start - 128 + TH)
    rows = (pos // 64).astype(np.float64)
    cols = (pos % 64).astype(np.float64)
    inv = 10000.0 ** (-np.arange(16, dtype=np.float64) / 16)
    C = np.zeros((64, TH), np.float64)
    Sg = np.zeros((64, TH), np.float64)
    for d in range(64):
        p_ = rows if d < 32 else cols
        dd = d % 32
        i = dd % 16
        ang = p_ * inv[i]
        C[d] = np.cos(ang)
        Sg[d] = -np.sin(ang) if dd < 16 else np.sin(ang)
    C = C.astype(np.float32)
    Sg = Sg.astype(np.float32)
    return np.concatenate([C, C], 0), np.concatenate([Sg, Sg], 0)


def _perm_matrix():
    P = np.zeros((128, 128), np.float32)
    for m in range(128):
        d = m % 64
        dd = d % 32
        partner = d + 16 if dd < 16 else d - 16
        P[(m // 64) * 64 + partner, m] = 1.0
    return P


def prep_inputs(inp):
    f = lambda a: np.ascontiguousarray(np.asarray(a, dtype=np.float32))
    x = f(inp['x']); c = f(inp['c']); ctx = f(inp['ctx']); c_ctx = f(inp['c_ctx'])
    w_ada = f(inp['w_ada'])[0]; b_ada = f(inp['b_ada'])[0]; w_in = f(inp['w_in'])[0]
    sink = f(inp['attn_sink'])[0]
    glg = f(inp['gmlp_ln_g'])[0]; glb = f(inp['gmlp_ln_b'])[0]
    w_s = f(inp['w_spatial'])[0]; b_s = f(inp['b_spatial'])[0]
    w_a = f(inp['w_branch_a'])[0]; w_b = f(inp['w_branch_b'])[0]; w_out = f(inp['w_out'])[0]
    ln1g = f(inp['ln1_g'])[0]; ln1b = f(inp['ln1_b'])[0]; ln2g = f(inp['ln2_g'])[0]; ln2b = f(inp['ln2_b'])[0]
    w_ffn_in = f(inp['w_ffn_in'])[0]; w_ffn_out = f(inp['w_ffn_out'])[0]

    def ktile(w):
        n = w.shape[1]
        return np.ascontiguousarray(w.reshape(8, 128, n).transpose(1, 0, 2)).reshape(128, 8 * n)

    wada_t = np.ascontiguousarray(w_ada.reshape(8, 128, 12, 512).transpose(2, 1, 0, 3)).reshape(12, 128, 4096)
    qcols = []
    for cc in range(4):
        qcols += list(range(cc * 64, cc * 64 + 64)) + list(range((4 + cc) * 64, (4 + cc) * 64 + 64))
    qkcols = qcols + list(range(512, 640))
    wqk = ktile(w_in[:, qkcols])
    wu = ktile(w_in[:, 768:1280])
    wvvb = ktile(w_in[:, list(range(640, 768)) + list(range(1280, 1792))])
    wga = w_in[:, 1792:2816]
    wgb = w_in[:, 2816:3840]
    wc1 = np.zeros((8, 128, 24, 128), np.float32)
    for m in range(8):
        cs = slice(m * 128, (m + 1) * 128)
        wc1[m, :, 0:4] = w_a[:, cs].reshape(4, 128, 128).transpose(1, 0, 2)
        wc1[m, :, 4:8] = w_b[:, cs].reshape(4, 128, 128).transpose(1, 0, 2)
        wc1[m, :, 8:16] = wga[:, cs].reshape(8, 128, 128).transpose(1, 0, 2)
        wc1[m, :, 16:24] = wgb[:, cs].reshape(8, 128, 128).transpose(1, 0, 2)
    wc1 = wc1.reshape(8, 128, 24 * 128)
    wout_t = ktile(w_out)
    wst = np.ascontiguousarray(w_s.transpose(2, 0, 1)).reshape(128, 8 * 128)
    bsbc = np.zeros((128, 4, 128), np.float32)
    for cc in range(4):
        for gi in range(2):
            bsbc[gi * 64:(gi + 1) * 64, cc, :] = b_s[2 * cc + gi][None, :]
    bsbc = bsbc.reshape(128, 512)
    badac = np.zeros((128, 4, 8, 2), np.float32)
    for kind, off in enumerate((0, 1024, 3072, 4096)):
        badac[:, kind, :, :] = b_ada[off:off + 1024].reshape(8, 128).T[:, :, None]
    badac = badac.reshape(128, 64)
    badabc = np.ascontiguousarray(np.broadcast_to(
        np.concatenate([b_ada[2048:3072], b_ada[5120:6144]])[None, :], (128, 2048)))
    wff1 = np.zeros((NR, 128, 8, 512), np.float32)
    wff2 = np.zeros((NR, 128, FG, 1024), np.float32)
    for r in range(NR):
        for fi in range(FG):
            fch = r * FG + fi
            wff1[r, :, :, fi * 128:(fi + 1) * 128] = w_ffn_in[:, fch * 128:(fch + 1) * 128].reshape(8, 128, 128).transpose(1, 0, 2)
            wff1[r, :, :, 256 + fi * 128:256 + (fi + 1) * 128] = \
                w_ffn_in[:, FFH + fch * 128:FFH + (fch + 1) * 128].reshape(8, 128, 128).transpose(1, 0, 2)
            wff2[r, :, fi, :] = w_ffn_out[fch * 128:(fch + 1) * 128, :]
    wff1 = wff1.reshape(NR, 128, 4096)
    wff2 = wff2.reshape(NR, 128, FG * 1024)
    ident = np.eye(128, dtype=np.float32)
    perm = _perm_matrix()
    ki = np.arange(128)[:, None]
    qi = np.arange(128)[None, :]
    maskP = (ki >= qi).astype(np.float32)
    maskN = (ki <= qi).astype(np.float32)
    zero = np.zeros((128, 128), np.float32)
    bc = lambda v: np.ascontiguousarray(np.broadcast_to(v[None, :], (128, v.shape[0])))
    shared = dict(wada=wada_t, badac=badac, badabc=badabc, wqk=wqk, wu=wu, wvvb=wvvb, wc1=wc1, wout=wout_t,
                  wst=wst, bsbc=bsbc, wff1=wff1, wff2=wff2, ident=ident, perm=perm, esink=bc(sink),
                  glg=bc(glg), glb=bc(glb), ln1g=bc(ln1g), ln1b=bc(ln1b), ln2g=bc(ln2g), ln2b=bc(ln2b))
    in_maps = []
    for core in range(NCORES):
        b = core // 4
        seg = core % 4
        start = seg * T
        xhalo = np.zeros((TH, D), np.float32)
        lo = max(start - 128, 0)
        hi = min(start + T + 128, SEQ)
        xhalo[lo - (start - 128): hi - (start - 128)] = x[b, lo:hi]
        cv = np.zeros((128, 16), np.float32)
        cv[:, 0:8] = c[b].reshape(8, 128).T
        cv[:, 8:16] = c_ctx.reshape(8, 128).T
        rc, rs = _rope_tables(start)
        mk = np.stack([zero if seg == 0 else maskP, maskP, maskN, zero if seg == 3 else maskN], 1).reshape(128, 512)
        m = dict(shared)
        m.update(xh=xhalo, ctxb=np.ascontiguousarray(ctx[b]), cvec=cv, ropec=rc, ropes=rs, masks=np.ascontiguousarray(mk))
        in_maps.append(m)
    return in_maps


_NC_CACHE = {}


def kernel(**inputs):
    in_maps = prep_inputs(inputs)
    if 'nc' not in _NC_CACHE:
        _NC_CACHE['nc'] = build_nc(None)
    nc = _NC_CACHE['nc']
    res = run_bass_kernel_spmd(nc, in_maps, core_ids=list(range(NCORES)))
    out = np.zeros((2, SEQ, D), np.float32)
    for core in range(NCORES):
        b = core // 4
        seg = core % 4
        out[b, seg * T:(seg + 1) * T] = res.results[core]["out"]
    return out
```
